# Optimizing a Trainium2 kernel written in Bass

```python
import math
import jax, jax.numpy as jnp
from jax import lax
import numpy as np

D_MODEL = 1024
BATCH = 32
SEQ = 256
DEPTH = 1
DEC_BATCH = 2
DEC_SEQ = 4096
PAST_LEN = 256

GRID_W = 64
D_A = D_MODEL
H_A = 8
DK = D_A // H_A
DV = D_A // H_A
CHUNK = 32
D_B = D_MODEL
SHORT_W = 3
FILT_EMB = 33
FILT_BANDS = (FILT_EMB - 1) // 2
FILT_ORDER = 64
DECAY_FAST = 0.3
DECAY_SLOW = 1.5
DECAY_TARGET = 1e-2
DECAY_SHIFT = 0.05
D_FF = ((8 * D_MODEL // 3 + 255) // 256) * 256
N_MOD = 6
W_IN_COLS = 5 * D_A + 3 * D_B + 2 * D_MODEL
SPLIT_IDX = (D_A, 2 * D_A, 3 * D_A, 4 * D_A, 5 * D_A, 5 * D_A + 3 * D_B)
ALPHA = (2.0 * DEPTH) ** 0.25
BETA = (8.0 * DEPTH) ** -0.25
LN_EPS = 1e-5
RMS_EPS = 1e-6

kernel_name = "hybrid_hgrn2_hyena_prefix_dit_step"


def layer_norm(x, g, b):
    xf = x.astype(jnp.float32)
    mu = xf.mean(-1, keepdims=True)
    var = jnp.square(xf - mu).mean(-1, keepdims=True)
    return ((xf - mu) * lax.rsqrt(var + LN_EPS) * g.astype(jnp.float32) + b.astype(jnp.float32)).astype(x.dtype)


def hgrn_chunk_scan(q, k, v, logf, s0):
    bsz, L = q.shape[0], q.shape[1]
    n = L // CHUNK
    q = q.reshape(bsz, n, CHUNK, H_A, DK)
    k = k.reshape(bsz, n, CHUNK, H_A, DK)
    v = v.reshape(bsz, n, CHUNK, H_A, DV)
    b = jnp.cumsum(logf.reshape(bsz, n, CHUNK, H_A, DK), axis=2)
    b_last = b[:, :, -1:]
    q_in = q * jnp.exp(b)
    k_in = k * jnp.exp(-b)
    mask = jnp.tril(jnp.ones((CHUNK, CHUNK), dtype=bool))
    scores = jnp.where(mask, jnp.einsum('bncha,bnsha->bnhcs', q_in, k_in), 0.0)
    o_intra = jnp.einsum('bnhcs,bnshv->bnchv', scores, v)
    k_st = k * jnp.exp(b_last - b)
    u = jnp.einsum('bncha,bnchv->bnhav', k_st, v)
    decay = jnp.exp(b_last[:, :, 0])

    def step(s, inp):
        dec, du = inp
        return dec[..., None] * s + du, s

    s_final, s_starts = lax.scan(step, s0, (jnp.moveaxis(decay, 1, 0), jnp.moveaxis(u, 1, 0)))
    s_starts = jnp.moveaxis(s_starts, 0, 1)
    o_inter = jnp.einsum('bncha,bnhav->bnchv', q_in, s_starts)
    return (o_intra + o_inter).reshape(bsz, L, H_A, DV), s_final


def hgrn2_mixer(q_raw, f_fwd, f_bwd, i_raw, g_raw, lb, norm_w, s0_fwd, s0_bwd):
    f32 = jnp.float32
    bsz, L, _ = q_raw.shape

    def heads(t):
        return t.astype(f32).reshape(bsz, L, H_A, -1)

    q = heads(jax.nn.silu(q_raw.astype(f32)))
    v = heads(i_raw)

    def gates(z, lb_d):
        fg = lb_d + (1.0 - lb_d) * jax.nn.sigmoid(z.astype(f32))
        return heads(1.0 - fg), heads(jnp.log(fg))

    k_f, lf_f = gates(f_fwd, lb[0])
    k_b, lf_b = gates(f_bwd, lb[1])
    flip = lambda t: jnp.flip(t, axis=1)
    o_f, s_f = hgrn_chunk_scan(q, k_f, v, lf_f, s0_fwd.astype(f32))
    o_b, s_b = hgrn_chunk_scan(flip(q), flip(k_b), flip(v), flip(lf_b), s0_bwd.astype(f32))
    o = o_f + flip(o_b)
    o = o * lax.rsqrt(jnp.mean(jnp.square(o), axis=-1, keepdims=True) + RMS_EPS) * norm_w.astype(f32)
    o = o.reshape(bsz, L, D_A) * jax.nn.silu(g_raw.astype(f32))
    return o.astype(q_raw.dtype), s_f, s_b


def short_conv(u, w, b, row_len):
    bsz, L, ch = u.shape
    n_rows = L // row_len
    u4 = u.reshape(bsz, n_rows, row_len, ch)
    up = jnp.pad(u4, ((0, 0), (0, 0), (1, 1), (0, 0)))
    y = up[:, :, :-2] * w[0] + up[:, :, 1:-1] * w[1] + up[:, :, 2:] * w[2] + b
    return y.reshape(bsz, L, ch)


def hyena_filters(L, w1, b1, w2, b2, w3, b3, freq, w4):
    f32 = jnp.float32
    t = jnp.linspace(0.0, 1.0, L, dtype=f32)[:, None]
    wpos = (2.0 * math.pi / L) * jnp.arange(L, dtype=f32)[:, None]
    bands = jnp.linspace(1e-4, FILT_BANDS - 1, FILT_BANDS, dtype=f32)[None, :]
    z = jnp.concatenate([t, jnp.cos(bands * wpos), -jnp.sin(bands * wpos)], axis=-1)
    freq = freq.astype(f32)
    h = jnp.sin(freq[0] * (z @ w1.astype(f32) + b1.astype(f32)))
    h = jnp.sin(freq[1] * (h @ w2.astype(f32) + b2.astype(f32)))
    h = jnp.sin(freq[2] * (h @ w3.astype(f32) + b3.astype(f32)))
    h = h @ w4.astype(f32)
    max_decay = math.log(DECAY_TARGET) / DECAY_FAST
    min_decay = math.log(DECAY_TARGET) / DECAY_SLOW
    deltas = jnp.linspace(min_decay, max_decay, D_B, dtype=f32)
    window = jnp.exp(-t * jnp.abs(deltas)) + DECAY_SHIFT
    h = h.reshape(L, 2, D_B) * window[:, None, :]
    return h[:, 0], h[:, 1]


def long_conv(u, h_fwd, h_bwd, skip):
    f32 = jnp.float32
    L = u.shape[1]
    kern = jnp.concatenate([h_fwd, jnp.zeros((1, D_B), f32), jnp.flip(h_bwd[1:], axis=0)], axis=0)
    uf = u.astype(f32)
    U = jnp.fft.rfft(uf, n=2 * L, axis=1)
    K = jnp.fft.rfft(kern, n=2 * L, axis=0)
    y = jnp.fft.irfft(U * K[None], n=2 * L, axis=1)[:, :L]
    return (y + uf * skip.astype(f32)).astype(u.dtype)


def hyena_mixer(u, p, row_len):
    u = short_conv(u, p['hy_conv_w'], p['hy_conv_b'], row_len)
    x0, x1, v = jnp.split(u, 3, axis=-1)
    h_f, h_b = hyena_filters(u.shape[1], p['filt_w1'], p['filt_b1'], p['filt_w2'], p['filt_b2'],
                             p['filt_w3'], p['filt_b3'], p['filt_freq'], p['filt_w4'])
    return x0 * long_conv(v * x1, h_f, h_b, p['hy_skip'])


def trunk_layer(x, cond, row_len, s0_fwd, s0_bwd, lb, p):
    mod = (jax.nn.silu(cond) @ p['ada_w'] + p['ada_b']).reshape(cond.shape[0], N_MOD, D_MODEL)
    shift1, scale1, gate1 = mod[:, None, 0], mod[:, None, 1], mod[:, None, 2]
    shift2, scale2, gate2 = mod[:, None, 3], mod[:, None, 4], mod[:, None, 5]

    h = x * (1.0 + scale1) + shift1
    proj = h @ p['w_in']
    q, f_fwd, f_bwd, i_raw, g_raw, hy, mg = jnp.split(proj, SPLIT_IDX, axis=-1)
    o_a, s_f, s_b = hgrn2_mixer(q, f_fwd, f_bwd, i_raw, g_raw, lb, p['hgrn_norm_w'], s0_fwd, s0_bwd)
    o_b = hyena_mixer(hy, p, row_len)
    gate_a, gate_b = jnp.split(jax.nn.sigmoid(mg), 2, axis=-1)
    mix = (gate_a * (o_a @ p['proj_a']) + gate_b * (o_b @ p['proj_b'])) @ p['w_out']
    x = layer_norm(ALPHA * x + gate1 * mix, p['ln1_g'], p['ln1_b'])

    h = x * (1.0 + scale2) + shift2
    gt, up = jnp.split(h @ p['ffn_w_in'], 2, axis=-1)
    ff = (jax.nn.silu(gt) * up) @ p['ffn_w_out']
    x = layer_norm(ALPHA * x + gate2 * ff, p['ln2_g'], p['ln2_b'])
    return x, s_f, s_b


def setup_inputs(seed: int = 0) -> dict:
    key = jax.random.key(seed)
    ks = jax.random.split(key, 32)
    f32 = jnp.float32

    def nrm(k, shape, scale):
        return jax.random.normal(k, shape, f32) * scale

    return {
        "x_prompt": nrm(ks[0], (BATCH, SEQ, D_MODEL), 1.0),
        "x_sample": nrm(ks[1], (DEC_BATCH, DEC_SEQ, D_MODEL), 1.0),
        "state_hgrn": nrm(ks[2], (DEC_BATCH, DEPTH, 2, H_A, DK, DV), 0.5),
        "c": nrm(ks[3], (DEC_BATCH, D_MODEL), 1.0),
        "c_ctx": nrm(ks[4], (D_MODEL,), 1.0),
        "ada_w": nrm(ks[5], (DEPTH, D_MODEL, N_MOD * D_MODEL), 0.5 * D_MODEL ** -0.5),
        "ada_b": nrm(ks[6], (DEPTH, N_MOD * D_MODEL), 0.02),
        "w_in": nrm(ks[7], (DEPTH, D_MODEL, W_IN_COLS), D_MODEL ** -0.5),
        "hgrn_lb_logits": nrm(ks[8], (DEPTH + 1, 2, D_A), 0.1),
        "hgrn_norm_w": 1.0 + nrm(ks[9], (DEPTH, DV), 0.05),
        "hy_conv_w": nrm(ks[10], (DEPTH, SHORT_W, 3 * D_B), SHORT_W ** -0.5),
        "hy_conv_b": nrm(ks[11], (DEPTH, 3 * D_B), 0.02),
        "filt_w1": nrm(ks[12], (DEPTH, FILT_EMB, FILT_ORDER), FILT_EMB ** -0.5),
        "filt_b1": nrm(ks[13], (DEPTH, FILT_ORDER), 0.1),
        "filt_w2": nrm(ks[14], (DEPTH, FILT_ORDER, FILT_ORDER), FILT_ORDER ** -0.5),
        "filt_b2": nrm(ks[15], (DEPTH, FILT_ORDER), 0.1),
        "filt_w3": nrm(ks[16], (DEPTH, FILT_ORDER, FILT_ORDER), FILT_ORDER ** -0.5),
        "filt_b3": nrm(ks[17], (DEPTH, FILT_ORDER), 0.1),
        "filt_freq": 1.0 + nrm(ks[18], (DEPTH, 3, FILT_ORDER), 0.05),
        "filt_w4": nrm(ks[19], (DEPTH, FILT_ORDER, 2 * D_B), FILT_ORDER ** -0.5),
        "hy_skip": nrm(ks[20], (DEPTH, D_B), 0.5),
        "proj_a": nrm(ks[21], (DEPTH, D_A, D_MODEL), D_A ** -0.5),
        "proj_b": nrm(ks[22], (DEPTH, D_B, D_MODEL), D_B ** -0.5),
        "w_out": nrm(ks[23], (DEPTH, D_MODEL, D_MODEL), BETA * D_MODEL ** -0.5),
        "ln1_g": 1.0 + nrm(ks[24], (DEPTH, D_MODEL), 0.05),
        "ln1_b": nrm(ks[25], (DEPTH, D_MODEL), 0.02),
        "ffn_w_in": nrm(ks[26], (DEPTH, D_MODEL, 2 * D_FF), D_MODEL ** -0.5),
        "ffn_w_out": nrm(ks[27], (DEPTH, D_FF, D_MODEL), BETA * D_FF ** -0.5),
        "ln2_g": 1.0 + nrm(ks[28], (DEPTH, D_MODEL), 0.05),
        "ln2_b": nrm(ks[29], (DEPTH, D_MODEL), 0.02),
    }


def reference(x_prompt, x_sample, state_hgrn, c, c_ctx, ada_w, ada_b, w_in, hgrn_lb_logits, hgrn_norm_w,
              hy_conv_w, hy_conv_b, filt_w1, filt_b1, filt_w2, filt_b2, filt_w3, filt_b3, filt_freq, filt_w4,
              hy_skip, proj_a, proj_b, w_out, ln1_g, ln1_b, ffn_w_in, ffn_w_out, ln2_g, ln2_b):
    lb_all = jnp.cumsum(jax.nn.softmax(hgrn_lb_logits.astype(jnp.float32), axis=0), axis=0)
    ctx_len = x_prompt.shape[1]
    rows = x_sample.shape[1] // GRID_W
    lat_row_len = x_sample.shape[1] // rows
    zero_state = jnp.zeros((x_prompt.shape[0], H_A, DK, DV), jnp.float32)
    y_prompt = x_prompt
    y_sample = x_sample
    ctx_states = []
    for l in range(DEPTH):
        p = dict(ada_w=ada_w[l], ada_b=ada_b[l], w_in=w_in[l], hgrn_norm_w=hgrn_norm_w[l],
                 hy_conv_w=hy_conv_w[l], hy_conv_b=hy_conv_b[l], filt_w1=filt_w1[l], filt_b1=filt_b1[l],
                 filt_w2=filt_w2[l], filt_b2=filt_b2[l], filt_w3=filt_w3[l], filt_b3=filt_b3[l],
                 filt_freq=filt_freq[l], filt_w4=filt_w4[l], hy_skip=hy_skip[l], proj_a=proj_a[l],
                 proj_b=proj_b[l], w_out=w_out[l], ln1_g=ln1_g[l], ln1_b=ln1_b[l],
                 ffn_w_in=ffn_w_in[l], ffn_w_out=ffn_w_out[l], ln2_g=ln2_g[l], ln2_b=ln2_b[l])
        lb = lb_all[l]
        y_prompt, s_f, s_b = trunk_layer(y_prompt, c_ctx[None, :], ctx_len, zero_state, zero_state, lb, p)
        ctx_states.append(jnp.stack([s_f, s_b], axis=1).astype(x_prompt.dtype))
        y_sample, _, _ = trunk_layer(y_sample, c, lat_row_len, state_hgrn[:, l, 0], state_hgrn[:, l, 1], lb, p)
    new_state_hgrn = jnp.stack(ctx_states, axis=1)
    return (y_prompt, y_sample, new_state_hgrn)
```

```python
import math
import contextlib
import numpy as np
import ml_dtypes
import concourse.bass as bass
import concourse.mybir as mybir
from concourse.bass_utils import run_bass_kernel_spmd

F32 = mybir.dt.float32
BF16 = mybir.dt.bfloat16
AF = mybir.ActivationFunctionType
ALU = mybir.AluOpType
AX = mybir.AxisListType

D = 1024
NH = 8
LP = 256
LS = 1024
ALPHA = (2.0 * 1) ** 0.25
MAGIC = 12582912.0
TWO_PI = 2.0 * math.pi
DEBUG = False


class Buf:
    __slots__ = ("name", "wtok", "rtoks")

    def __init__(self, name):
        self.name = name
        self.wtok = []
        self.rtoks = []


class V:
    __slots__ = ("ap", "b", "o")

    def __init__(self, ap, b, o=None):
        self.ap = ap
        self.b = b
        self.o = o

    def __getitem__(self, k):
        return V(self.ap[k], self.b, self.o)

    def r(self, pat, **kw):
        return V(self.ap.rearrange(pat, **kw), self.b, self.o)

    def bc(self, axis, shape):
        return V(self.ap.unsqueeze(axis).to_broadcast(shape), self.b, self.o)

    def pb(self, n=128):
        return V(self.ap.partition_broadcast(n), self.b, self.o)


class Tl:
    def __init__(self, P, t, name):
        self.P = P
        self.t = t
        self.b = Buf(name)
        self.name = name
        self.dsem = None

    def __getitem__(self, k):
        return V(self.t[k], self.b, self)

    def sem(self):
        if self.dsem is None:
            self.dsem = self.P.new_sem("d_" + self.name)
        return self.dsem


class Dr:
    def __init__(self, ap, name):
        self.ap = ap
        self.name = name
        self.bufs = {}

    def at(self, key, ap=None):
        if key not in self.bufs:
            self.bufs[key] = Buf(self.name + str(key))
        return V(self.ap if ap is None else ap, self.bufs[key], None)


class Planner:
    ENGS = ("pe", "act", "dve", "pool", "sp")

    def __init__(self, nc):
        self.nc = nc
        self.streams = {e: [] for e in self.ENGS}
        self.sems = {}
        self.cnt = {}
        self.waited = {e: {} for e in self.ENGS}
        self._ctx = []
        self.free_sems = []
        self.ninst = 0
        for e in ("pe", "act", "dve", "pool"):
            self.new_sem("E_" + e)

    def new_sem(self, name):
        if name in self.sems:
            name = name + "_%d" % len(self.sems)
        cm = self.nc.semaphore(name)
        h = cm.__enter__()
        self._ctx.append(cm)
        self.sems[name] = h
        self.cnt[name] = 0
        return name

    def close(self):
        for cm in reversed(self._ctx):
            cm.__exit__(None, None, None)

    def _waits(self, eng, deps):
        need = {}
        for (s, v) in deps:
            if need.get(s, 0) < v:
                need[s] = v
        out = []
        for s, v in need.items():
            if s == "E_pe" and eng == "pe":
                continue
            if self.waited[eng].get(s, 0) >= v:
                continue
            self.waited[eng][s] = v
            out.append((s, v))
        return out

    @staticmethod
    def _deps(reads, writes):
        deps = []
        for b in reads:
            deps += b.wtok
        for b in writes:
            deps += b.wtok
            deps += b.rtoks
        return deps

    @staticmethod
    def _commit(tok, reads, writes):
        for b in reads:
            b.rtoks.append(tok)
        for b in writes:
            b.wtok = [tok]
            b.rtoks = []

    def op(self, eng, fn, reads=(), writes=()):
        waits = self._waits(eng, self._deps(reads, writes))
        sname = "E_" + eng
        self.cnt[sname] += 1
        tok = (sname, self.cnt[sname])
        sems = self.sems
        self.ninst += 1 + len(waits)

        def thunk(h, waits=waits, fn=fn, sname=sname):
            for (s, v) in waits:
                h.wait_ge(sems[s], v)
            fn(h).then_inc(sems[sname], 1)
        self.streams[eng].append(thunk)
        self._commit(tok, reads, writes)
        return tok

    def dma(self, q, sem, pairs, reads=(), writes=()):
        waits = self._waits(q, self._deps(reads, writes))
        self.cnt[sem] += 16 * len(pairs)
        tok = (sem, self.cnt[sem])
        sems = self.sems
        self.ninst += len(pairs) + len(waits)

        def thunk(h, waits=waits, pairs=pairs, sem=sem):
            for (s, v) in waits:
                h.wait_ge(sems[s], v)
            for (o, i) in pairs:
                h.dma_start(out=o, in_=i).then_inc(sems[sem], 16)
        self.streams[q].append(thunk)
        self._commit(tok, reads, writes)
        return tok

    def barrier(self):
        snap = [(s, v) for s, v in self.cnt.items() if v > 0]
        sems = self.sems
        for eng in self.ENGS:
            waits = self._waits(eng, snap)
            self.ninst += len(waits)

            def thunk(h, waits=waits):
                for (s, v) in waits:
                    h.wait_ge(sems[s], v)
            self.streams[eng].append(thunk)

    def emit(self):
        nc = self.nc
        st = self.streams
        with nc.allow_non_contiguous_dma(reason="small vectors"), nc.Block() as block:
            @block.tensor
            def _(h):
                for t in st["pe"]:
                    t(h)

            @block.scalar
            def _(h):
                for t in st["act"]:
                    t(h)

            @block.vector
            def _(h):
                for t in st["dve"]:
                    t(h)

            @block.gpsimd
            def _(h):
                for t in st["pool"]:
                    t(h)

            @block.sync
            def _(h):
                for t in st["sp"]:
                    t(h)


def _bf(a):
    return np.ascontiguousarray(a.astype(np.float32)).astype(ml_dtypes.bfloat16)


def _ptile(a):
    n = a.shape[0] // 128
    return np.ascontiguousarray(a.reshape(n, 128, -1).transpose(1, 0, 2))


def _dft_consts(L):
    N = 2 * L
    f = np.arange(L, dtype=np.float64)[:, None] + 0.5
    n = np.arange(L, dtype=np.float64)[None, :]
    w = 2 * np.pi * f / N
    Cs = np.cos(w * (n + 0.5))
    Ss = np.sin(w * (n + 0.5))
    C0T = (np.cos(w * n) * (2.0 / N)).T
    S0Tn = (-np.sin(w * n) * (2.0 / N)).T
    return [_bf(_ptile(m)) for m in (Cs, Ss, C0T, S0Tn)]


def _filt_feats(L, pos):
    pos = np.asarray(pos)
    t_all = np.linspace(0.0, 1.0, L, dtype=np.float32)
    t = t_all[pos][:, None]
    wpos = (np.float32(2.0 * math.pi / L) * np.arange(L, dtype=np.float32))[pos][:, None]
    bands = np.linspace(1e-4, 15.0, 16, dtype=np.float32)[None, :]
    z = np.concatenate([t, np.cos(bands * wpos), -np.sin(bands * wpos)], axis=-1).astype(np.float32)
    return np.ascontiguousarray(z.T), t_all[pos]


def _ctx_order(c):
    return list(range(0, c)) + list(range(3, c, -1))


_CONST_CACHE = {}


def _shared_consts():
    if _CONST_CACHE:
        return _CONST_CACHE
    s = np.arange(128)[:, None]
    c = np.arange(128)[None, :]
    cc = {}
    cc["ident"] = _bf(np.eye(128))
    mq_f = (s <= c).astype(np.float32) - (s <= 63)
    mst_f = (s > c).astype(np.float32)
    mq_b = (s >= c).astype(np.float32) - (s >= 64)
    mst_b = (s < c).astype(np.float32)
    cc["cmat"] = np.ascontiguousarray(np.stack([mq_f, mst_f, mq_b, mst_b], 1).astype(np.float32))
    cc["mst"] = (mst_f.astype(np.float32), mst_b.astype(np.float32))
    cols_f = np.stack([(np.arange(128) <= 63), np.ones(128)], 1)
    cols_b = np.stack([(np.arange(128) >= 64), np.ones(128)], 1)
    cc["cols2"] = np.ascontiguousarray(np.stack([cols_f, cols_b], 1).astype(np.float32))
    mk_f = np.tile((s <= c).astype(np.float32), (1, 4))
    mk_b = np.tile((s >= c).astype(np.float32), (1, 4))
    cc["masks"] = _bf(np.stack([mk_f, mk_b], 1))
    cs, ss, c0, s0 = _dft_consts(LS)
    cc["dftS"] = np.ascontiguousarray(np.stack([cs, ss, c0, s0], 1))
    cs, ss, c0, s0 = _dft_consts(LP)
    cc["dftP"] = np.ascontiguousarray(np.stack([cs, ss, c0, s0], 1))
    max_decay = math.log(1e-2) / 0.3
    min_decay = math.log(1e-2) / 1.5
    deltas = np.linspace(min_decay, max_decay, D, dtype=np.float32)
    cc["absd"] = np.abs(deltas).astype(np.float32)
    _CONST_CACHE.update(cc)
    return cc


def _core_inputs(r, I):
    cc = _shared_consts()
    b, c = r // 4, r % 4
    f32 = np.float32
    m = {}
    m["xp"] = np.ascontiguousarray(I["x_prompt"][4 * r:4 * r + 4].reshape(1024, D))
    order = _ctx_order(c)
    segs = [c] + order
    m["xs"] = np.ascontiguousarray(np.concatenate([I["x_sample"][b, 1024 * g:1024 * g + 1024] for g in segs], 0))
    cond2 = np.stack([I["c_ctx"], I["c"][b]], 0)
    m["condT"] = np.ascontiguousarray(cond2.reshape(2, 8, 128).transpose(2, 1, 0))
    m["s0"] = np.ascontiguousarray(I["state_hgrn"][b, 0].reshape(16, 128, 128).transpose(1, 0, 2))
    m["ada_w"] = I["ada_w"][0]
    m["ada_b2"] = np.ascontiguousarray(np.broadcast_to(I["ada_b"][0][None], (2, 6 * D)))
    w_in = I["w_in"][0]
    m["w_in"] = w_in
    m["w_fsel"] = np.ascontiguousarray(np.stack(
        [w_in[:, 1024:2048] if g < c else w_in[:, 2048:3072] for g in order], 0))
    lbl = I["hgrn_lb_logits"]
    rows = [lbl[:, 0], lbl[:, 1]] + [lbl[:, 0] if g < c else lbl[:, 1] for g in order]
    m["lbl5"] = np.ascontiguousarray(np.stack(rows, 0).transpose(1, 0, 2))
    m["normw"] = np.ascontiguousarray(np.tile(I["hgrn_norm_w"][0], 8))
    m["convw"] = np.ascontiguousarray(I["hy_conv_w"][0].reshape(3, 24, 128).transpose(2, 1, 0))
    m["convb"] = np.ascontiguousarray(I["hy_conv_b"][0].reshape(24, 128).T)
    m["fw1"] = I["filt_w1"][0]
    m["fw2"] = I["filt_w2"][0]
    m["fw3"] = I["filt_w3"][0]
    m["fb"] = np.ascontiguousarray(np.stack([I["filt_b1"][0], I["filt_b2"][0], I["filt_b3"][0]], 1))
    m["ffreq"] = np.ascontiguousarray(I["filt_freq"][0].T)
    w4 = I["filt_w4"][0]
    w4f, w4b = w4[:, :D], w4[:, D:]
    zt_list, t_list, w4_list = [], [], [w4f, w4b]
    zp, tp = _filt_feats(LP, np.arange(LP))
    zt_list.append(zp)
    t_list.append(tp)
    for g in segs:
        dl = c - g
        j = np.arange(LS)
        if dl >= 0:
            pf = LS * dl + j
            wf = w4f
        else:
            pf = LS * (-dl) - j
            wf = w4b
        if dl > 0:
            pb = LS * dl - j
            wb = w4f
        else:
            pb = LS * (-dl) + j
            wb = w4b
        pb = np.clip(pb, 0, 4095)
        for (pp, ww) in ((pf, wf), (pb, wb)):
            z, t = _filt_feats(4096, pp)
            zt_list.append(z)
            t_list.append(t)
            w4_list.append(ww)
    m["zt"] = np.ascontiguousarray(np.concatenate(zt_list, 1))
    tt = np.concatenate(t_list, 0)
    m["negt"] = np.ascontiguousarray((-tt).reshape(66, 128).T.astype(f32))
    m["w4all"] = np.ascontiguousarray(np.stack(w4_list, 1))
    m["skipT"] = np.ascontiguousarray(I["hy_skip"][0].reshape(8, 128).T)
    m["absd"] = cc["absd"]
    m["proj_a"] = I["proj_a"][0]
    m["proj_b"] = I["proj_b"][0]
    m["w_out"] = I["w_out"][0]
    m["lnrows"] = np.ascontiguousarray(np.stack([I["ln1_g"][0], I["ln1_b"][0], I["ln2_g"][0], I["ln2_b"][0]], 0))
    m["ffn_w_in"] = I["ffn_w_in"][0]
    m["ffn_w_out"] = I["ffn_w_out"][0]
    m["ident"] = cc["ident"]
    m["cmat"] = cc["cmat"]
    mst_f, mst_b = cc["mst"]
    m["mctx"] = np.ascontiguousarray(np.stack([mst_f if g < c else mst_b for g in order], 1))
    m["cols2"] = cc["cols2"]
    m["masks"] = cc["masks"]
    mfb = np.zeros((128, 3, 2), f32)
    for k, g in enumerate(order):
        mfb[:, k, 0] = 1.0 if g < c else 0.0
        mfb[:, k, 1] = 0.0 if g < c else 1.0
    m["mfb"] = mfb
    m["dftS"] = cc["dftS"]
    m["dftP"] = cc["dftP"]
    return m


IN_SPECS = [
    ("xp", [1024, D], F32), ("xs", [4096, D], F32), ("condT", [128, 8, 2], F32), ("s0", [128, 16, 128], F32),
    ("ada_w", [D, 6 * D], F32), ("ada_b2", [2, 6 * D], F32), ("w_in", [D, 10240], F32),
    ("w_fsel", [3, D, D], F32), ("lbl5", [2, 5, D], F32), ("normw", [D], F32),
    ("convw", [128, 24, 3], F32), ("convb", [128, 24], F32), ("fw1", [33, 64], F32), ("fw2", [64, 64], F32),
    ("fw3", [64, 64], F32), ("fb", [64, 3], F32), ("ffreq", [64, 3], F32), ("zt", [33, 8448], F32),
    ("negt", [128, 66], F32), ("w4all", [64, 10, D], F32), ("skipT", [128, 8], F32), ("absd", [D], F32),
    ("proj_a", [D, D], F32), ("proj_b", [D, D], F32), ("w_out", [D, D], F32), ("lnrows", [4, D], F32),
    ("ffn_w_in", [D, 5632], F32), ("ffn_w_out", [2816, D], F32), ("ident", [128, 128], BF16),
    ("cmat", [128, 4, 128], F32), ("mctx", [128, 3, 128], F32), ("cols2", [128, 2, 2], F32),
    ("masks", [128, 2, 512], BF16), ("mfb", [128, 3, 2], F32), ("dftS", [128, 4, 8, 1024], BF16),
    ("dftP", [128, 4, 2, 256], BF16),
]


def build_program(stop_after=None):
    nc = bass.Bass("TRN2", target_bir_lowering=False)
    P = Planner(nc)
    I = {}
    for (name, shape, dt) in IN_SPECS:
        I[name] = Dr(nc.dram_tensor(name, shape, dt, kind="ExternalInput").ap(), name)
    yp = Dr(nc.dram_tensor("yp", [1024, D], F32, kind="ExternalOutput").ap(), "yp")
    ys = Dr(nc.dram_tensor("ys", [1024, D], F32, kind="ExternalOutput").ap(), "ys")
    st_out = Dr(nc.dram_tensor("st", [4, 2, 8, 128, 128], F32, kind="ExternalOutput").ap(), "st")
    skind = "ExternalOutput" if DEBUG else "Internal"

    def scratch(name, shape, dt):
        return Dr(nc.dram_tensor(name, shape, dt, kind=skind).ap(), name)
    rows_d = scratch("rows_d", [2, 6, D], F32)
    lb_d = scratch("lb_d", [5, 2, D], F32)
    hT_d = scratch("hT_d", [40, 128, 8, 128], BF16)
    oaT_d = scratch("oaT_d", [128, 8, 2048], BF16)
    obT_d = scratch("obT_d", [128, 8, 2048], BF16)
    x1_d = scratch("x1_d", [16, 128, D], F32)
    h2T_d = scratch("h2T_d", [16, 128, 8, 128], BF16)
    qc_d = scratch("qc_d", [16, 2, 128, 512], F32)
    vc_d = scratch("vc_d", [16, 2, 128, 512], BF16)

    es_all = contextlib.ExitStack()

    uid = [0]

    def alloc(es, name, shape, dt, psum=False):
        uid[0] += 1
        name = "%s_%d" % (name, uid[0])
        cm = nc.psum_tensor("t_" + name, shape, dt) if psum else nc.sbuf_tensor("t_" + name, shape, dt)
        return Tl(P, es.enter_context(cm), name)

    def bufs(*vs):
        out = []
        for v in vs:
            if v is None or isinstance(v, (int, float)):
                continue
            if v.b not in out:
                out.append(v.b)
        return out

    def A(v):
        return v.ap if isinstance(v, V) else v

    def MM(out, lhsT, rhs, start=True, stop=True):
        P.op("pe", lambda h: h.matmul(out.ap, lhsT.ap, rhs.ap, start=start, stop=stop),
             reads=bufs(lhsT, rhs), writes=bufs(out))

    def TR(out, in_, ident):
        P.op("pe", lambda h: h.transpose(out.ap, in_.ap, ident.ap), reads=bufs(in_, ident), writes=bufs(out))

    def ACT(out, in_, func, bias=None, scale=None, eng="act"):
        kw = {}
        if bias is not None:
            kw["bias"] = A(bias)
        if scale is not None:
            kw["scale"] = A(scale)
        P.op(eng, lambda h: h.activation(out.ap, in_.ap, func, **kw), reads=bufs(in_, bias, scale), writes=bufs(out))

    def TT(eng, out, in0, in1, op):
        P.op(eng, lambda h: h.tensor_tensor(out.ap, in0.ap, in1.ap, op), reads=bufs(in0, in1), writes=bufs(out))

    def TS(eng, out, in0, s1, s2, op0, op1=None):
        if op1 is None:
            P.op(eng, lambda h: h.tensor_scalar(out.ap, in0.ap, A(s1), None, op0), reads=bufs(in0, s1), writes=bufs(out))
        else:
            P.op(eng, lambda h: h.tensor_scalar(out.ap, in0.ap, A(s1), A(s2), op0, op1),
                 reads=bufs(in0, s1, s2), writes=bufs(out))

    def STT(eng, out, in0, scalar, in1, op0, op1):
        P.op(eng, lambda h: h.scalar_tensor_tensor(out.ap, in0.ap, A(scalar), in1.ap, op0, op1),
             reads=bufs(in0, scalar, in1), writes=bufs(out))

    def CP(eng, out, in_):
        if eng == "act":
            P.op(eng, lambda h: h.copy(out.ap, in_.ap), reads=bufs(in_), writes=bufs(out))
        else:
            P.op(eng, lambda h: h.tensor_copy(out.ap, in_.ap), reads=bufs(in_), writes=bufs(out))

    def MEMSET(eng, out, val):
        P.op(eng, lambda h: h.memset(out.ap, val), reads=[], writes=bufs(out))

    def LOAD(q, out, in_):
        pairs = list(zip(out, in_)) if isinstance(out, list) else [(out, in_)]
        tl = pairs[0][0].o
        P.dma(q, tl.sem(), [(o.ap, i.ap) for (o, i) in pairs],
              reads=bufs(*[i for (_, i) in pairs]), writes=bufs(*[o for (o, _) in pairs]))

    def STORE(q, out, in_):
        tl = in_.o
        P.dma(q, tl.sem(), [(out.ap, in_.ap)], reads=bufs(in_), writes=bufs(out))

    def wview(dr, c0, c1, key=None):
        return dr.at(key if key is not None else "w", dr.ap[:, c0:c1].rearrange("(k p) n -> p k n", p=128))

    es0 = es_all
    ident = alloc(es0, "ident", [128, 128], BF16)
    ones1 = alloc(es0, "ones1", [128, 1], F32)
    LOAD("sp", ident[:], I["ident"].at("w"))
    MEMSET("dve", ones1[:], 1.0)

    def psum_std(es):
        pa_ = [alloc(es, "pa%d" % i, [128, 512], F32, psum=True) for i in range(4)]
        pw_ = alloc(es, "pw", [128, 1024], F32, psum=True)
        pt_ = [alloc(es, "pt%d" % i, [128, 1024], BF16, psum=True) for i in range(2)]
        return pa_, pw_, pt_

    def rowload(q, out, dr_v):
        LOAD(q, out, dr_v.pb(128))

    def hv(v):
        return v.r("p (h v) -> p h v", h=8)

    with contextlib.ExitStack() as es:
        pa, pw, pt = psum_std(es)
        scT = alloc(es, "scT", [128, 8, 2], F32)
        adab = alloc(es, "adab", [2, 6 * D], F32)
        mod = alloc(es, "mod", [2, 6 * D], F32)
        adaw = [alloc(es, "adaw%d" % i, [128, 8, 512], F32) for i in range(2)]
        lnr = alloc(es, "lnr", [2, 2, D], F32)
        lbt = alloc(es, "lbt", [5, 2, D], F32)
        lbo = alloc(es, "lbo", [5, 2, D], F32)
        LOAD("sp", scT[:], I["condT"].at("w"))
        LOAD("sp", adab[:], I["ada_b2"].at("w"))
        ACT(scT[:], scT[:], AF.Silu)
        adw = I["ada_w"]
        for blk in range(12):
            t = adaw[blk % 2]
            LOAD("sp" if blk % 2 == 0 else "act", t[:], wview(adw, blk * 512, (blk + 1) * 512))
            for k in range(8):
                MM(pa[0][0:2, :], scT[:, k, :], t[:, k, :], start=(k == 0), stop=(k == 7))
            TT("dve", mod[:, blk * 512:(blk + 1) * 512], pa[0][0:2, :], adab[:, blk * 512:(blk + 1) * 512], ALU.add)
        LOAD("sp", [lnr[:, 0, :], lnr[:, 1, :]],
             [I["lnrows"].at("w", I["lnrows"].ap[0]).pb(2), I["lnrows"].at("w", I["lnrows"].ap[1]).pb(2)])
        TS("dve", mod[:, 1 * D:2 * D], mod[:, 1 * D:2 * D], 1.0, None, ALU.add)
        TS("dve", mod[:, 4 * D:5 * D], mod[:, 4 * D:5 * D], 1.0, None, ALU.add)
        TT("dve", lnr[:, 1, :], lnr[:, 1, :], mod[:, 4 * D:5 * D], ALU.mult)
        TT("dve", lnr[:, 1, :], lnr[:, 1, :], mod[:, 3 * D:4 * D], ALU.add)
        TT("dve", lnr[:, 0, :], lnr[:, 0, :], mod[:, 4 * D:5 * D], ALU.mult)
        rv = rows_d.ap
        STORE("sp", rows_d.at(0, rv[:, 0, :]), mod[:, 0:D])
        STORE("sp", rows_d.at(1, rv[:, 1, :]), mod[:, D:2 * D])
        STORE("sp", rows_d.at(2, rv[:, 2, :]), mod[:, 2 * D:3 * D])
        STORE("sp", rows_d.at(3, rv[:, 3, :]), lnr[:, 0, :])
        STORE("sp", rows_d.at(4, rv[:, 4, :]), lnr[:, 1, :])
        STORE("sp", rows_d.at(5, rv[:, 5, :]), mod[:, 5 * D:6 * D])
        LOAD("sp", [lbt[:, 0, :], lbt[:, 1, :]], [I["lbl5"].at("w", I["lbl5"].ap[0]), I["lbl5"].at("w", I["lbl5"].ap[1])])
        TT("dve", lbt[:, 0, :], lbt[:, 0, :], lbt[:, 1, :], ALU.subtract)
        ACT(lbo[:, 0, :], lbt[:, 0, :], AF.Sigmoid)
        TS("dve", lbo[:, 1, :], lbo[:, 0, :], -0.5, 0.5, ALU.mult, ALU.add)
        TT("dve", lbo[:, 0, :], lbo[:, 0, :], lbo[:, 1, :], ALU.add)
        STORE("sp", lb_d.at("w"), lbo[:])
        P.barrier()

        mrow = alloc(es, "mrow", [128, 2, 2, D], F32)
        for cnd in range(2):
            rowload("sp", mrow[:, cnd, 0, :], rows_d.at(0, rv[cnd, 0, :]))
            rowload("sp", mrow[:, cnd, 1, :], rows_d.at(1, rv[cnd, 1, :]))
        xt = [alloc(es, "xt%d" % i, [128, D], F32) for i in range(3)]
        hb = [alloc(es, "hb%d" % i, [128, D], BF16) for i in range(2)]
        hTt = [alloc(es, "hTt%d" % i, [128, 8, 128], BF16) for i in range(2)]
        def xload(ti):
            src = I["xp"].ap[ti * 128:(ti + 1) * 128, :] if ti < 8 else I["xs"].ap[(ti - 8) * 128:(ti - 7) * 128, :]
            LOAD("sp", xt[ti % 3][:], (I["xp"] if ti < 8 else I["xs"]).at("w", src))
        xload(0)
        xload(1)
        for ti in range(40):
            cnd = 0 if ti < 8 else 1
            x_ = xt[ti % 3]
            h_ = hb[ti % 2]
            o_ = hTt[ti % 2]
            if ti + 2 < 40:
                xload(ti + 2)
            TT("dve", x_[:], x_[:], mrow[:, cnd, 1, :], ALU.mult)
            TT("pool", h_[:], x_[:], mrow[:, cnd, 0, :], ALU.add)
            for k in range(8):
                TR(pt[ti % 2][:, k * 128:(k + 1) * 128], h_[:, k * 128:(k + 1) * 128], ident[:])
            CP("act", o_[:].r("p k t -> p (k t)"), pt[ti % 2][:])
            STORE("sp", hT_d.at(ti, hT_d.ap[ti]), o_[:])
        P.barrier()
    if stop_after == 0:
        return finish(nc, P, es_all, [yp, ys, st_out])

    es_ab = contextlib.ExitStack()
    cmat = alloc(es_ab, "cmat", [128, 4, 128], F32)
    cols2 = alloc(es_ab, "cols2", [128, 2, 2], F32)
    masks = alloc(es_ab, "masks", [128, 2, 512], BF16)
    Sst = [[alloc(es_ab, "S%d_%d" % (d, hh), [128, 4, 128], F32) for hh in range(2)] for d in range(2)]
    PW = [alloc(es_ab, "PW%d" % i, [128, 512], F32, psum=True) for i in range(2)]
    PC = [alloc(es_ab, "PC%d" % i, [128, 512], F32, psum=True) for i in range(2)]
    PS = [alloc(es_ab, "PS%d" % i, [128, 512], F32, psum=True) for i in range(2)]
    PT = [alloc(es_ab, "PT%d" % i, [128, 1024], BF16, psum=True) for i in range(2)]
    LOAD("sp", cmat[:], I["cmat"].at("w"))
    LOAD("sp", cols2[:], I["cols2"].at("w"))
    LOAD("sp", masks[:], I["masks"].at("w"))
    for d in range(2):
        for hh in range(2):
            LOAD("sp", Sst[d][hh][:], I["s0"].at("w", I["s0"].ap[:, 8 * d + 4 * hh:8 * d + 4 * hh + 4, :]))

    def hv4(v):
        return v.r("p (h v) -> p h v", h=4)

    def run_rr(gens):
        alive = [True] * len(gens)
        while any(alive):
            for i, g in enumerate(gens):
                if alive[i]:
                    try:
                        next(g)
                    except StopIteration:
                        alive[i] = False

    with contextlib.ExitStack() as es:
        Wi = alloc(es, "Wi", [128, 8, D], BF16)
        Wf = [alloc(es, "Wf%d" % i, [128, 8, D], BF16) for i in range(2)]
        lbr = alloc(es, "lbr", [128, 2, D], F32)
        mctx = alloc(es, "mctx", [128, 3, 128], F32)
        mfb = alloc(es, "mfb", [128, 3, 2], F32)
        hTa = [alloc(es, "hTa%d" % i, [128, 8, 128], BF16) for i in range(4)]
        PTf = [V(PT[k].t[:].bitcast(F32), PT[k].b, PT[k]) for k in range(2)]
        banksA = {(0, 0): (PW[0][:], PC[0][:]), (0, 1): (PW[1][:], PC[1][:]),
                  (1, 0): (PS[0][:], PTf[0]), (1, 1): (PS[1][:], PTf[1])}
        scA = {}
        for tp in range(2):
            for hh in range(2):
                sfx = "A%d%d" % (tp, hh)
                scA[(tp, hh)] = dict(
                    sg=alloc(es, "sg" + sfx, [128, 512], F32), lf=alloc(es, "lf" + sfx, [128, 512], F32),
                    kk=alloc(es, "kk" + sfx, [128, 512], F32), kst=alloc(es, "kst" + sfx, [128, 512], BF16),
                    vb=alloc(es, "vb" + sfx, [128, 512], BF16), Dt=alloc(es, "Dt" + sfx, [128, 4], F32),
                    al=alloc(es, "al" + sfx, [128, 4], F32), be=alloc(es, "be" + sfx, [128, 4], F32),
                    tmpU=alloc(es, "tmpU" + sfx, [128, 4, 128], F32))
        aggA = [dict(Dagg=alloc(es, "DaggA%d" % hh, [128, 4], F32), Uagg=alloc(es, "UaggA%d" % hh, [128, 4, 128], F32),
                     al=alloc(es, "alS%d" % hh, [128, 4], F32)) for hh in range(2)]
        LOAD("pool", Wi[:], wview(I["w_in"], 3072, 4096))
        LOAD("sp", mctx[:], I["mctx"].at("w"))
        LOAD("sp", mfb[:], I["mfb"].at("w"))

        def ctx_gen(sl, tp, hh, hT, W_):
            c = scA[(tp, hh)]
            g = aggA[hh]
            X, Y = banksA[(tp, hh)]
            cs = slice(hh * 512, (hh + 1) * 512)
            sg, lf, kk, kst, vb = c["sg"], c["lf"], c["kk"], c["kst"], c["vb"]
            Dt, al, be, tmpU = c["Dt"], c["al"], c["be"], c["tmpU"]
            Dagg, Uagg = g["Dagg"], g["Uagg"]
            for k in range(8):
                MM(X, hT[:, k, :], W_[:, k, cs], start=(k == 0), stop=(k == 7))
            ACT(sg[:], X, AF.Tanh, scale=0.5)
            yield
            TT("dve", sg[:], sg[:], lbr[:, 1, cs], ALU.mult)
            TT("dve", sg[:], sg[:], lbr[:, 0, cs], ALU.add)
            yield
            ACT(lf[:], sg[:], AF.Ln)
            TS("pool", kk[:], sg[:], -1.0, 1.0, ALU.mult, ALU.add)
            yield
            MM(Y, mctx[:, sl, :], lf[:])
            ACT(sg[:], Y, AF.Exp)
            TT("dve", kst[:], kk[:], sg[:], ALU.mult)
            yield
            for h in range(4):
                MM(X[:, h:h + 1], lf[:, h * 128:(h + 1) * 128], ones1[:])
            ACT(Dt[:], X[:, 0:4], AF.Exp)
            yield
            for k in range(8):
                MM(Y, hT[:, k, :], Wi[:, k, cs], start=(k == 0), stop=(k == 7))
            CP("act", vb[:], Y)
            yield
            for h in range(4):
                MM(X[:, h * 128:(h + 1) * 128], kst[:, h * 128:(h + 1) * 128], vb[:, h * 128:(h + 1) * 128])
            yield
            TS("dve", al[:], Dt[:], -1.0, mfb[:, sl, 0:1], ALU.add, ALU.mult)
            TS("dve", al[:], al[:], 1.0, None, ALU.add)
            TS("dve", be[:], Dagg[:], -1.0, mfb[:, sl, 1:2], ALU.add, ALU.mult)
            TS("dve", be[:], be[:], 1.0, None, ALU.add)
            TT("dve", Dagg[:], Dagg[:], Dt[:], ALU.mult)
            TT("dve", tmpU[:], hv4(X), be[:].bc(2, [128, 4, 128]), ALU.mult)
            TT("pool", Uagg[:], Uagg[:], al[:].bc(2, [128, 4, 128]), ALU.mult)
            TT("pool", Uagg[:], Uagg[:], tmpU[:], ALU.add)
            yield

        def hload(ti):
            LOAD("sp", hTa[ti % 4][:], hT_d.at(ti, hT_d.ap[ti]))
        hload(16)
        hload(17)
        for sl in range(3):
            W_ = Wf[sl % 2]
            LOAD("pool", W_[:], I["w_fsel"].at("w", I["w_fsel"].ap[sl].rearrange("(k p) n -> p k n", p=128)))
            rowload("sp", lbr[:, 0, :], lb_d.at("w", lb_d.ap[2 + sl, 0, :]))
            rowload("sp", lbr[:, 1, :], lb_d.at("w", lb_d.ap[2 + sl, 1, :]))
            for hh in range(2):
                MEMSET("dve", aggA[hh]["Uagg"][:], 0.0)
                MEMSET("dve", aggA[hh]["Dagg"][:], 1.0)
            for t in range(0, 8, 2):
                ti = 16 + sl * 8 + t
                for nx in (ti + 2, ti + 3):
                    if nx < 40:
                        hload(nx)
                gens = []
                for tp in range(2):
                    for hh in range(2):
                        gens.append(ctx_gen(sl, tp, hh, hTa[(ti + tp) % 4], W_))
                run_rr(gens)
            for hh in range(2):
                g = aggA[hh]
                for d in range(2):
                    TS("dve", g["al"][:], g["Dagg"][:], -1.0, mfb[:, sl, d:d + 1], ALU.add, ALU.mult)
                    TS("dve", g["al"][:], g["al"][:], 1.0, None, ALU.add)
                    TT("dve", Sst[d][hh][:], Sst[d][hh][:], g["al"][:].bc(2, [128, 4, 128]), ALU.mult)
                    STT("dve", Sst[d][hh][:], g["Uagg"][:], mfb[:, sl, d:d + 1], Sst[d][hh][:], ALU.mult, ALU.add)
        P.barrier()
    if stop_after == 1:
        return finish(nc, P, es_all, [yp, ys, st_out])

    with contextlib.ExitStack() as es:
        Wg = {}
        for gi, (nm, c0) in enumerate((("q", 0), ("ff", 1024), ("i", 3072), ("fb", 2048), ("g", 4096))):
            Wg[nm] = alloc(es, "W" + nm, [128, 8, D], BF16)
            LOAD("pool", Wg[nm][:], wview(I["w_in"], c0, c0 + 1024))
        lbr = alloc(es, "lbrB", [128, 2, 2, D], F32)
        nrow = alloc(es, "nrow", [128, D], F32)
        for d in range(2):
            rowload("sp", lbr[:, d, 0, :], lb_d.at("w", lb_d.ap[d, 0, :]))
            rowload("sp", lbr[:, d, 1, :], lb_d.at("w", lb_d.ap[d, 1, :]))
        rowload("sp", nrow[:], I["normw"].at("w"))
        hTa = [alloc(es, "hTb%d" % i, [128, 8, 128], BF16) for i in range(2)]
        scB = []
        for hh in range(2):
            c = {}
            for nm in ("qs", "sg", "lf", "kk", "e1", "osum", "gs"):
                c[nm] = alloc(es, nm + "B%d" % hh, [128, 512], F32)
            for nm in ("qin", "kin", "kst", "vb", "scm", "oab"):
                c[nm] = alloc(es, nm + "B%d" % hh, [128, 512], BF16)
            for nm in ("qinT", "kinT", "Sq"):
                c[nm] = alloc(es, nm + "B%d" % hh, [128, 4, 128], BF16)
            c["eb"] = alloc(es, "ebB%d" % hh, [128, 4, 2], F32)
            c["ssq"] = alloc(es, "ssqB%d" % hh, [128, 4], F32)
            c["ofb"] = alloc(es, "ofbB%d" % hh, [128, 8, 512], BF16)
            c["Sp"] = [alloc(es, "SpB%d_%d" % (hh, d), [128, 4, 128], F32) for d in range(2)]
            c["oaT"] = [alloc(es, "oaTB%d_%d" % (hh, i), [128, 4, 128], BF16) for i in range(2)]
            c["no"] = 0
            scB.append(c)

        def pass_gen(d, S, slot, final, tokcol, hh, hT, tix):
            c = scB[hh]
            cs = slice(hh * 512, (hh + 1) * 512)
            PWh, PCh, PSh, PTh = PW[hh], PC[hh], PS[hh], PT[hh]
            qs, sg, lf, kk, e1, osum, gs = c["qs"], c["sg"], c["lf"], c["kk"], c["e1"], c["osum"], c["gs"]
            qin, kin, kst, vb, scm, oab = c["qin"], c["kin"], c["kst"], c["vb"], c["scm"], c["oab"]
            qinT, kinT, Sq, eb, ssq, ofb = c["qinT"], c["kinT"], c["Sq"], c["eb"], c["ssq"], c["ofb"]

            def proj(W, dst):
                for k in range(8):
                    MM(dst[:], hT[:, k, :], W[:, k, cs], start=(k == 0), stop=(k == 7))
            if d == 0:
                proj(Wg["q"], PWh)
                ACT(qs[:], PWh[:], AF.Tanh, scale=0.5)
            else:
                LOAD("sp", qs[:], qc_d.at((tix, hh), qc_d.ap[tix, hh]))
                LOAD("sp", vb[:], vc_d.at((tix, hh), vc_d.ap[tix, hh]))
            proj(Wg["ff" if d == 0 else "fb"], PCh)
            ACT(sg[:], PCh[:], AF.Tanh, scale=0.5)
            if d == 0:
                STT("dve", qs[:], qs[:], 1.0, PWh[:], ALU.add, ALU.mult)
                STORE("sp", qc_d.at((tix, hh), qc_d.ap[tix, hh]), qs[:])
            yield
            TT("dve", sg[:], sg[:], lbr[:, d, 1, cs], ALU.mult)
            TT("dve", sg[:], sg[:], lbr[:, d, 0, cs], ALU.add)
            if d == 0:
                proj(Wg["i"], PWh)
                CP("act", vb[:], PWh[:])
                STORE("sp", vc_d.at((tix, hh), vc_d.ap[tix, hh]), vb[:])
            yield
            ACT(lf[:], sg[:], AF.Ln)
            TS("pool", kk[:], sg[:], -1.0, 1.0, ALU.mult, ALU.add)
            yield
            MM(PCh[:], cmat[:, 2 * d, :], lf[:])
            ACT(e1[:], PCh[:], AF.Exp)
            STT("dve", qin[:], qs[:], 0.5, e1[:], ALU.mult, ALU.mult)
            yield
            ACT(e1[:], PCh[:], AF.Exp, scale=-1.0)
            TT("pool", kin[:], kk[:], e1[:], ALU.mult)
            MM(PWh[:], cmat[:, 2 * d + 1, :], lf[:])
            yield
            ACT(e1[:], PWh[:], AF.Exp)
            TT("dve", kst[:], kk[:], e1[:], ALU.mult)
            for h in range(4):
                MM(PSh[:, 2 * h:2 * h + 2], lf[:, h * 128:(h + 1) * 128], cols2[:, d, :])
            ACT(eb[:].r("p h t -> p (h t)"), PSh[:, 0:8], AF.Exp)
            yield
            for h in range(4):
                TR(PTh[:, h * 128:(h + 1) * 128], qin[:, h * 128:(h + 1) * 128], ident[:])
            CP("act", qinT[:].r("p h t -> p (h t)"), PTh[:, 0:512])
            for h in range(4):
                TR(PTh[:, 512 + h * 128:512 + (h + 1) * 128], kin[:, h * 128:(h + 1) * 128], ident[:])
            CP("dve", kinT[:].r("p h t -> p (h t)"), PTh[:, 512:1024])
            TT("pool", Sq[:], S[:], eb[:, :, 0].bc(2, [128, 4, 128]), ALU.mult)
            yield
            for h in range(4):
                MM(PSh[:, h * 128:(h + 1) * 128], kinT[:, h, :], qinT[:, h, :])
            TT("dve", scm[:], PSh[:], masks[:, d, :], ALU.mult)
            yield
            for h in range(4):
                MM(PCh[:, h * 128:(h + 1) * 128], scm[:, h * 128:(h + 1) * 128], vb[:, h * 128:(h + 1) * 128],
                   start=True, stop=False)
                MM(PCh[:, h * 128:(h + 1) * 128], qinT[:, h, :], Sq[:, h, :], start=False, stop=True)
            if not final:
                CP("act", ofb[:, slot, :], PCh[:])
            else:
                TT("dve", osum[:], PCh[:], ofb[:, slot, :], ALU.add)
            for h in range(4):
                MM(PSh[:, h * 128:(h + 1) * 128], kst[:, h * 128:(h + 1) * 128], vb[:, h * 128:(h + 1) * 128])
            yield
            TT("pool", S[:], S[:], eb[:, :, 1].bc(2, [128, 4, 128]), ALU.mult)
            TT("dve", S[:], S[:], hv4(PSh[:]), ALU.add)
            yield
            if final:
                proj(Wg["g"], PWh)
                ACT(gs[:], PWh[:], AF.Tanh, scale=0.5)
                STT("dve", gs[:], gs[:], 1.0, PWh[:], ALU.add, ALU.mult)
                TT("pool", e1[:], osum[:], osum[:], ALU.mult)
                P.op("dve", lambda h_: h_.tensor_reduce(ssq[:].ap, hv4(e1[:]).ap, AX.X, ALU.add),
                     reads=[e1.b], writes=[ssq.b])
                TS("dve", ssq[:], ssq[:], 1.0 / 128.0, 1e-6, ALU.mult, ALU.add)
                ACT(ssq[:], ssq[:], AF.Sqrt)
                P.op("dve", lambda h_: h_.reciprocal(ssq[:].ap, ssq[:].ap), reads=[ssq.b], writes=[ssq.b])
                yield
                TT("dve", hv4(osum[:]), hv4(osum[:]), ssq[:].bc(2, [128, 4, 128]), ALU.mult)
                TT("pool", osum[:], osum[:], nrow[:, cs], ALU.mult)
                STT("dve", oab[:], osum[:], 0.5, gs[:], ALU.mult, ALU.mult)
                yield
                for h in range(4):
                    TR(PTh[:, h * 128:(h + 1) * 128], oab[:, h * 128:(h + 1) * 128], ident[:])
                o_ = c["oaT"][c["no"] % 2]
                c["no"] += 1
                CP("act", o_[:].r("p h t -> p (h t)"), PTh[:, 0:512])
                STORE("sp", oaT_d.at(("t", tokcol, hh), oaT_d.ap[:, 4 * hh:4 * hh + 4, tokcol:tokcol + 128]), o_[:])
                yield

        sched = []
        for j in range(4):
            sched += [(2 * j, 0, j, 0, False), (2 * j + 1, 0, j, 1, False), (2 * j + 1, 1, j, 1, True), (2 * j, 1, j, 0, True)]
        sched += [(8 + t, 0, 4, t, False) for t in range(8)] + [(8 + t, 1, 4, t, True) for t in range(7, -1, -1)]
        LOAD("sp", hTa[0][:], hT_d.at(sched[0][0], hT_d.ap[sched[0][0]]))
        for idx, (ti, d, unit, slot, final) in enumerate(sched):
            hT = hTa[idx % 2]
            if idx + 1 < len(sched):
                nti = sched[idx + 1][0]
                LOAD("sp", hTa[(idx + 1) % 2][:], hT_d.at(nti, hT_d.ap[nti]))
            tokcol = ti * 128 if unit < 4 else 1024 + (ti - 8) * 128
            if unit < 4 and d == 0 and slot == 0:
                for hh in range(2):
                    MEMSET("dve", scB[hh]["Sp"][0][:], 0.0)
                    MEMSET("pool", scB[hh]["Sp"][1][:], 0.0)
            gens = []
            for hh in range(2):
                S = scB[hh]["Sp"][d] if unit < 4 else Sst[d][hh]
                gens.append(pass_gen(d, S, slot, final, tokcol, hh, hT, ti))
            run_rr(gens)
            if unit < 4 and ((d == 0 and slot == 1) or (d == 1 and slot == 0)):
                for hh in range(2):
                    STORE("sp", st_out.at((d, unit, hh), st_out.ap[unit, d, 4 * hh:4 * hh + 4].rearrange("h a v -> a h v")),
                          scB[hh]["Sp"][d][:])
        P.barrier()
    es_ab.close()
    if stop_after == 2:
        return finish(nc, P, es_all, [yp, ys, st_out])

    with contextlib.ExitStack() as es:
        pa = [alloc(es, "q%d" % i, [128, 512], F32, psum=True) for i in range(7)]
        pt = [alloc(es, "ptc", [128, 1024], BF16, psum=True)]
        rot = {"i": 0}

        def nb():
            rot["i"] += 1
            return pa[4 + rot["i"] % 3]
        dS = alloc(es, "dS", [128, 4, 8, 1024], BF16)
        dP = alloc(es, "dP", [128, 4, 2, 256], BF16)
        LOAD("act", [dS[:, i] for i in range(4)], [I["dftS"].at("w", I["dftS"].ap[:, i]) for i in range(4)])
        LOAD("act", dP[:], I["dftP"].at("w"))
        absr = alloc(es, "absr", [128, D], F32)
        rowload("sp", absr[:], I["absd"].at("w"))
        negt = alloc(es, "negt", [128, 66], F32)
        convw = alloc(es, "convw", [128, 24, 3], F32)
        convb = alloc(es, "convb", [128, 24], F32)
        skipT = alloc(es, "skipT", [128, 8], F32)
        LOAD("sp", negt[:], I["negt"].at("w"))
        LOAD("sp", convw[:], I["convw"].at("w"))
        LOAD("sp", convb[:], I["convb"].at("w"))
        LOAD("sp", skipT[:], I["skipT"].at("w"))
        h3T = alloc(es, "h3T", [64, 8448], BF16)
        fw1 = alloc(es, "fw1", [33, 64], F32)
        fw2 = alloc(es, "fw2", [64, 64], F32)
        fw3 = alloc(es, "fw3", [64, 64], F32)
        fbt = alloc(es, "fbt", [64, 3], F32)
        frq = alloc(es, "frq", [64, 3], F32)

        Wx = [alloc(es, "Wx%d" % i, [128, 8, 3, 128], BF16) for i in range(2)]
        w4c = [alloc(es, "w4c%d" % i, [64, 10, 128], BF16) for i in range(2)]
        hTblk = [alloc(es, "hTblk%d" % i, [128, 8, 512], BF16) for i in range(2)]
        cy = [alloc(es, "cy%d" % i, [128, 512], F32) for i in range(2)]
        ub2 = [alloc(es, "ub%d" % i, [128, 512], BF16) for i in range(2)]
        uTk = [alloc(es, "uTk%d" % i, [128, 2048], BF16) for i in range(2)]
        x0k = [alloc(es, "x0k%d" % i, [128, 2048], BF16) for i in range(2)]
        utm = [alloc(es, "utm%d" % i, [128, 40, 128], BF16) for i in range(2)]
        ew = alloc(es, "ew", [128, 512], F32)
        hwf = alloc(es, "hwf", [128, 512], F32)
        hwb = alloc(es, "hwb", [128, 512], F32)
        hpm2 = [alloc(es, "hpm%d" % i, [128, 2, 8, 128], BF16) for i in range(2)]
        Ksb = alloc(es, "Ksb", [128, 2, 1024], BF16)
        CSsb2 = [alloc(es, "CSsb%d" % i, [128, 2, 1024], BF16) for i in range(2)]
        identf = alloc(es, "identf", [128, 128], F32)
        CP("dve", identf[:], ident[:])
        tA = alloc(es, "tA", [128, 1024], F32)
        PQ = alloc(es, "PQ", [128, 2, 1024], F32)
        PQT = alloc(es, "PQT", [128, 2, 8, 128], BF16)

        def taps(ct, s, njt, tile_f, tile_b, wf, wb, same_pos, hpm):
            for g0 in range(0, njt, 4):
                ng = min(4, njt - g0)
                for side, (tb, wi, dst) in enumerate(((tile_f, wf, hwf), (tile_b, wb, hwb))):
                    for i in range(ng):
                        tcol = (tb + g0 + i) * 128
                        MM(pa[2 + side][:, i * 128:(i + 1) * 128], h3T[:, tcol:tcol + 128], w4c[s][:, wi, :])
                    if side == 0 or not same_pos:
                        for i in range(ng):
                            ACT(ew[:, i * 128:(i + 1) * 128], absr[:, ct * 128:(ct + 1) * 128], AF.Exp,
                                scale=negt[:, tb + g0 + i:tb + g0 + i + 1])
                    STT("dve", dst[:, 0:ng * 128], ew[:, 0:ng * 128], 0.05, pa[2 + side][:, 0:ng * 128], ALU.add, ALU.mult)
                    if side == 1 and g0 == 0:
                        MEMSET("dve", dst[0:1, 0:128], 0.0)
                TT("pool", hpm[:, 0, g0:g0 + ng, :],
                   hwf[:, 0:ng * 128].r("p (j c) -> p j c", j=ng), hwb[:, 0:ng * 128].r("p (j c) -> p j c", j=ng), ALU.add)
                TT("pool", hpm[:, 1, g0:g0 + ng, :],
                   hwf[:, 0:ng * 128].r("p (j c) -> p j c", j=ng), hwb[:, 0:ng * 128].r("p (j c) -> p j c", j=ng),
                   ALU.subtract)
                yield

        def pq_update(first, CSsb):
            Cu, Su = CSsb[:, 0, :], CSsb[:, 1, :]
            Kr, Ki = Ksb[:, 0, :], Ksb[:, 1, :]
            Pv, Qv = PQ[:, 0, :], PQ[:, 1, :]
            if first:
                TT("dve", Pv, Cu, Kr, ALU.mult)
                TT("pool", Qv, Su, Kr, ALU.mult)
            else:
                TT("dve", tA[:], Cu, Kr, ALU.mult)
                TT("dve", Pv, Pv, tA[:], ALU.add)
                TT("dve", tA[:], Su, Kr, ALU.mult)
                TT("dve", Qv, Qv, tA[:], ALU.add)
            TT("dve", tA[:], Su, Ki, ALU.mult)
            TT("dve", Pv, Pv, tA[:], ALU.add)
            TT("dve", tA[:], Cu, Ki, ALU.mult)
            TT("dve", Qv, Qv, tA[:], ALU.subtract)

        def pq_transpose():
            for w in range(2):
                for g in range(2):
                    bk = nb()
                    for i in range(4):
                        TR(bk[:, i * 128:(i + 1) * 128], PQ[:, w, (4 * g + i) * 128:(4 * g + i + 1) * 128], identf[:])
                    CP("act", PQT[:, w, 4 * g:4 * g + 4, :].r("p i c -> p (i c)"), bk[:])

        def epilogue_half(ct, s, tok0, half, bk):
            hs = slice(half * 512, (half + 1) * 512)
            ts_ = slice(tok0 + half * 512, tok0 + (half + 1) * 512)
            STT("dve", tA[:, hs], uTk[s][:, ts_], skipT[:, ct:ct + 1], bk[:], ALU.mult, ALU.add)
            TT("pool", Ksb[:, 0, hs], tA[:, hs], x0k[s][:, ts_], ALU.mult)
            if half == 1:
                STORE("sp", obT_d.at(("c", ct, tok0), obT_d.ap[:, ct, tok0:tok0 + 1024]), Ksb[:, 0, :])

        def blk_load(bi):
            LOAD("sp", [hTblk[bi % 2][:, :, i * 128:(i + 1) * 128] for i in range(4)],
                 [hT_d.at(4 * bi + i, hT_d.ap[4 * bi + i]) for i in range(4)])

        def part1(ct, s):
            LOAD("pool", [Wx[s][:, :, g, :] for g in range(3)],
                 [wview(I["w_in"], 5120 + g * 1024 + ct * 128, 5120 + g * 1024 + (ct + 1) * 128) for g in range(3)])
            LOAD("pool", w4c[s][:], I["w4all"].at("w", I["w4all"].ap[:, :, ct * 128:(ct + 1) * 128]))
            blk_load(0)

            def u_transpose(bi):
                ub = ub2[bi % 2]
                for i in range(4):
                    TR(pt[0][:, i * 128:(i + 1) * 128], ub[:, i * 128:(i + 1) * 128], ident[:])
                CP("act", utm[s][:, 4 * bi:4 * bi + 4, :].r("p i c -> p (i c)"), pt[0][:, 0:512])

            for bi in range(10):
                if bi + 1 < 10:
                    blk_load(bi + 1)
                hb_ = hTblk[bi % 2]
                ub = ub2[bi % 2]
                groups = (1, 2, 0) if bi < 4 else (1, 2)
                rl = 256 if bi < 2 else 64
                for gi, g in enumerate(groups):
                    ps = pa[gi % 2]
                    for k in range(8):
                        MM(ps[:], Wx[s][:, k, g, :], hb_[:, k, :], start=(k == 0), stop=(k == 7))
                    ci = g * 8 + ct
                    if gi == 2:
                        TT("pool", ub[:], cy[0][:], cy[1][:], ALU.mult)
                    y_ = cy[gi % 2]
                    ACT(y_[:], ps[:], AF.Identity, bias=convb[:, ci:ci + 1], scale=convw[:, ci, 1:2])
                    y3 = y_[:].r("p (r t) -> p r t", t=rl)
                    x3 = ps[:].r("p (r t) -> p r t", t=rl)
                    STT("dve", y3[:, :, 1:rl], x3[:, :, 0:rl - 1], convw[:, ci, 0:1], y3[:, :, 1:rl], ALU.mult, ALU.add)
                    STT("dve", y3[:, :, 0:rl - 1], x3[:, :, 1:rl], convw[:, ci, 2:3], y3[:, :, 0:rl - 1], ALU.mult, ALU.add)
                if len(groups) == 2:
                    TT("pool", ub[:], cy[0][:], cy[1][:], ALU.mult)
                if bi < 4:
                    CP("pool", uTk[s][:, bi * 512:(bi + 1) * 512], ub[:])
                    CP("act", x0k[s][:, bi * 512:(bi + 1) * 512], cy[0][:])
                if bi > 0:
                    u_transpose(bi - 1)
                yield
            u_transpose(9)
            yield

        def part2(ct, s):
            U = utm[s]

            def cusu_prompt(CSsb):
                for w in range(2):
                    for jp in range(2):
                        bk = nb()
                        for j in (2 * jp, 2 * jp + 1):
                            for nt in range(2):
                                MM(bk[:, (j % 2) * 256:(j % 2 + 1) * 256], U[:, 2 * j + nt, :], dP[:, w, nt, :],
                                   start=(nt == 0), stop=(nt == 1))
                        CP("act", CSsb[:, w, jp * 512:(jp + 1) * 512], bk[:])

            def cusu_sample(kb, CSsb):
                for w in range(2):
                    for half in range(2):
                        bk = nb()
                        for nt in range(8):
                            MM(bk[:], U[:, 8 + 8 * kb + nt, :],
                               dS[:, w, nt, half * 512:(half + 1) * 512], start=(nt == 0), stop=(nt == 7))
                        CP("act", CSsb[:, w, half * 512:(half + 1) * 512], bk[:])

            def k_sample(hpm):
                for w in range(2):
                    for half in range(2):
                        bk = nb()
                        for jt in range(8):
                            MM(bk[:], hpm[:, w, jt, :],
                               dS[:, 2 + w, jt, half * 512:(half + 1) * 512], start=(jt == 0), stop=(jt == 7))
                        CP("act", Ksb[:, w, half * 512:(half + 1) * 512], bk[:])

            def stap(kb):
                return taps(ct, s, 8, 2 + 16 * kb, 2 + 16 * kb + 8, 2 + 2 * kb, 3 + 2 * kb, False, hpm2[(kb + 1) % 2])

            yield from taps(ct, s, 2, 0, 0, 0, 1, True, hpm2[0])
            cusu_prompt(CSsb2[0])
            yield
            yield from stap(0)
            hp_ = hpm2[0]
            bk = nb()
            for jt in range(2):
                MM(bk[:, 0:256], hp_[:, 0, jt, :], dP[:, 2, jt, :], start=(jt == 0), stop=(jt == 1))
            for jt in range(2):
                MM(bk[:, 256:512], hp_[:, 1, jt, :], dP[:, 3, jt, :], start=(jt == 0), stop=(jt == 1))
            for j in range(4):
                CP("act", Ksb[:, 0, j * 256:(j + 1) * 256], bk[:, 0:256])
                CP("act", Ksb[:, 1, j * 256:(j + 1) * 256], bk[:, 256:512])
            yield
            cusu_sample(0, CSsb2[1])
            yield
            pq_update(True, CSsb2[0])
            yield
            pq_transpose()
            for jp in range(2):
                bk = nb()
                for j in (2 * jp, 2 * jp + 1):
                    n = 0
                    for w in range(2):
                        for ft in range(2):
                            MM(bk[:, (j % 2) * 256:(j % 2 + 1) * 256], PQT[:, w, 2 * j + ft, :], dP[:, w, ft, :],
                               start=(n == 0), stop=(n == 3))
                            n += 1
                epilogue_half(ct, s, 0, jp, bk)
            yield
            for kb in range(4):
                if kb + 1 < 4:
                    yield from stap(kb + 1)
                k_sample(hpm2[(kb + 1) % 2])
                yield
                if kb + 1 < 4:
                    cusu_sample(kb + 1, CSsb2[kb % 2])
                    yield
                pq_update(kb == 0, CSsb2[(kb + 1) % 2])
                yield
            pq_transpose()
            for half in range(2):
                bk = nb()
                n = 0
                for w in range(2):
                    for ft in range(8):
                        MM(bk[:], PQT[:, w, ft, :],
                           dS[:, w, ft, half * 512:(half + 1) * 512], start=(n == 0), stop=(n == 15))
                        n += 1
                epilogue_half(ct, s, 1024, half, bk)
            yield

        zb1 = Buf("z1buf")
        zsem = [PQ.sem(), P.new_sem("zsem1")]

        def mlp_setup():
            LOAD("sp", fw1[:], I["fw1"].at("w"))
            LOAD("sp", fw2[:], I["fw2"].at("w"))
            LOAD("sp", fw3[:], I["fw3"].at("w"))
            LOAD("sp", fbt[:], I["fb"].at("w"))
            LOAD("sp", frq[:], I["ffreq"].at("w"))
            TT("dve", fbt[:], fbt[:], frq[:], ALU.mult)

        def mlp_gen(cid):
            ws = [fw1, fw2, fw3]
            if cid == 0:
                aa, kf, ss = tA[0:64, 0:512], tA[0:64, 512:1024], PQ[0:64, 0, 0:512]
                zv = PQ[0:33, 0, 512:1024]
                banks = (pa[4], pa[5])
            else:
                aa, kf, ss = ew[0:64, :], hwf[0:64, :], hwb[0:64, :]
                zv = V(PQ.t[0:33, 1, 0:512], zb1, None)
                banks = (pa[6], pa[2])
            for ch in range(cid, 17, 2):
                c0 = ch * 512
                n = min(512, 8448 - c0)
                P.dma("sp", zsem[cid], [(zv[:, 0:n].ap, I["zt"].ap[:, c0:c0 + n])], reads=[], writes=[zv.b])
                src = zv
                for l in range(3):
                    pv = banks[l % 2][0:64, 0:n]
                    MM(pv, ws[l][:], src[:, 0:n])
                    TS("dve", aa[:, 0:n], pv, frq[:, l:l + 1], fbt[:, l:l + 1], ALU.mult, ALU.add)
                    TS("dve", kf[:, 0:n], aa[:, 0:n], 1.0 / TWO_PI, MAGIC, ALU.mult, ALU.add)
                    TS("dve", kf[:, 0:n], kf[:, 0:n], -MAGIC, None, ALU.add)
                    STT("dve", aa[:, 0:n], kf[:, 0:n], -TWO_PI, aa[:, 0:n], ALU.mult, ALU.add)
                    if l < 2:
                        ACT(ss[:, 0:n], aa[:, 0:n], AF.Sin)
                        src = ss
                    else:
                        ACT(h3T[:, c0:c0 + n], aa[:, 0:n], AF.Sin)
                    yield

        def chain(*gs):
            for g in gs:
                yield from g

        mlp_setup()
        run_rr([mlp_gen(0), mlp_gen(1), chain(part1(0, 0), part1(1, 1))])
        for ct in range(8):
            g2 = part2(ct, ct % 2)
            g1 = part1(ct + 1, (ct + 1) % 2) if 1 <= ct < 7 else iter(())
            alive = [True, True]
            while alive[0] or alive[1]:
                for _ in range(1):
                    if alive[1]:
                        try:
                            next(g2)
                        except StopIteration:
                            alive[1] = False
                if alive[0]:
                    try:
                        next(g1)
                    except StopIteration:
                        alive[0] = False
        P.barrier()
    if stop_after == 3:
        return finish(nc, P, es_all, [yp, ys, st_out])

    rv = rows_d.ap

    def layer_norm_tile(es_tiles, r, grow, brow, out_main, extra=None):
        stats, mv, xn = es_tiles
        for c2 in range(2):
            P.op("dve", lambda h_, c2=c2: h_.bn_stats(stats[:, c2, :].ap, r[:, c2 * 512:(c2 + 1) * 512].ap),
                 reads=[r.b], writes=[stats.b])
        P.op("dve", lambda h_: h_.bn_aggr(mv[:, 0:2].ap, stats[:].ap), reads=[stats.b], writes=[mv.b])
        TS("dve", mv[:, 2:3], mv[:, 1:2], 1e-5, None, ALU.add)
        ACT(mv[:, 2:3], mv[:, 2:3], AF.Sqrt)
        P.op("dve", lambda h_: h_.reciprocal(mv[:, 3:4].ap, mv[:, 2:3].ap), reads=[mv.b], writes=[mv.b])
        TS("dve", xn[:], r[:], mv[:, 0:1], mv[:, 3:4], ALU.subtract, ALU.mult)
        TT("pool", out_main, xn[:], grow, ALU.mult)
        TT("pool", out_main, out_main, brow, ALU.add)
        if extra is not None:
            G, B, o2, tmp = extra
            TT("dve", tmp, xn[:], G, ALU.mult)
            TT("dve", o2, tmp, B, ALU.add)

    with contextlib.ExitStack() as es:
        pa, pw, pt = psum_std(es)
        pA = alloc(es, "pA", [128, 8, D], BF16)
        pB = alloc(es, "pB", [128, 8, D], BF16)
        wO = alloc(es, "wO", [128, 8, D], BF16)
        Wmg = alloc(es, "Wmg", [128, 8, 2 * D], BF16)
        LOAD("pool", Wmg[:, :, 0:D], wview(I["w_in"], 8192, 8192 + D))
        LOAD("pool", pA[:], wview(I["proj_a"], 0, D))
        LOAD("pool", Wmg[:, :, D:2 * D], wview(I["w_in"], 8192 + D, 10240))
        LOAD("pool", pB[:], wview(I["proj_b"], 0, D))
        LOAD("pool", wO[:], wview(I["w_out"], 0, D))
        oaTh = alloc(es, "oaTh", [128, 8, 1024], BF16)
        obTh = alloc(es, "obTh", [128, 8, 1024], BF16)
        hTh = alloc(es, "hTh", [128, 8, 1024], BF16)
        mT = alloc(es, "mT", [128, 8, 1024], BF16)
        rws = alloc(es, "rws", [128, 5, D], F32)
        gas = alloc(es, "gas", [128, 512], F32)
        gbs = alloc(es, "gbs", [128, 512], F32)
        m1 = alloc(es, "m1", [128, 512], F32)
        m2 = alloc(es, "m2", [128, 512], F32)
        xt = [alloc(es, "xtD%d" % i, [128, D], F32) for i in range(2)]
        rr = alloc(es, "rr", [128, D], F32)
        xn = alloc(es, "xn", [128, D], F32)
        x1t = [alloc(es, "x1t%d" % i, [128, D], F32) for i in range(2)]
        h2b = alloc(es, "h2b", [128, D], BF16)
        h2Tt = [alloc(es, "h2Tt%d" % i, [128, 8, 128], BF16) for i in range(2)]
        stats = alloc(es, "stats", [128, 2, 6], F32)
        mv = alloc(es, "mv", [128, 4], F32)
        for hf in range(2):
            LOAD("sp", oaTh[:], oaT_d.at(("h", hf), oaT_d.ap[:, :, hf * 1024:(hf + 1) * 1024]))
            LOAD("sp", obTh[:], obT_d.at(("h", hf), obT_d.ap[:, :, hf * 1024:(hf + 1) * 1024]))
            LOAD("sp", [hTh[:, :, t * 128:(t + 1) * 128] for t in range(8)],
                 [hT_d.at(hf * 8 + t, hT_d.ap[hf * 8 + t]) for t in range(8)])
            rowload("sp", rws[:, 0, :], rows_d.at(2, rv[hf, 2, :]))
            rowload("sp", rws[:, 1, :], rows_d.at(3, rv[hf, 3, :]))
            rowload("sp", rws[:, 2, :], rows_d.at(4, rv[hf, 4, :]))
            rowload("sp", rws[:, 3, :], I["lnrows"].at("w", I["lnrows"].ap[0]))
            rowload("sp", rws[:, 4, :], I["lnrows"].at("w", I["lnrows"].ap[1]))
            for j in range(8):
                for tc in range(2):
                    ts_ = slice(tc * 512, (tc + 1) * 512)
                    cs_ = slice(j * 128, (j + 1) * 128)
                    cs2 = slice(D + j * 128, D + (j + 1) * 128)
                    for k in range(8):
                        MM(pa[2][:], Wmg[:, k, cs_], hTh[:, k, ts_], start=(k == 0), stop=(k == 7))
                    for k in range(8):
                        MM(pa[3][:], Wmg[:, k, cs2], hTh[:, k, ts_], start=(k == 0), stop=(k == 7))
                    for k in range(8):
                        MM(pa[0][:], pA[:, k, cs_], oaTh[:, k, ts_], start=(k == 0), stop=(k == 7))
                    for k in range(8):
                        MM(pa[1][:], pB[:, k, cs_], obTh[:, k, ts_], start=(k == 0), stop=(k == 7))
                    ACT(gas[:], pa[2][:], AF.Sigmoid)
                    ACT(gbs[:], pa[3][:], AF.Sigmoid)
                    TT("dve", m1[:], gas[:], pa[0][:], ALU.mult)
                    TT("dve", m2[:], gbs[:], pa[1][:], ALU.mult)
                    TT("pool", mT[:, j, ts_], m1[:], m2[:], ALU.add)
            for t in range(8):
                gi = hf * 8 + t
                for half2 in range(2):
                    for k in range(8):
                        MM(pw[:, half2 * 512:(half2 + 1) * 512], mT[:, k, t * 128:(t + 1) * 128],
                           wO[:, k, half2 * 512:(half2 + 1) * 512], start=(k == 0), stop=(k == 7))
                x_ = xt[t % 2]
                if t == 0:
                    LOAD("sp", xt[0][:], (I["xp"] if hf == 0 else I["xs"]).at("w", (I["xp"] if hf == 0 else I["xs"]).ap[0:128, :]))
                if t + 1 < 8:
                    LOAD("sp", xt[(t + 1) % 2][:], (I["xp"] if hf == 0 else I["xs"]).at(
                        "w", (I["xp"] if hf == 0 else I["xs"]).ap[(t + 1) * 128:(t + 2) * 128, :]))
                TT("dve", rr[:], pw[:], rws[:, 0, :], ALU.mult)
                STT("dve", rr[:], x_[:], ALPHA, rr[:], ALU.mult, ALU.add)
                x1_ = x1t[t % 2]
                layer_norm_tile((stats, mv, xn), rr, rws[:, 3, :], rws[:, 4, :], x1_[:],
                                extra=(rws[:, 1, :], rws[:, 2, :], h2b[:], rr[:]))
                STORE("sp", x1_d.at(gi, x1_d.ap[gi]), x1_[:])
                for k in range(8):
                    TR(pt[t % 2][:, k * 128:(k + 1) * 128], h2b[:, k * 128:(k + 1) * 128], ident[:])
                o_ = h2Tt[t % 2]
                CP("act", o_[:].r("p k t -> p (k t)"), pt[t % 2][:])
                STORE("sp", h2T_d.at(gi, h2T_d.ap[gi]), o_[:])
        P.barrier()
    if stop_after == 4:
        return finish(nc, P, es_all, [yp, ys, st_out])

    with contextlib.ExitStack() as es:
        pa, pw, pt = psum_std(es)
        wout = alloc(es, "wout", [128, 22, D], BF16)
        actT = alloc(es, "actT", [128, 22, 1024], BF16)
        h2Th = alloc(es, "h2Th", [128, 8, 1024], BF16)
        Wfi = [alloc(es, "Wfi%d" % i, [128, 8, 2, 256], BF16) for i in range(2)]
        rws = alloc(es, "rws2", [128, 3, D], F32)
        sgt = [alloc(es, "sgt%d" % i, [128, 512], F32) for i in range(2)]
        x1t = [alloc(es, "x1u%d" % i, [128, D], F32) for i in range(2)]
        rr = alloc(es, "rr2", [128, D], F32)
        xn = alloc(es, "xn2", [128, D], F32)
        yt = [alloc(es, "yt%d" % i, [128, D], F32) for i in range(2)]
        stats = alloc(es, "stats2", [128, 2, 6], F32)
        mv = alloc(es, "mv2", [128, 4], F32)
        for hf in range(2):
            LOAD("sp", [h2Th[:, :, t * 128:(t + 1) * 128] for t in range(8)],
                 [h2T_d.at(hf * 8 + t, h2T_d.ap[hf * 8 + t]) for t in range(8)])
            rowload("sp", rws[:, 0, :], rows_d.at(5, rv[hf, 5, :]))
            rowload("sp", rws[:, 1, :], I["lnrows"].at("w", I["lnrows"].ap[2]))
            rowload("sp", rws[:, 2, :], I["lnrows"].at("w", I["lnrows"].ap[3]))
            def wfi_load(fbk):
                W2 = Wfi[fbk % 2]
                LOAD("pool", [W2[:, :, 0, :], W2[:, :, 1, :]],
                     [wview(I["ffn_w_in"], fbk * 256, (fbk + 1) * 256),
                      wview(I["ffn_w_in"], 2816 + fbk * 256, 2816 + (fbk + 1) * 256)])
            wfi_load(0)
            for fbk in range(11):
                W_ = Wfi[fbk % 2]
                if fbk + 1 < 11:
                    wfi_load(fbk + 1)
                if hf == 0 and fbk == 1:
                    LOAD("pool", wout[:], wview(I["ffn_w_out"], 0, D))
                for sub in range(2):
                    j = fbk * 2 + sub
                    for tc in range(2):
                        ts_ = slice(tc * 512, (tc + 1) * 512)
                        pg, pu, sg_ = pa[2 * tc], pa[2 * tc + 1], sgt[tc]
                        for k in range(8):
                            MM(pg[:], W_[:, k, 0, sub * 128:(sub + 1) * 128], h2Th[:, k, ts_],
                               start=(k == 0), stop=(k == 7))
                        for k in range(8):
                            MM(pu[:], W_[:, k, 1, sub * 128:(sub + 1) * 128], h2Th[:, k, ts_],
                               start=(k == 0), stop=(k == 7))
                        ACT(sg_[:], pg[:], AF.Silu)
                        TT("dve", actT[:, j, ts_], sg_[:], pu[:], ALU.mult)
            for t in range(8):
                gi = hf * 8 + t
                for half2 in range(2):
                    for j in range(22):
                        MM(pw[:, half2 * 512:(half2 + 1) * 512], actT[:, j, t * 128:(t + 1) * 128],
                           wout[:, j, half2 * 512:(half2 + 1) * 512], start=(j == 0), stop=(j == 21))
                x1_ = x1t[t % 2]
                if t == 0:
                    LOAD("sp", x1t[0][:], x1_d.at(gi, x1_d.ap[gi]))
                if t + 1 < 8:
                    LOAD("sp", x1t[(t + 1) % 2][:], x1_d.at(gi + 1, x1_d.ap[gi + 1]))
                TT("dve", rr[:], pw[:], rws[:, 0, :], ALU.mult)
                STT("dve", rr[:], x1_[:], ALPHA, rr[:], ALU.mult, ALU.add)
                y_ = yt[t % 2]
                layer_norm_tile((stats, mv, xn), rr, rws[:, 1, :], rws[:, 2, :], y_[:])
                dst = yp if hf == 0 else ys
                STORE("sp", dst.at(t, dst.ap[t * 128:(t + 1) * 128, :]), y_[:])
        P.barrier()
    return finish(nc, P, es_all, [yp, ys, st_out])


def finish(nc, P, es_all, outs):
    P.barrier()
    P.emit()
    try:
        es_all.close()
    except Exception:
        pass
    P.close()
    return nc


_NC_CACHE = {}


def kernel(**inputs):
    I = {k: np.asarray(v) for k, v in inputs.items()}
    if "nc" not in _NC_CACHE:
        _NC_CACHE["nc"] = build_program()
    nc = _NC_CACHE["nc"]
    in_maps = [_core_inputs(r, I) for r in range(8)]
    res = run_bass_kernel_spmd(nc, in_maps, core_ids=list(range(8)))
    y_prompt = np.concatenate([res.results[r]["yp"].reshape(4, 256, D) for r in range(8)], 0)
    y_sample = np.zeros((2, 4096, D), np.float32)
    for r in range(8):
        y_sample[r // 4, 1024 * (r % 4):1024 * (r % 4 + 1)] = res.results[r]["ys"]
    new_state = np.concatenate([res.results[r]["st"].reshape(4, 1, 2, 8, 128, 128) for r in range(8)], 0)
    return (y_prompt.astype(np.float32), y_sample, new_state.astype(np.float32))
```

```python
import math
import contextlib
import numpy as np
import ml_dtypes
import concourse.bass as bass
import concourse.mybir as mybir
from concourse.bass_utils import run_bass_kernel_spmd

F32 = mybir.dt.float32
BF16 = mybir.dt.bfloat16
AF = mybir.ActivationFunctionType
ALU = mybir.AluOpType
AX = mybir.AxisListType

D = 1024
NH = 8
LP = 256
LS = 1024
ALPHA = (2.0 * 1) ** 0.25
MAGIC = 12582912.0
TWO_PI = 2.0 * math.pi
DEBUG = False


class Buf:
    __slots__ = ("name", "wtok", "rtoks")

    def __init__(self, name):
        self.name = name
        self.wtok = []
        self.rtoks = []


class V:
    __slots__ = ("ap", "b", "o")

    def __init__(self, ap, b, o=None):
        self.ap = ap
        self.b = b
        self.o = o

    def __getitem__(self, k):
        return V(self.ap[k], self.b, self.o)

    def r(self, pat, **kw):
        return V(self.ap.rearrange(pat, **kw), self.b, self.o)

    def bc(self, axis, shape):
        return V(self.ap.unsqueeze(axis).to_broadcast(shape), self.b, self.o)

    def pb(self, n=128):
        return V(self.ap.partition_broadcast(n), self.b, self.o)


class Tl:
    def __init__(self, P, t, name):
        self.P = P
        self.t = t
        self.b = Buf(name)
        self.name = name
        self.dsem = None

    def __getitem__(self, k):
        return V(self.t[k], self.b, self)

    def sem(self):
        if self.dsem is None:
            self.dsem = self.P.new_sem("d_" + self.name)
        return self.dsem


class Dr:
    def __init__(self, ap, name):
        self.ap = ap
        self.name = name
        self.bufs = {}

    def at(self, key, ap=None):
        if key not in self.bufs:
            self.bufs[key] = Buf(self.name + str(key))
        return V(self.ap if ap is None else ap, self.bufs[key], None)


class Planner:
    ENGS = ("pe", "act", "dve", "pool", "sp")

    def __init__(self, nc):
        self.nc = nc
        self.streams = {e: [] for e in self.ENGS}
        self.sems = {}
        self.cnt = {}
        self.waited = {e: {} for e in self.ENGS}
        self._ctx = []
        self.free_sems = []
        self.ninst = 0
        for e in ("pe", "act", "dve", "pool"):
            self.new_sem("E_" + e)

    def new_sem(self, name):
        if name in self.sems:
            name = name + "_%d" % len(self.sems)
        cm = self.nc.semaphore(name)
        h = cm.__enter__()
        self._ctx.append(cm)
        self.sems[name] = h
        self.cnt[name] = 0
        return name

    def close(self):
        for cm in reversed(self._ctx):
            cm.__exit__(None, None, None)

    def _waits(self, eng, deps):
        need = {}
        for (s, v) in deps:
            if need.get(s, 0) < v:
                need[s] = v
        out = []
        for s, v in need.items():
            if s == "E_pe" and eng == "pe":
                continue
            if self.waited[eng].get(s, 0) >= v:
                continue
            self.waited[eng][s] = v
            out.append((s, v))
        return out

    @staticmethod
    def _deps(reads, writes):
        deps = []
        for b in reads:
            deps += b.wtok
        for b in writes:
            deps += b.wtok
            deps += b.rtoks
        return deps

    @staticmethod
    def _commit(tok, reads, writes):
        for b in reads:
            b.rtoks.append(tok)
        for b in writes:
            b.wtok = [tok]
            b.rtoks = []

    def op(self, eng, fn, reads=(), writes=()):
        waits = self._waits(eng, self._deps(reads, writes))
        sname = "E_" + eng
        self.cnt[sname] += 1
        tok = (sname, self.cnt[sname])
        sems = self.sems
        self.ninst += 1 + len(waits)

        def thunk(h, waits=waits, fn=fn, sname=sname):
            for (s, v) in waits:
                h.wait_ge(sems[s], v)
            fn(h).then_inc(sems[sname], 1)
        self.streams[eng].append(thunk)
        self._commit(tok, reads, writes)
        return tok

    def dma(self, q, sem, pairs, reads=(), writes=()):
        waits = self._waits(q, self._deps(reads, writes))
        self.cnt[sem] += 16 * len(pairs)
        tok = (sem, self.cnt[sem])
        sems = self.sems
        self.ninst += len(pairs) + len(waits)

        def thunk(h, waits=waits, pairs=pairs, sem=sem):
            for (s, v) in waits:
                h.wait_ge(sems[s], v)
            for (o, i) in pairs:
                h.dma_start(out=o, in_=i).then_inc(sems[sem], 16)
        self.streams[q].append(thunk)
        self._commit(tok, reads, writes)
        return tok

    def barrier(self):
        snap = [(s, v) for s, v in self.cnt.items() if v > 0]
        sems = self.sems
        for eng in self.ENGS:
            waits = self._waits(eng, snap)
            self.ninst += len(waits)

            def thunk(h, waits=waits):
                for (s, v) in waits:
                    h.wait_ge(sems[s], v)
            self.streams[eng].append(thunk)

    def emit(self):
        nc = self.nc
        st = self.streams
        with nc.allow_non_contiguous_dma(reason="small vectors"), nc.Block() as block:
            @block.tensor
            def _(h):
                for t in st["pe"]:
                    t(h)

            @block.scalar
            def _(h):
                for t in st["act"]:
                    t(h)

            @block.vector
            def _(h):
                for t in st["dve"]:
                    t(h)

            @block.gpsimd
            def _(h):
                for t in st["pool"]:
                    t(h)

            @block.sync
            def _(h):
                for t in st["sp"]:
                    t(h)


def _bf(a):
    return np.ascontiguousarray(a.astype(np.float32)).astype(ml_dtypes.bfloat16)


def _ptile(a):
    n = a.shape[0] // 128
    return np.ascontiguousarray(a.reshape(n, 128, -1).transpose(1, 0, 2))


def _dft_consts(L):
    N = 2 * L
    f = np.arange(L, dtype=np.float64)[:, None] + 0.5
    n = np.arange(L, dtype=np.float64)[None, :]
    w = 2 * np.pi * f / N
    Cs = np.cos(w * (n + 0.5))
    Ss = np.sin(w * (n + 0.5))
    C0T = (np.cos(w * n) * (2.0 / N)).T
    S0Tn = (-np.sin(w * n) * (2.0 / N)).T
    return [_bf(_ptile(m)) for m in (Cs, Ss, C0T, S0Tn)]


def _filt_feats(L, pos):
    pos = np.asarray(pos)
    t_all = np.linspace(0.0, 1.0, L, dtype=np.float32)
    t = t_all[pos][:, None]
    wpos = (np.float32(2.0 * math.pi / L) * np.arange(L, dtype=np.float32))[pos][:, None]
    bands = np.linspace(1e-4, 15.0, 16, dtype=np.float32)[None, :]
    z = np.concatenate([t, np.cos(bands * wpos), -np.sin(bands * wpos)], axis=-1).astype(np.float32)
    return np.ascontiguousarray(z.T), t_all[pos]


def _ctx_order(c):
    return list(range(0, c)) + list(range(3, c, -1))


_CONST_CACHE = {}


def _shared_consts():
    if _CONST_CACHE:
        return _CONST_CACHE
    s = np.arange(128)[:, None]
    c = np.arange(128)[None, :]
    cc = {}
    cc["ident"] = _bf(np.eye(128))
    mq_f = (s <= c).astype(np.float32) - (s <= 63)
    mst_f = (s > c).astype(np.float32)
    mq_b = (s >= c).astype(np.float32) - (s >= 64)
    mst_b = (s < c).astype(np.float32)
    cc["cmat"] = np.ascontiguousarray(np.stack([mq_f, mst_f, mq_b, mst_b], 1).astype(np.float32))
    cc["mst"] = (mst_f.astype(np.float32), mst_b.astype(np.float32))
    cols_f = np.stack([(np.arange(128) <= 63), np.ones(128)], 1)
    cols_b = np.stack([(np.arange(128) >= 64), np.ones(128)], 1)
    cc["cols2"] = np.ascontiguousarray(np.stack([cols_f, cols_b], 1).astype(np.float32))
    mk_f = np.tile((s <= c).astype(np.float32), (1, 4))
    mk_b = np.tile((s >= c).astype(np.float32), (1, 4))
    cc["masks"] = _bf(np.stack([mk_f, mk_b], 1))
    cs, ss, c0, s0 = _dft_consts(LS)
    cc["dftS"] = np.ascontiguousarray(np.stack([cs, ss, c0, s0], 1))
    cs, ss, c0, s0 = _dft_consts(LP)
    cc["dftP"] = np.ascontiguousarray(np.stack([cs, ss, c0, s0], 1))
    max_decay = math.log(1e-2) / 0.3
    min_decay = math.log(1e-2) / 1.5
    deltas = np.linspace(min_decay, max_decay, D, dtype=np.float32)
    cc["absd"] = np.abs(deltas).astype(np.float32)
    _CONST_CACHE.update(cc)
    return cc


def _core_inputs(r, I):
    cc = _shared_consts()
    b, c = r // 4, r % 4
    f32 = np.float32
    m = {}
    m["xp"] = np.ascontiguousarray(I["x_prompt"][4 * r:4 * r + 4].reshape(1024, D))
    order = _ctx_order(c)
    segs = [c] + order
    m["xs"] = np.ascontiguousarray(np.concatenate([I["x_sample"][b, 1024 * g:1024 * g + 1024] for g in segs], 0))
    cond2 = np.stack([I["c_ctx"], I["c"][b]], 0)
    m["condT"] = np.ascontiguousarray(cond2.reshape(2, 8, 128).transpose(2, 1, 0))
    m["s0"] = np.ascontiguousarray(I["state_hgrn"][b, 0].reshape(16, 128, 128).transpose(1, 0, 2))
    m["ada_w"] = I["ada_w"][0]
    m["ada_b2"] = np.ascontiguousarray(np.broadcast_to(I["ada_b"][0][None], (2, 6 * D)))
    w_in = I["w_in"][0]
    m["w_in"] = w_in
    m["w_fsel"] = np.ascontiguousarray(np.stack(
        [w_in[:, 1024:2048] if g < c else w_in[:, 2048:3072] for g in order], 0))
    lbl = I["hgrn_lb_logits"]
    rows = [lbl[:, 0], lbl[:, 1]] + [lbl[:, 0] if g < c else lbl[:, 1] for g in order]
    m["lbl5"] = np.ascontiguousarray(np.stack(rows, 0).transpose(1, 0, 2))
    m["normw"] = np.ascontiguousarray(np.tile(I["hgrn_norm_w"][0], 8))
    m["convw"] = np.ascontiguousarray(I["hy_conv_w"][0].reshape(3, 24, 128).transpose(2, 1, 0))
    m["convb"] = np.ascontiguousarray(I["hy_conv_b"][0].reshape(24, 128).T)
    m["fw1"] = I["filt_w1"][0]
    m["fw2"] = I["filt_w2"][0]
    m["fw3"] = I["filt_w3"][0]
    m["fb"] = np.ascontiguousarray(np.stack([I["filt_b1"][0], I["filt_b2"][0], I["filt_b3"][0]], 1))
    m["ffreq"] = np.ascontiguousarray(I["filt_freq"][0].T)
    w4 = I["filt_w4"][0]
    w4f, w4b = w4[:, :D], w4[:, D:]
    zt_list, t_list, w4_list = [], [], [w4f, w4b]
    zp, tp = _filt_feats(LP, np.arange(LP))
    zt_list.append(zp)
    t_list.append(tp)
    for g in segs:
        dl = c - g
        j = np.arange(LS)
        if dl >= 0:
            pf = LS * dl + j
            wf = w4f
        else:
            pf = LS * (-dl) - j
            wf = w4b
        if dl > 0:
            pb = LS * dl - j
            wb = w4f
        else:
            pb = LS * (-dl) + j
            wb = w4b
        pb = np.clip(pb, 0, 4095)
        for (pp, ww) in ((pf, wf), (pb, wb)):
            z, t = _filt_feats(4096, pp)
            zt_list.append(z)
            t_list.append(t)
            w4_list.append(ww)
    m["zt"] = np.ascontiguousarray(np.concatenate(zt_list, 1))
    tt = np.concatenate(t_list, 0)
    m["negt"] = np.ascontiguousarray((-tt).reshape(66, 128).T.astype(f32))
    m["w4all"] = np.ascontiguousarray(np.stack(w4_list, 1))
    m["skipT"] = np.ascontiguousarray(I["hy_skip"][0].reshape(8, 128).T)
    m["absd"] = cc["absd"]
    m["proj_a"] = I["proj_a"][0]
    m["proj_b"] = I["proj_b"][0]
    m["w_out"] = I["w_out"][0]
    m["lnrows"] = np.ascontiguousarray(np.stack([I["ln1_g"][0], I["ln1_b"][0], I["ln2_g"][0], I["ln2_b"][0]], 0))
    m["ffn_w_in"] = I["ffn_w_in"][0]
    m["ffn_w_out"] = I["ffn_w_out"][0]
    m["ident"] = cc["ident"]
    m["cmat"] = cc["cmat"]
    mst_f, mst_b = cc["mst"]
    m["mctx"] = np.ascontiguousarray(np.stack([mst_f if g < c else mst_b for g in order], 1))
    m["cols2"] = cc["cols2"]
    m["masks"] = cc["masks"]
    mfb = np.zeros((128, 3, 2), f32)
    for k, g in enumerate(order):
        mfb[:, k, 0] = 1.0 if g < c else 0.0
        mfb[:, k, 1] = 0.0 if g < c else 1.0
    m["mfb"] = mfb
    m["dftS"] = cc["dftS"]
    m["dftP"] = cc["dftP"]
    return m


IN_SPECS = [
    ("xp", [1024, D], F32), ("xs", [4096, D], F32), ("condT", [128, 8, 2], F32), ("s0", [128, 16, 128], F32),
    ("ada_w", [D, 6 * D], F32), ("ada_b2", [2, 6 * D], F32), ("w_in", [D, 10240], F32),
    ("w_fsel", [3, D, D], F32), ("lbl5", [2, 5, D], F32), ("normw", [D], F32),
    ("convw", [128, 24, 3], F32), ("convb", [128, 24], F32), ("fw1", [33, 64], F32), ("fw2", [64, 64], F32),
    ("fw3", [64, 64], F32), ("fb", [64, 3], F32), ("ffreq", [64, 3], F32), ("zt", [33, 8448], F32),
    ("negt", [128, 66], F32), ("w4all", [64, 10, D], F32), ("skipT", [128, 8], F32), ("absd", [D], F32),
    ("proj_a", [D, D], F32), ("proj_b", [D, D], F32), ("w_out", [D, D], F32), ("lnrows", [4, D], F32),
    ("ffn_w_in", [D, 5632], F32), ("ffn_w_out", [2816, D], F32), ("ident", [128, 128], BF16),
    ("cmat", [128, 4, 128], F32), ("mctx", [128, 3, 128], F32), ("cols2", [128, 2, 2], F32),
    ("masks", [128, 2, 512], BF16), ("mfb", [128, 3, 2], F32), ("dftS", [128, 4, 8, 1024], BF16),
    ("dftP", [128, 4, 2, 256], BF16),
]


def build_program(stop_after=None):
    nc = bass.Bass("TRN2", target_bir_lowering=False)
    P = Planner(nc)
    I = {}
    for (name, shape, dt) in IN_SPECS:
        I[name] = Dr(nc.dram_tensor(name, shape, dt, kind="ExternalInput").ap(), name)
    yp = Dr(nc.dram_tensor("yp", [1024, D], F32, kind="ExternalOutput").ap(), "yp")
    ys = Dr(nc.dram_tensor("ys", [1024, D], F32, kind="ExternalOutput").ap(), "ys")
    st_out = Dr(nc.dram_tensor("st", [4, 2, 8, 128, 128], F32, kind="ExternalOutput").ap(), "st")
    skind = "ExternalOutput" if DEBUG else "Internal"

    def scratch(name, shape, dt):
        return Dr(nc.dram_tensor(name, shape, dt, kind=skind).ap(), name)
    rows_d = scratch("rows_d", [2, 6, D], F32)
    lb_d = scratch("lb_d", [5, 2, D], F32)
    hT_d = scratch("hT_d", [40, 128, 8, 128], BF16)
    oaT_d = scratch("oaT_d", [128, 8, 2048], BF16)
    obT_d = scratch("obT_d", [128, 8, 2048], BF16)
    x1_d = scratch("x1_d", [16, 128, D], F32)
    h2T_d = scratch("h2T_d", [16, 128, 8, 128], BF16)
    qc_d = scratch("qc_d", [16, 2, 128, 512], F32)
    vc_d = scratch("vc_d", [16, 2, 128, 512], BF16)

    es_all = contextlib.ExitStack()

    uid = [0]

    def alloc(es, name, shape, dt, psum=False):
        uid[0] += 1
        name = "%s_%d" % (name, uid[0])
        cm = nc.psum_tensor("t_" + name, shape, dt) if psum else nc.sbuf_tensor("t_" + name, shape, dt)
        return Tl(P, es.enter_context(cm), name)

    def bufs(*vs):
        out = []
        for v in vs:
            if v is None or isinstance(v, (int, float)):
                continue
            if v.b not in out:
                out.append(v.b)
        return out

    def A(v):
        return v.ap if isinstance(v, V) else v

    def MM(out, lhsT, rhs, start=True, stop=True):
        P.op("pe", lambda h: h.matmul(out.ap, lhsT.ap, rhs.ap, start=start, stop=stop),
             reads=bufs(lhsT, rhs), writes=bufs(out))

    def TR(out, in_, ident):
        P.op("pe", lambda h: h.transpose(out.ap, in_.ap, ident.ap), reads=bufs(in_, ident), writes=bufs(out))

    def ACT(out, in_, func, bias=None, scale=None, eng="act"):
        kw = {}
        if bias is not None:
            kw["bias"] = A(bias)
        if scale is not None:
            kw["scale"] = A(scale)
        P.op(eng, lambda h: h.activation(out.ap, in_.ap, func, **kw), reads=bufs(in_, bias, scale), writes=bufs(out))

    def TT(eng, out, in0, in1, op):
        P.op(eng, lambda h: h.tensor_tensor(out.ap, in0.ap, in1.ap, op), reads=bufs(in0, in1), writes=bufs(out))

    def TS(eng, out, in0, s1, s2, op0, op1=None):
        if op1 is None:
            P.op(eng, lambda h: h.tensor_scalar(out.ap, in0.ap, A(s1), None, op0), reads=bufs(in0, s1), writes=bufs(out))
        else:
            P.op(eng, lambda h: h.tensor_scalar(out.ap, in0.ap, A(s1), A(s2), op0, op1),
                 reads=bufs(in0, s1, s2), writes=bufs(out))

    def STT(eng, out, in0, scalar, in1, op0, op1):
        P.op(eng, lambda h: h.scalar_tensor_tensor(out.ap, in0.ap, A(scalar), in1.ap, op0, op1),
             reads=bufs(in0, scalar, in1), writes=bufs(out))

    def CP(eng, out, in_):
        if eng == "act":
            P.op(eng, lambda h: h.copy(out.ap, in_.ap), reads=bufs(in_), writes=bufs(out))
        else:
            P.op(eng, lambda h: h.tensor_copy(out.ap, in_.ap), reads=bufs(in_), writes=bufs(out))

    def MEMSET(eng, out, val):
        P.op(eng, lambda h: h.memset(out.ap, val), reads=[], writes=bufs(out))

    def LOAD(q, out, in_):
        pairs = list(zip(out, in_)) if isinstance(out, list) else [(out, in_)]
        tl = pairs[0][0].o
        P.dma(q, tl.sem(), [(o.ap, i.ap) for (o, i) in pairs],
              reads=bufs(*[i for (_, i) in pairs]), writes=bufs(*[o for (o, _) in pairs]))

    def STORE(q, out, in_):
        tl = in_.o
        P.dma(q, tl.sem(), [(out.ap, in_.ap)], reads=bufs(in_), writes=bufs(out))

    def wview(dr, c0, c1, key=None):
        return dr.at(key if key is not None else "w", dr.ap[:, c0:c1].rearrange("(k p) n -> p k n", p=128))

    es0 = es_all
    ident = alloc(es0, "ident", [128, 128], BF16)
    ones1 = alloc(es0, "ones1", [128, 1], F32)
    LOAD("sp", ident[:], I["ident"].at("w"))
    MEMSET("dve", ones1[:], 1.0)

    def psum_std(es):
        pa_ = [alloc(es, "pa%d" % i, [128, 512], F32, psum=True) for i in range(4)]
        pw_ = alloc(es, "pw", [128, 1024], F32, psum=True)
        pt_ = [alloc(es, "pt%d" % i, [128, 1024], BF16, psum=True) for i in range(2)]
        return pa_, pw_, pt_

    def rowload(q, out, dr_v):
        LOAD(q, out, dr_v.pb(128))

    def hv(v):
        return v.r("p (h v) -> p h v", h=8)

    with contextlib.ExitStack() as es:
        pa, pw, pt = psum_std(es)
        scT = alloc(es, "scT", [128, 8, 2], F32)
        adab = alloc(es, "adab", [2, 6 * D], F32)
        mod = alloc(es, "mod", [2, 6 * D], F32)
        adaw = [alloc(es, "adaw%d" % i, [128, 8, 512], F32) for i in range(2)]
        lnr = alloc(es, "lnr", [2, 2, D], F32)
        lbt = alloc(es, "lbt", [5, 2, D], F32)
        lbo = alloc(es, "lbo", [5, 2, D], F32)
        LOAD("sp", scT[:], I["condT"].at("w"))
        LOAD("sp", adab[:], I["ada_b2"].at("w"))
        ACT(scT[:], scT[:], AF.Silu)
        adw = I["ada_w"]
        for blk in range(12):
            t = adaw[blk % 2]
            LOAD("sp" if blk % 2 == 0 else "act", t[:], wview(adw, blk * 512, (blk + 1) * 512))
            for k in range(8):
                MM(pa[0][0:2, :], scT[:, k, :], t[:, k, :], start=(k == 0), stop=(k == 7))
            TT("dve", mod[:, blk * 512:(blk + 1) * 512], pa[0][0:2, :], adab[:, blk * 512:(blk + 1) * 512], ALU.add)
        LOAD("sp", [lnr[:, 0, :], lnr[:, 1, :]],
             [I["lnrows"].at("w", I["lnrows"].ap[0]).pb(2), I["lnrows"].at("w", I["lnrows"].ap[1]).pb(2)])
        TS("dve", mod[:, 1 * D:2 * D], mod[:, 1 * D:2 * D], 1.0, None, ALU.add)
        TS("dve", mod[:, 4 * D:5 * D], mod[:, 4 * D:5 * D], 1.0, None, ALU.add)
        TT("dve", lnr[:, 1, :], lnr[:, 1, :], mod[:, 4 * D:5 * D], ALU.mult)
        TT("dve", lnr[:, 1, :], lnr[:, 1, :], mod[:, 3 * D:4 * D], ALU.add)
        TT("dve", lnr[:, 0, :], lnr[:, 0, :], mod[:, 4 * D:5 * D], ALU.mult)
        rv = rows_d.ap
        STORE("sp", rows_d.at(0, rv[:, 0, :]), mod[:, 0:D])
        STORE("sp", rows_d.at(1, rv[:, 1, :]), mod[:, D:2 * D])
        STORE("sp", rows_d.at(2, rv[:, 2, :]), mod[:, 2 * D:3 * D])
        STORE("sp", rows_d.at(3, rv[:, 3, :]), lnr[:, 0, :])
        STORE("sp", rows_d.at(4, rv[:, 4, :]), lnr[:, 1, :])
        STORE("sp", rows_d.at(5, rv[:, 5, :]), mod[:, 5 * D:6 * D])
        LOAD("sp", [lbt[:, 0, :], lbt[:, 1, :]], [I["lbl5"].at("w", I["lbl5"].ap[0]), I["lbl5"].at("w", I["lbl5"].ap[1])])
        TT("dve", lbt[:, 0, :], lbt[:, 0, :], lbt[:, 1, :], ALU.subtract)
        ACT(lbo[:, 0, :], lbt[:, 0, :], AF.Sigmoid)
        TS("dve", lbo[:, 1, :], lbo[:, 0, :], -0.5, 0.5, ALU.mult, ALU.add)
        TT("dve", lbo[:, 0, :], lbo[:, 0, :], lbo[:, 1, :], ALU.add)
        STORE("sp", lb_d.at("w"), lbo[:])
        P.barrier()

        mrow = alloc(es, "mrow", [128, 2, 2, D], F32)
        for cnd in range(2):
            rowload("sp", mrow[:, cnd, 0, :], rows_d.at(0, rv[cnd, 0, :]))
            rowload("sp", mrow[:, cnd, 1, :], rows_d.at(1, rv[cnd, 1, :]))
        xt = [alloc(es, "xt%d" % i, [128, D], F32) for i in range(3)]
        hb = [alloc(es, "hb%d" % i, [128, D], BF16) for i in range(2)]
        hTt = [alloc(es, "hTt%d" % i, [128, 8, 128], BF16) for i in range(2)]
        def xload(ti):
            src = I["xp"].ap[ti * 128:(ti + 1) * 128, :] if ti < 8 else I["xs"].ap[(ti - 8) * 128:(ti - 7) * 128, :]
            LOAD("sp", xt[ti % 3][:], (I["xp"] if ti < 8 else I["xs"]).at("w", src))
        xload(0)
        xload(1)
        for ti in range(40):
            cnd = 0 if ti < 8 else 1
            x_ = xt[ti % 3]
            h_ = hb[ti % 2]
            o_ = hTt[ti % 2]
            if ti + 2 < 40:
                xload(ti + 2)
            TT("dve", x_[:], x_[:], mrow[:, cnd, 1, :], ALU.mult)
            TT("pool", h_[:], x_[:], mrow[:, cnd, 0, :], ALU.add)
            for k in range(8):
                TR(pt[ti % 2][:, k * 128:(k + 1) * 128], h_[:, k * 128:(k + 1) * 128], ident[:])
            CP("act", o_[:].r("p k t -> p (k t)"), pt[ti % 2][:])
            STORE("sp", hT_d.at(ti, hT_d.ap[ti]), o_[:])
        P.barrier()
    if stop_after == 0:
        return finish(nc, P, es_all, [yp, ys, st_out])

    es_ab = contextlib.ExitStack()
    cmat = alloc(es_ab, "cmat", [128, 4, 128], F32)
    cols2 = alloc(es_ab, "cols2", [128, 2, 2], F32)
    masks = alloc(es_ab, "masks", [128, 2, 512], BF16)
    Sst = [[alloc(es_ab, "S%d_%d" % (d, hh), [128, 4, 128], F32) for hh in range(2)] for d in range(2)]
    PW = [alloc(es_ab, "PW%d" % i, [128, 512], F32, psum=True) for i in range(2)]
    PC = [alloc(es_ab, "PC%d" % i, [128, 512], F32, psum=True) for i in range(2)]
    PS = [alloc(es_ab, "PS%d" % i, [128, 512], F32, psum=True) for i in range(2)]
    PT = [alloc(es_ab, "PT%d" % i, [128, 1024], BF16, psum=True) for i in range(2)]
    LOAD("sp", cmat[:], I["cmat"].at("w"))
    LOAD("sp", cols2[:], I["cols2"].at("w"))
    LOAD("sp", masks[:], I["masks"].at("w"))
    for d in range(2):
        for hh in range(2):
            LOAD("sp", Sst[d][hh][:], I["s0"].at("w", I["s0"].ap[:, 8 * d + 4 * hh:8 * d + 4 * hh + 4, :]))

    def hv4(v):
        return v.r("p (h v) -> p h v", h=4)

    def run_rr(gens):
        alive = [True] * len(gens)
        while any(alive):
            for i, g in enumerate(gens):
                if alive[i]:
                    try:
                        next(g)
                    except StopIteration:
                        alive[i] = False

    with contextlib.ExitStack() as es:
        Wi = alloc(es, "Wi", [128, 8, D], BF16)
        Wf = [alloc(es, "Wf%d" % i, [128, 8, D], BF16) for i in range(2)]
        lbr = alloc(es, "lbr", [128, 2, D], F32)
        mctx = alloc(es, "mctx", [128, 3, 128], F32)
        mfb = alloc(es, "mfb", [128, 3, 2], F32)
        hTa = [alloc(es, "hTa%d" % i, [128, 8, 128], BF16) for i in range(4)]
        PTf = [V(PT[k].t[:].bitcast(F32), PT[k].b, PT[k]) for k in range(2)]
        banksA = {(0, 0): (PW[0][:], PC[0][:]), (0, 1): (PW[1][:], PC[1][:]),
                  (1, 0): (PS[0][:], PTf[0]), (1, 1): (PS[1][:], PTf[1])}
        scA = {}
        for tp in range(2):
            for hh in range(2):
                sfx = "A%d%d" % (tp, hh)
                scA[(tp, hh)] = dict(
                    sg=alloc(es, "sg" + sfx, [128, 512], F32), lf=alloc(es, "lf" + sfx, [128, 512], F32),
                    kk=alloc(es, "kk" + sfx, [128, 512], F32), kst=alloc(es, "kst" + sfx, [128, 512], BF16),
                    vb=alloc(es, "vb" + sfx, [128, 512], BF16), Dt=alloc(es, "Dt" + sfx, [128, 4], F32),
                    al=alloc(es, "al" + sfx, [128, 4], F32), be=alloc(es, "be" + sfx, [128, 4], F32),
                    tmpU=alloc(es, "tmpU" + sfx, [128, 4, 128], F32))
        aggA = [dict(Dagg=alloc(es, "DaggA%d" % hh, [128, 4], F32), Uagg=alloc(es, "UaggA%d" % hh, [128, 4, 128], F32),
                     al=alloc(es, "alS%d" % hh, [128, 4], F32)) for hh in range(2)]
        LOAD("pool", Wi[:], wview(I["w_in"], 3072, 4096))
        LOAD("sp", mctx[:], I["mctx"].at("w"))
        LOAD("sp", mfb[:], I["mfb"].at("w"))

        def ctx_gen(sl, tp, hh, hT, W_):
            c = scA[(tp, hh)]
            g = aggA[hh]
            X, Y = banksA[(tp, hh)]
            cs = slice(hh * 512, (hh + 1) * 512)
            sg, lf, kk, kst, vb = c["sg"], c["lf"], c["kk"], c["kst"], c["vb"]
            Dt, al, be, tmpU = c["Dt"], c["al"], c["be"], c["tmpU"]
            Dagg, Uagg = g["Dagg"], g["Uagg"]
            for k in range(8):
                MM(X, hT[:, k, :], W_[:, k, cs], start=(k == 0), stop=(k == 7))
            ACT(sg[:], X, AF.Tanh, scale=0.5)
            yield
            TT("dve", sg[:], sg[:], lbr[:, 1, cs], ALU.mult)
            TT("dve", sg[:], sg[:], lbr[:, 0, cs], ALU.add)
            yield
            ACT(lf[:], sg[:], AF.Ln)
            TS("pool", kk[:], sg[:], -1.0, 1.0, ALU.mult, ALU.add)
            yield
            MM(Y, mctx[:, sl, :], lf[:])
            ACT(sg[:], Y, AF.Exp)
            TT("dve", kst[:], kk[:], sg[:], ALU.mult)
            yield
            for h in range(4):
                MM(X[:, h:h + 1], lf[:, h * 128:(h + 1) * 128], ones1[:])
            ACT(Dt[:], X[:, 0:4], AF.Exp)
            yield
            for k in range(8):
                MM(Y, hT[:, k, :], Wi[:, k, cs], start=(k == 0), stop=(k == 7))
            CP("act", vb[:], Y)
            yield
            for h in range(4):
                MM(X[:, h * 128:(h + 1) * 128], kst[:, h * 128:(h + 1) * 128], vb[:, h * 128:(h + 1) * 128])
            yield
            TS("dve", al[:], Dt[:], -1.0, mfb[:, sl, 0:1], ALU.add, ALU.mult)
            TS("dve", al[:], al[:], 1.0, None, ALU.add)
            TS("dve", be[:], Dagg[:], -1.0, mfb[:, sl, 1:2], ALU.add, ALU.mult)
            TS("dve", be[:], be[:], 1.0, None, ALU.add)
            TT("dve", Dagg[:], Dagg[:], Dt[:], ALU.mult)
            TT("dve", tmpU[:], hv4(X), be[:].bc(2, [128, 4, 128]), ALU.mult)
            TT("pool", Uagg[:], Uagg[:], al[:].bc(2, [128, 4, 128]), ALU.mult)
            TT("pool", Uagg[:], Uagg[:], tmpU[:], ALU.add)
            yield

        def hload(ti):
            LOAD("sp", hTa[ti % 4][:], hT_d.at(ti, hT_d.ap[ti]))
        hload(16)
        hload(17)
        for sl in range(3):
            W_ = Wf[sl % 2]
            LOAD("pool", W_[:], I["w_fsel"].at("w", I["w_fsel"].ap[sl].rearrange("(k p) n -> p k n", p=128)))
            rowload("sp", lbr[:, 0, :], lb_d.at("w", lb_d.ap[2 + sl, 0, :]))
            rowload("sp", lbr[:, 1, :], lb_d.at("w", lb_d.ap[2 + sl, 1, :]))
            for hh in range(2):
                MEMSET("dve", aggA[hh]["Uagg"][:], 0.0)
                MEMSET("dve", aggA[hh]["Dagg"][:], 1.0)
            for t in range(0, 8, 2):
                ti = 16 + sl * 8 + t
                for nx in (ti + 2, ti + 3):
                    if nx < 40:
                        hload(nx)
                gens = []
                for tp in range(2):
                    for hh in range(2):
                        gens.append(ctx_gen(sl, tp, hh, hTa[(ti + tp) % 4], W_))
                run_rr(gens)
            for hh in range(2):
                g = aggA[hh]
                for d in range(2):
                    TS("dve", g["al"][:], g["Dagg"][:], -1.0, mfb[:, sl, d:d + 1], ALU.add, ALU.mult)
                    TS("dve", g["al"][:], g["al"][:], 1.0, None, ALU.add)
                    TT("dve", Sst[d][hh][:], Sst[d][hh][:], g["al"][:].bc(2, [128, 4, 128]), ALU.mult)
                    STT("dve", Sst[d][hh][:], g["Uagg"][:], mfb[:, sl, d:d + 1], Sst[d][hh][:], ALU.mult, ALU.add)
        P.barrier()
    if stop_after == 1:
        return finish(nc, P, es_all, [yp, ys, st_out])

    with contextlib.ExitStack() as es:
        Wg = {}
        for gi, (nm, c0) in enumerate((("q", 0), ("ff", 1024), ("i", 3072), ("fb", 2048), ("g", 4096))):
            Wg[nm] = alloc(es, "W" + nm, [128, 8, D], BF16)
            LOAD("pool", Wg[nm][:], wview(I["w_in"], c0, c0 + 1024))
        lbr = alloc(es, "lbrB", [128, 2, 2, D], F32)
        nrow = alloc(es, "nrow", [128, D], F32)
        for d in range(2):
            rowload("sp", lbr[:, d, 0, :], lb_d.at("w", lb_d.ap[d, 0, :]))
            rowload("sp", lbr[:, d, 1, :], lb_d.at("w", lb_d.ap[d, 1, :]))
        rowload("sp", nrow[:], I["normw"].at("w"))
        hTa = [alloc(es, "hTb%d" % i, [128, 8, 128], BF16) for i in range(2)]
        scB = []
        for hh in range(2):
            c = {}
            for nm in ("qs", "sg", "lf", "kk", "e1", "osum", "gs"):
                c[nm] = alloc(es, nm + "B%d" % hh, [128, 512], F32)
            for nm in ("qin", "kin", "kst", "vb", "scm", "oab"):
                c[nm] = alloc(es, nm + "B%d" % hh, [128, 512], BF16)
            for nm in ("qinT", "kinT", "Sq"):
                c[nm] = alloc(es, nm + "B%d" % hh, [128, 4, 128], BF16)
            c["eb"] = alloc(es, "ebB%d" % hh, [128, 4, 2], F32)
            c["ssq"] = alloc(es, "ssqB%d" % hh, [128, 4], F32)
            c["ofb"] = alloc(es, "ofbB%d" % hh, [128, 8, 512], BF16)
            c["Sp"] = [alloc(es, "SpB%d_%d" % (hh, d), [128, 4, 128], F32) for d in range(2)]
            c["oaT"] = [alloc(es, "oaTB%d_%d" % (hh, i), [128, 4, 128], BF16) for i in range(2)]
            c["no"] = 0
            scB.append(c)

        def pass_gen(d, S, slot, final, tokcol, hh, hT, tix):
            c = scB[hh]
            cs = slice(hh * 512, (hh + 1) * 512)
            PWh, PCh, PSh, PTh = PW[hh], PC[hh], PS[hh], PT[hh]
            qs, sg, lf, kk, e1, osum, gs = c["qs"], c["sg"], c["lf"], c["kk"], c["e1"], c["osum"], c["gs"]
            qin, kin, kst, vb, scm, oab = c["qin"], c["kin"], c["kst"], c["vb"], c["scm"], c["oab"]
            qinT, kinT, Sq, eb, ssq, ofb = c["qinT"], c["kinT"], c["Sq"], c["eb"], c["ssq"], c["ofb"]

            def proj(W, dst):
                for k in range(8):
                    MM(dst[:], hT[:, k, :], W[:, k, cs], start=(k == 0), stop=(k == 7))
            if d == 0:
                proj(Wg["q"], PWh)
                ACT(qs[:], PWh[:], AF.Tanh, scale=0.5)
            else:
                LOAD("sp", qs[:], qc_d.at((tix, hh), qc_d.ap[tix, hh]))
                LOAD("sp", vb[:], vc_d.at((tix, hh), vc_d.ap[tix, hh]))
            proj(Wg["ff" if d == 0 else "fb"], PCh)
            ACT(sg[:], PCh[:], AF.Tanh, scale=0.5)
            if d == 0:
                STT("dve", qs[:], qs[:], 1.0, PWh[:], ALU.add, ALU.mult)
                STORE("sp", qc_d.at((tix, hh), qc_d.ap[tix, hh]), qs[:])
            yield
            TT("dve", sg[:], sg[:], lbr[:, d, 1, cs], ALU.mult)
            TT("dve", sg[:], sg[:], lbr[:, d, 0, cs], ALU.add)
            if d == 0:
                proj(Wg["i"], PWh)
                CP("act", vb[:], PWh[:])
                STORE("sp", vc_d.at((tix, hh), vc_d.ap[tix, hh]), vb[:])
            yield
            ACT(lf[:], sg[:], AF.Ln)
            TS("pool", kk[:], sg[:], -1.0, 1.0, ALU.mult, ALU.add)
            yield
            MM(PCh[:], cmat[:, 2 * d, :], lf[:])
            ACT(e1[:], PCh[:], AF.Exp)
            STT("dve", qin[:], qs[:], 0.5, e1[:], ALU.mult, ALU.mult)
            yield
            ACT(e1[:], PCh[:], AF.Exp, scale=-1.0)
            TT("pool", kin[:], kk[:], e1[:], ALU.mult)
            MM(PWh[:], cmat[:, 2 * d + 1, :], lf[:])
            yield
            ACT(e1[:], PWh[:], AF.Exp)
            TT("dve", kst[:], kk[:], e1[:], ALU.mult)
            for h in range(4):
                MM(PSh[:, 2 * h:2 * h + 2], lf[:, h * 128:(h + 1) * 128], cols2[:, d, :])
            ACT(eb[:].r("p h t -> p (h t)"), PSh[:, 0:8], AF.Exp)
            yield
            for h in range(4):
                TR(PTh[:, h * 128:(h + 1) * 128], qin[:, h * 128:(h + 1) * 128], ident[:])
            CP("act", qinT[:].r("p h t -> p (h t)"), PTh[:, 0:512])
            for h in range(4):
                TR(PTh[:, 512 + h * 128:512 + (h + 1) * 128], kin[:, h * 128:(h + 1) * 128], ident[:])
            CP("dve", kinT[:].r("p h t -> p (h t)"), PTh[:, 512:1024])
            TT("pool", Sq[:], S[:], eb[:, :, 0].bc(2, [128, 4, 128]), ALU.mult)
            yield
            for h in range(4):
                MM(PSh[:, h * 128:(h + 1) * 128], kinT[:, h, :], qinT[:, h, :])
            TT("dve", scm[:], PSh[:], masks[:, d, :], ALU.mult)
            yield
            for h in range(4):
                MM(PCh[:, h * 128:(h + 1) * 128], scm[:, h * 128:(h + 1) * 128], vb[:, h * 128:(h + 1) * 128],
                   start=True, stop=False)
                MM(PCh[:, h * 128:(h + 1) * 128], qinT[:, h, :], Sq[:, h, :], start=False, stop=True)
            if not final:
                CP("act", ofb[:, slot, :], PCh[:])
            else:
                TT("dve", osum[:], PCh[:], ofb[:, slot, :], ALU.add)
            for h in range(4):
                MM(PSh[:, h * 128:(h + 1) * 128], kst[:, h * 128:(h + 1) * 128], vb[:, h * 128:(h + 1) * 128])
            yield
            TT("pool", S[:], S[:], eb[:, :, 1].bc(2, [128, 4, 128]), ALU.mult)
            TT("dve", S[:], S[:], hv4(PSh[:]), ALU.add)
            yield
            if final:
                proj(Wg["g"], PWh)
                ACT(gs[:], PWh[:], AF.Tanh, scale=0.5)
                STT("dve", gs[:], gs[:], 1.0, PWh[:], ALU.add, ALU.mult)
                TT("pool", e1[:], osum[:], osum[:], ALU.mult)
                P.op("dve", lambda h_: h_.tensor_reduce(ssq[:].ap, hv4(e1[:]).ap, AX.X, ALU.add),
                     reads=[e1.b], writes=[ssq.b])
                TS("dve", ssq[:], ssq[:], 1.0 / 128.0, 1e-6, ALU.mult, ALU.add)
                ACT(ssq[:], ssq[:], AF.Ln)
                ACT(ssq[:], ssq[:], AF.Exp, scale=-0.5)
                yield
                TT("dve", hv4(osum[:]), hv4(osum[:]), ssq[:].bc(2, [128, 4, 128]), ALU.mult)
                TT("pool", osum[:], osum[:], nrow[:, cs], ALU.mult)
                STT("dve", oab[:], osum[:], 0.5, gs[:], ALU.mult, ALU.mult)
                yield
                for h in range(4):
                    TR(PTh[:, h * 128:(h + 1) * 128], oab[:, h * 128:(h + 1) * 128], ident[:])
                o_ = c["oaT"][c["no"] % 2]
                c["no"] += 1
                CP("act", o_[:].r("p h t -> p (h t)"), PTh[:, 0:512])
                STORE("sp", oaT_d.at(("t", tokcol, hh), oaT_d.ap[:, 4 * hh:4 * hh + 4, tokcol:tokcol + 128]), o_[:])
                yield

        sched = []
        for j in range(4):
            sched += [(2 * j, 0, j, 0, False), (2 * j + 1, 0, j, 1, False), (2 * j + 1, 1, j, 1, True), (2 * j, 1, j, 0, True)]
        sched += [(8 + t, 0, 4, t, False) for t in range(8)] + [(8 + t, 1, 4, t, True) for t in range(7, -1, -1)]
        LOAD("sp", hTa[0][:], hT_d.at(sched[0][0], hT_d.ap[sched[0][0]]))
        for idx, (ti, d, unit, slot, final) in enumerate(sched):
            hT = hTa[idx % 2]
            if idx + 1 < len(sched):
                nti = sched[idx + 1][0]
                LOAD("sp", hTa[(idx + 1) % 2][:], hT_d.at(nti, hT_d.ap[nti]))
            tokcol = ti * 128 if unit < 4 else 1024 + (ti - 8) * 128
            if unit < 4 and d == 0 and slot == 0:
                for hh in range(2):
                    MEMSET("dve", scB[hh]["Sp"][0][:], 0.0)
                    MEMSET("pool", scB[hh]["Sp"][1][:], 0.0)
            gens = []
            for hh in range(2):
                S = scB[hh]["Sp"][d] if unit < 4 else Sst[d][hh]
                gens.append(pass_gen(d, S, slot, final, tokcol, hh, hT, ti))
            run_rr(gens)
            if unit < 4 and ((d == 0 and slot == 1) or (d == 1 and slot == 0)):
                for hh in range(2):
                    STORE("sp", st_out.at((d, unit, hh), st_out.ap[unit, d, 4 * hh:4 * hh + 4].rearrange("h a v -> a h v")),
                          scB[hh]["Sp"][d][:])
        P.barrier()
    es_ab.close()
    if stop_after == 2:
        return finish(nc, P, es_all, [yp, ys, st_out])

    with contextlib.ExitStack() as es:
        pa = [alloc(es, "q%d" % i, [128, 512], F32, psum=True) for i in range(7)]
        pt = [alloc(es, "ptc", [128, 1024], BF16, psum=True)]
        rot = {"i": 0}

        def nb():
            rot["i"] += 1
            return pa[4 + rot["i"] % 3]
        dS = alloc(es, "dS", [128, 4, 8, 1024], BF16)
        dP = alloc(es, "dP", [128, 4, 2, 256], BF16)
        LOAD("act", [dS[:, i] for i in range(4)], [I["dftS"].at("w", I["dftS"].ap[:, i]) for i in range(4)])
        LOAD("act", dP[:], I["dftP"].at("w"))
        absr = alloc(es, "absr", [128, D], F32)
        rowload("sp", absr[:], I["absd"].at("w"))
        negt = alloc(es, "negt", [128, 66], F32)
        convw = alloc(es, "convw", [128, 24, 3], F32)
        convb = alloc(es, "convb", [128, 24], F32)
        skipT = alloc(es, "skipT", [128, 8], F32)
        LOAD("sp", negt[:], I["negt"].at("w"))
        LOAD("sp", convw[:], I["convw"].at("w"))
        LOAD("sp", convb[:], I["convb"].at("w"))
        LOAD("sp", skipT[:], I["skipT"].at("w"))
        h3T = alloc(es, "h3T", [64, 8448], BF16)
        fw1 = alloc(es, "fw1", [33, 64], F32)
        fw2 = alloc(es, "fw2", [64, 64], F32)
        fw3 = alloc(es, "fw3", [64, 64], F32)
        fbt = alloc(es, "fbt", [64, 3], F32)
        frq = alloc(es, "frq", [64, 3], F32)

        Wx = [alloc(es, "Wx%d" % i, [128, 8, 3, 128], BF16) for i in range(2)]
        w4c = [alloc(es, "w4c%d" % i, [64, 10, 128], BF16) for i in range(2)]
        hTblk = [alloc(es, "hTblk%d" % i, [128, 8, 512], BF16) for i in range(2)]
        cy = [alloc(es, "cy%d" % i, [128, 512], F32) for i in range(2)]
        ub2 = [alloc(es, "ub%d" % i, [128, 512], BF16) for i in range(2)]
        uTk = [alloc(es, "uTk%d" % i, [128, 2048], BF16) for i in range(2)]
        x0k = [alloc(es, "x0k%d" % i, [128, 2048], BF16) for i in range(2)]
        utm = [alloc(es, "utm%d" % i, [128, 40, 128], BF16) for i in range(2)]
        ew = alloc(es, "ew", [128, 512], F32)
        hwf = alloc(es, "hwf", [128, 512], F32)
        hwb = alloc(es, "hwb", [128, 512], F32)
        hpm2 = [alloc(es, "hpm%d" % i, [128, 2, 8, 128], BF16) for i in range(2)]
        Ksb = alloc(es, "Ksb", [128, 2, 1024], BF16)
        CSsb2 = [alloc(es, "CSsb%d" % i, [128, 2, 1024], BF16) for i in range(2)]
        identf = alloc(es, "identf", [128, 128], F32)
        CP("dve", identf[:], ident[:])
        tA = alloc(es, "tA", [128, 1024], F32)
        PQ = alloc(es, "PQ", [128, 2, 1024], F32)
        PQT = alloc(es, "PQT", [128, 2, 8, 128], BF16)

        def taps(ct, s, njt, tile_f, tile_b, wf, wb, same_pos, hpm):
            for g0 in range(0, njt, 4):
                ng = min(4, njt - g0)
                for side, (tb, wi, dst) in enumerate(((tile_f, wf, hwf), (tile_b, wb, hwb))):
                    for i in range(ng):
                        tcol = (tb + g0 + i) * 128
                        MM(pa[2 + side][:, i * 128:(i + 1) * 128], h3T[:, tcol:tcol + 128], w4c[s][:, wi, :])
                    if side == 0 or not same_pos:
                        for i in range(ng):
                            ACT(ew[:, i * 128:(i + 1) * 128], absr[:, ct * 128:(ct + 1) * 128], AF.Exp,
                                scale=negt[:, tb + g0 + i:tb + g0 + i + 1])
                    STT("dve", dst[:, 0:ng * 128], ew[:, 0:ng * 128], 0.05, pa[2 + side][:, 0:ng * 128], ALU.add, ALU.mult)
                    if side == 1 and g0 == 0:
                        MEMSET("dve", dst[0:1, 0:128], 0.0)
                TT("pool", hpm[:, 0, g0:g0 + ng, :],
                   hwf[:, 0:ng * 128].r("p (j c) -> p j c", j=ng), hwb[:, 0:ng * 128].r("p (j c) -> p j c", j=ng), ALU.add)
                TT("pool", hpm[:, 1, g0:g0 + ng, :],
                   hwf[:, 0:ng * 128].r("p (j c) -> p j c", j=ng), hwb[:, 0:ng * 128].r("p (j c) -> p j c", j=ng),
                   ALU.subtract)
                yield

        def pq_update(first, CSsb):
            Cu, Su = CSsb[:, 0, :], CSsb[:, 1, :]
            Kr, Ki = Ksb[:, 0, :], Ksb[:, 1, :]
            Pv, Qv = PQ[:, 0, :], PQ[:, 1, :]
            if first:
                TT("dve", Pv, Cu, Kr, ALU.mult)
                TT("pool", Qv, Su, Kr, ALU.mult)
            else:
                TT("dve", tA[:], Cu, Kr, ALU.mult)
                TT("dve", Pv, Pv, tA[:], ALU.add)
                TT("dve", tA[:], Su, Kr, ALU.mult)
                TT("dve", Qv, Qv, tA[:], ALU.add)
            TT("dve", tA[:], Su, Ki, ALU.mult)
            TT("dve", Pv, Pv, tA[:], ALU.add)
            TT("dve", tA[:], Cu, Ki, ALU.mult)
            TT("dve", Qv, Qv, tA[:], ALU.subtract)

        def pq_transpose():
            for w in range(2):
                for g in range(2):
                    bk = nb()
                    for i in range(4):
                        TR(bk[:, i * 128:(i + 1) * 128], PQ[:, w, (4 * g + i) * 128:(4 * g + i + 1) * 128], identf[:])
                    CP("act", PQT[:, w, 4 * g:4 * g + 4, :].r("p i c -> p (i c)"), bk[:])

        def epilogue_half(ct, s, tok0, half, bk):
            hs = slice(half * 512, (half + 1) * 512)
            ts_ = slice(tok0 + half * 512, tok0 + (half + 1) * 512)
            STT("dve", tA[:, hs], uTk[s][:, ts_], skipT[:, ct:ct + 1], bk[:], ALU.mult, ALU.add)
            TT("pool", Ksb[:, 0, hs], tA[:, hs], x0k[s][:, ts_], ALU.mult)
            if half == 1:
                STORE("sp", obT_d.at(("c", ct, tok0), obT_d.ap[:, ct, tok0:tok0 + 1024]), Ksb[:, 0, :])

        def blk_load(bi):
            LOAD("sp", [hTblk[bi % 2][:, :, i * 128:(i + 1) * 128] for i in range(4)],
                 [hT_d.at(4 * bi + i, hT_d.ap[4 * bi + i]) for i in range(4)])

        def part1(ct, s):
            LOAD("pool", [Wx[s][:, :, g, :] for g in range(3)],
                 [wview(I["w_in"], 5120 + g * 1024 + ct * 128, 5120 + g * 1024 + (ct + 1) * 128) for g in range(3)])
            LOAD("pool", w4c[s][:], I["w4all"].at("w", I["w4all"].ap[:, :, ct * 128:(ct + 1) * 128]))
            blk_load(0)

            def u_transpose(bi):
                ub = ub2[bi % 2]
                for i in range(4):
                    TR(pt[0][:, i * 128:(i + 1) * 128], ub[:, i * 128:(i + 1) * 128], ident[:])
                CP("act", utm[s][:, 4 * bi:4 * bi + 4, :].r("p i c -> p (i c)"), pt[0][:, 0:512])

            for bi in range(10):
                if bi + 1 < 10:
                    blk_load(bi + 1)
                hb_ = hTblk[bi % 2]
                ub = ub2[bi % 2]
                groups = (1, 2, 0) if bi < 4 else (1, 2)
                rl = 256 if bi < 2 else 64
                for gi, g in enumerate(groups):
                    ps = pa[gi % 2]
                    for k in range(8):
                        MM(ps[:], Wx[s][:, k, g, :], hb_[:, k, :], start=(k == 0), stop=(k == 7))
                    ci = g * 8 + ct
                    if gi == 2:
                        TT("pool", ub[:], cy[0][:], cy[1][:], ALU.mult)
                    y_ = cy[gi % 2]
                    ACT(y_[:], ps[:], AF.Identity, bias=convb[:, ci:ci + 1], scale=convw[:, ci, 1:2])
                    y3 = y_[:].r("p (r t) -> p r t", t=rl)
                    x3 = ps[:].r("p (r t) -> p r t", t=rl)
                    STT("dve", y3[:, :, 1:rl], x3[:, :, 0:rl - 1], convw[:, ci, 0:1], y3[:, :, 1:rl], ALU.mult, ALU.add)
                    STT("dve", y3[:, :, 0:rl - 1], x3[:, :, 1:rl], convw[:, ci, 2:3], y3[:, :, 0:rl - 1], ALU.mult, ALU.add)
                if len(groups) == 2:
                    TT("pool", ub[:], cy[0][:], cy[1][:], ALU.mult)
                if bi < 4:
                    CP("pool", uTk[s][:, bi * 512:(bi + 1) * 512], ub[:])
                    CP("act", x0k[s][:, bi * 512:(bi + 1) * 512], cy[0][:])
                if bi > 0:
                    u_transpose(bi - 1)
                yield
            u_transpose(9)
            yield

        def part2(ct, s):
            U = utm[s]

            def cusu_prompt(CSsb):
                for w in range(2):
                    for jp in range(2):
                        bk = nb()
                        for j in (2 * jp, 2 * jp + 1):
                            for nt in range(2):
                                MM(bk[:, (j % 2) * 256:(j % 2 + 1) * 256], U[:, 2 * j + nt, :], dP[:, w, nt, :],
                                   start=(nt == 0), stop=(nt == 1))
                        CP("act", CSsb[:, w, jp * 512:(jp + 1) * 512], bk[:])

            def cusu_sample(kb, CSsb):
                for w in range(2):
                    for half in range(2):
                        bk = nb()
                        for nt in range(8):
                            MM(bk[:], U[:, 8 + 8 * kb + nt, :],
                               dS[:, w, nt, half * 512:(half + 1) * 512], start=(nt == 0), stop=(nt == 7))
                        CP("act", CSsb[:, w, half * 512:(half + 1) * 512], bk[:])

            def k_sample(hpm):
                for w in range(2):
                    for half in range(2):
                        bk = nb()
                        for jt in range(8):
                            MM(bk[:], hpm[:, w, jt, :],
                               dS[:, 2 + w, jt, half * 512:(half + 1) * 512], start=(jt == 0), stop=(jt == 7))
                        CP("act", Ksb[:, w, half * 512:(half + 1) * 512], bk[:])

            def stap(kb):
                return taps(ct, s, 8, 2 + 16 * kb, 2 + 16 * kb + 8, 2 + 2 * kb, 3 + 2 * kb, False, hpm2[(kb + 1) % 2])

            yield from taps(ct, s, 2, 0, 0, 0, 1, True, hpm2[0])
            cusu_prompt(CSsb2[0])
            yield
            yield from stap(0)
            hp_ = hpm2[0]
            bk = nb()
            for jt in range(2):
                MM(bk[:, 0:256], hp_[:, 0, jt, :], dP[:, 2, jt, :], start=(jt == 0), stop=(jt == 1))
            for jt in range(2):
                MM(bk[:, 256:512], hp_[:, 1, jt, :], dP[:, 3, jt, :], start=(jt == 0), stop=(jt == 1))
            for j in range(4):
                CP("act", Ksb[:, 0, j * 256:(j + 1) * 256], bk[:, 0:256])
                CP("act", Ksb[:, 1, j * 256:(j + 1) * 256], bk[:, 256:512])
            yield
            cusu_sample(0, CSsb2[1])
            yield
            pq_update(True, CSsb2[0])
            yield
            pq_transpose()
            for jp in range(2):
                bk = nb()
                for j in (2 * jp, 2 * jp + 1):
                    n = 0
                    for w in range(2):
                        for ft in range(2):
                            MM(bk[:, (j % 2) * 256:(j % 2 + 1) * 256], PQT[:, w, 2 * j + ft, :], dP[:, w, ft, :],
                               start=(n == 0), stop=(n == 3))
                            n += 1
                epilogue_half(ct, s, 0, jp, bk)
            yield
            for kb in range(4):
                if kb + 1 < 4:
                    yield from stap(kb + 1)
                k_sample(hpm2[(kb + 1) % 2])
                yield
                if kb + 1 < 4:
                    cusu_sample(kb + 1, CSsb2[kb % 2])
                    yield
                pq_update(kb == 0, CSsb2[(kb + 1) % 2])
                yield
            pq_transpose()
            for half in range(2):
                bk = nb()
                n = 0
                for w in range(2):
                    for ft in range(8):
                        MM(bk[:], PQT[:, w, ft, :],
                           dS[:, w, ft, half * 512:(half + 1) * 512], start=(n == 0), stop=(n == 15))
                        n += 1
                epilogue_half(ct, s, 1024, half, bk)
            yield

        zb1 = Buf("z1buf")
        zsem = [PQ.sem(), P.new_sem("zsem1")]

        def mlp_setup():
            LOAD("sp", fw1[:], I["fw1"].at("w"))
            LOAD("sp", fw2[:], I["fw2"].at("w"))
            LOAD("sp", fw3[:], I["fw3"].at("w"))
            LOAD("sp", fbt[:], I["fb"].at("w"))
            LOAD("sp", frq[:], I["ffreq"].at("w"))
            TT("dve", fbt[:], fbt[:], frq[:], ALU.mult)

        def mlp_gen(cid):
            ws = [fw1, fw2, fw3]
            if cid == 0:
                aa, kf, ss = tA[0:64, 0:512], tA[0:64, 512:1024], PQ[0:64, 0, 0:512]
                zv = PQ[0:33, 0, 512:1024]
                banks = (pa[4], pa[5])
            else:
                aa, kf, ss = ew[0:64, :], hwf[0:64, :], hwb[0:64, :]
                zv = V(PQ.t[0:33, 1, 0:512], zb1, None)
                banks = (pa[6], pa[2])
            for ch in range(cid, 17, 2):
                c0 = ch * 512
                n = min(512, 8448 - c0)
                P.dma("sp", zsem[cid], [(zv[:, 0:n].ap, I["zt"].ap[:, c0:c0 + n])], reads=[], writes=[zv.b])
                src = zv
                for l in range(3):
                    pv = banks[l % 2][0:64, 0:n]
                    MM(pv, ws[l][:], src[:, 0:n])
                    TS("dve", aa[:, 0:n], pv, frq[:, l:l + 1], fbt[:, l:l + 1], ALU.mult, ALU.add)
                    TS("dve", kf[:, 0:n], aa[:, 0:n], 1.0 / TWO_PI, MAGIC, ALU.mult, ALU.add)
                    TS("dve", kf[:, 0:n], kf[:, 0:n], -MAGIC, None, ALU.add)
                    STT("dve", aa[:, 0:n], kf[:, 0:n], -TWO_PI, aa[:, 0:n], ALU.mult, ALU.add)
                    if l < 2:
                        ACT(ss[:, 0:n], aa[:, 0:n], AF.Sin)
                        src = ss
                    else:
                        ACT(h3T[:, c0:c0 + n], aa[:, 0:n], AF.Sin)
                    yield

        def chain(*gs):
            for g in gs:
                yield from g

        mlp_setup()
        run_rr([mlp_gen(0), mlp_gen(1), chain(part1(0, 0), part1(1, 1))])
        for ct in range(8):
            g2 = part2(ct, ct % 2)
            g1 = part1(ct + 1, (ct + 1) % 2) if 1 <= ct < 7 else iter(())
            alive = [True, True]
            while alive[0] or alive[1]:
                for _ in range(1):
                    if alive[1]:
                        try:
                            next(g2)
                        except StopIteration:
                            alive[1] = False
                if alive[0]:
                    try:
                        next(g1)
                    except StopIteration:
                        alive[0] = False
        P.barrier()
    if stop_after == 3:
        return finish(nc, P, es_all, [yp, ys, st_out])

    rv = rows_d.ap

    def layer_norm_tile(es_tiles, r, grow, brow, out_main, extra=None):
        stats, mv, xn = es_tiles
        for c2 in range(2):
            P.op("dve", lambda h_, c2=c2: h_.bn_stats(stats[:, c2, :].ap, r[:, c2 * 512:(c2 + 1) * 512].ap),
                 reads=[r.b], writes=[stats.b])
        P.op("dve", lambda h_: h_.bn_aggr(mv[:, 0:2].ap, stats[:].ap), reads=[stats.b], writes=[mv.b])
        TS("dve", mv[:, 2:3], mv[:, 1:2], 1e-5, None, ALU.add)
        ACT(mv[:, 2:3], mv[:, 2:3], AF.Sqrt)
        P.op("dve", lambda h_: h_.reciprocal(mv[:, 3:4].ap, mv[:, 2:3].ap), reads=[mv.b], writes=[mv.b])
        TS("dve", xn[:], r[:], mv[:, 0:1], mv[:, 3:4], ALU.subtract, ALU.mult)
        TT("pool", out_main, xn[:], grow, ALU.mult)
        TT("pool", out_main, out_main, brow, ALU.add)
        if extra is not None:
            G, B, o2, tmp = extra
            TT("dve", tmp, xn[:], G, ALU.mult)
            TT("dve", o2, tmp, B, ALU.add)

    with contextlib.ExitStack() as es:
        pa, pw, pt = psum_std(es)
        pA = alloc(es, "pA", [128, 8, D], BF16)
        pB = alloc(es, "pB", [128, 8, D], BF16)
        wO = alloc(es, "wO", [128, 8, D], BF16)
        Wmg = alloc(es, "Wmg", [128, 8, 2 * D], BF16)
        LOAD("pool", Wmg[:, :, 0:D], wview(I["w_in"], 8192, 8192 + D))
        LOAD("pool", pA[:], wview(I["proj_a"], 0, D))
        LOAD("pool", Wmg[:, :, D:2 * D], wview(I["w_in"], 8192 + D, 10240))
        LOAD("pool", pB[:], wview(I["proj_b"], 0, D))
        LOAD("pool", wO[:], wview(I["w_out"], 0, D))
        oaTh = alloc(es, "oaTh", [128, 8, 1024], BF16)
        obTh = alloc(es, "obTh", [128, 8, 1024], BF16)
        hTh = alloc(es, "hTh", [128, 8, 1024], BF16)
        mT = alloc(es, "mT", [128, 8, 1024], BF16)
        rws = alloc(es, "rws", [128, 5, D], F32)
        gas = alloc(es, "gas", [128, 512], F32)
        gbs = alloc(es, "gbs", [128, 512], F32)
        m1 = alloc(es, "m1", [128, 512], F32)
        m2 = alloc(es, "m2", [128, 512], F32)
        xt = [alloc(es, "xtD%d" % i, [128, D], F32) for i in range(2)]
        rr = alloc(es, "rr", [128, D], F32)
        xn = alloc(es, "xn", [128, D], F32)
        x1t = [alloc(es, "x1t%d" % i, [128, D], F32) for i in range(2)]
        h2b = alloc(es, "h2b", [128, D], BF16)
        h2Tt = [alloc(es, "h2Tt%d" % i, [128, 8, 128], BF16) for i in range(2)]
        stats = alloc(es, "stats", [128, 2, 6], F32)
        mv = alloc(es, "mv", [128, 4], F32)
        for hf in range(2):
            LOAD("sp", oaTh[:], oaT_d.at(("h", hf), oaT_d.ap[:, :, hf * 1024:(hf + 1) * 1024]))
            LOAD("sp", obTh[:], obT_d.at(("h", hf), obT_d.ap[:, :, hf * 1024:(hf + 1) * 1024]))
            LOAD("sp", [hTh[:, :, t * 128:(t + 1) * 128] for t in range(8)],
                 [hT_d.at(hf * 8 + t, hT_d.ap[hf * 8 + t]) for t in range(8)])
            rowload("sp", rws[:, 0, :], rows_d.at(2, rv[hf, 2, :]))
            rowload("sp", rws[:, 1, :], rows_d.at(3, rv[hf, 3, :]))
            rowload("sp", rws[:, 2, :], rows_d.at(4, rv[hf, 4, :]))
            rowload("sp", rws[:, 3, :], I["lnrows"].at("w", I["lnrows"].ap[0]))
            rowload("sp", rws[:, 4, :], I["lnrows"].at("w", I["lnrows"].ap[1]))
            for j in range(8):
                for tc in range(2):
                    ts_ = slice(tc * 512, (tc + 1) * 512)
                    cs_ = slice(j * 128, (j + 1) * 128)
                    cs2 = slice(D + j * 128, D + (j + 1) * 128)
                    for k in range(8):
                        MM(pa[2][:], Wmg[:, k, cs_], hTh[:, k, ts_], start=(k == 0), stop=(k == 7))
                    for k in range(8):
                        MM(pa[3][:], Wmg[:, k, cs2], hTh[:, k, ts_], start=(k == 0), stop=(k == 7))
                    for k in range(8):
                        MM(pa[0][:], pA[:, k, cs_], oaTh[:, k, ts_], start=(k == 0), stop=(k == 7))
                    for k in range(8):
                        MM(pa[1][:], pB[:, k, cs_], obTh[:, k, ts_], start=(k == 0), stop=(k == 7))
                    ACT(gas[:], pa[2][:], AF.Sigmoid)
                    ACT(gbs[:], pa[3][:], AF.Sigmoid)
                    TT("dve", m1[:], gas[:], pa[0][:], ALU.mult)
                    TT("dve", m2[:], gbs[:], pa[1][:], ALU.mult)
                    TT("pool", mT[:, j, ts_], m1[:], m2[:], ALU.add)
            for t in range(8):
                gi = hf * 8 + t
                for half2 in range(2):
                    for k in range(8):
                        MM(pw[:, half2 * 512:(half2 + 1) * 512], mT[:, k, t * 128:(t + 1) * 128],
                           wO[:, k, half2 * 512:(half2 + 1) * 512], start=(k == 0), stop=(k == 7))
                x_ = xt[t % 2]
                if t == 0:
                    LOAD("sp", xt[0][:], (I["xp"] if hf == 0 else I["xs"]).at("w", (I["xp"] if hf == 0 else I["xs"]).ap[0:128, :]))
                if t + 1 < 8:
                    LOAD("sp", xt[(t + 1) % 2][:], (I["xp"] if hf == 0 else I["xs"]).at(
                        "w", (I["xp"] if hf == 0 else I["xs"]).ap[(t + 1) * 128:(t + 2) * 128, :]))
                TT("dve", rr[:], pw[:], rws[:, 0, :], ALU.mult)
                STT("dve", rr[:], x_[:], ALPHA, rr[:], ALU.mult, ALU.add)
                x1_ = x1t[t % 2]
                layer_norm_tile((stats, mv, xn), rr, rws[:, 3, :], rws[:, 4, :], x1_[:],
                                extra=(rws[:, 1, :], rws[:, 2, :], h2b[:], rr[:]))
                STORE("sp", x1_d.at(gi, x1_d.ap[gi]), x1_[:])
                for k in range(8):
                    TR(pt[t % 2][:, k * 128:(k + 1) * 128], h2b[:, k * 128:(k + 1) * 128], ident[:])
                o_ = h2Tt[t % 2]
                CP("act", o_[:].r("p k t -> p (k t)"), pt[t % 2][:])
                STORE("sp", h2T_d.at(gi, h2T_d.ap[gi]), o_[:])
        P.barrier()
    if stop_after == 4:
        return finish(nc, P, es_all, [yp, ys, st_out])

    with contextlib.ExitStack() as es:
        pa, pw, pt = psum_std(es)
        wout = alloc(es, "wout", [128, 22, D], BF16)
        actT = alloc(es, "actT", [128, 22, 1024], BF16)
        h2Th = alloc(es, "h2Th", [128, 8, 1024], BF16)
        Wfi = [alloc(es, "Wfi%d" % i, [128, 8, 2, 256], BF16) for i in range(2)]
        rws = alloc(es, "rws2", [128, 3, D], F32)
        sgt = [alloc(es, "sgt%d" % i, [128, 512], F32) for i in range(2)]
        x1t = [alloc(es, "x1u%d" % i, [128, D], F32) for i in range(2)]
        rr = alloc(es, "rr2", [128, D], F32)
        xn = alloc(es, "xn2", [128, D], F32)
        yt = [alloc(es, "yt%d" % i, [128, D], F32) for i in range(2)]
        stats = alloc(es, "stats2", [128, 2, 6], F32)
        mv = alloc(es, "mv2", [128, 4], F32)
        for hf in range(2):
            LOAD("sp", [h2Th[:, :, t * 128:(t + 1) * 128] for t in range(8)],
                 [h2T_d.at(hf * 8 + t, h2T_d.ap[hf * 8 + t]) for t in range(8)])
            rowload("sp", rws[:, 0, :], rows_d.at(5, rv[hf, 5, :]))
            rowload("sp", rws[:, 1, :], I["lnrows"].at("w", I["lnrows"].ap[2]))
            rowload("sp", rws[:, 2, :], I["lnrows"].at("w", I["lnrows"].ap[3]))
            def wfi_load(fbk):
                W2 = Wfi[fbk % 2]
                LOAD("pool", [W2[:, :, 0, :], W2[:, :, 1, :]],
                     [wview(I["ffn_w_in"], fbk * 256, (fbk + 1) * 256),
                      wview(I["ffn_w_in"], 2816 + fbk * 256, 2816 + (fbk + 1) * 256)])
            wfi_load(0)
            for fbk in range(11):
                W_ = Wfi[fbk % 2]
                if fbk + 1 < 11:
                    wfi_load(fbk + 1)
                if hf == 0 and fbk == 1:
                    LOAD("pool", wout[:], wview(I["ffn_w_out"], 0, D))
                for sub in range(2):
                    j = fbk * 2 + sub
                    for tc in range(2):
                        ts_ = slice(tc * 512, (tc + 1) * 512)
                        pg, pu, sg_ = pa[2 * tc], pa[2 * tc + 1], sgt[tc]
                        for k in range(8):
                            MM(pg[:], W_[:, k, 0, sub * 128:(sub + 1) * 128], h2Th[:, k, ts_],
                               start=(k == 0), stop=(k == 7))
                        for k in range(8):
                            MM(pu[:], W_[:, k, 1, sub * 128:(sub + 1) * 128], h2Th[:, k, ts_],
                               start=(k == 0), stop=(k == 7))
                        ACT(sg_[:], pg[:], AF.Silu)
                        TT("dve", actT[:, j, ts_], sg_[:], pu[:], ALU.mult)
            for t in range(8):
                gi = hf * 8 + t
                for half2 in range(2):
                    for j in range(22):
                        MM(pw[:, half2 * 512:(half2 + 1) * 512], actT[:, j, t * 128:(t + 1) * 128],
                           wout[:, j, half2 * 512:(half2 + 1) * 512], start=(j == 0), stop=(j == 21))
                x1_ = x1t[t % 2]
                if t == 0:
                    LOAD("sp", x1t[0][:], x1_d.at(gi, x1_d.ap[gi]))
                if t + 1 < 8:
                    LOAD("sp", x1t[(t + 1) % 2][:], x1_d.at(gi + 1, x1_d.ap[gi + 1]))
                TT("dve", rr[:], pw[:], rws[:, 0, :], ALU.mult)
                STT("dve", rr[:], x1_[:], ALPHA, rr[:], ALU.mult, ALU.add)
                y_ = yt[t % 2]
                layer_norm_tile((stats, mv, xn), rr, rws[:, 1, :], rws[:, 2, :], y_[:])
                dst = yp if hf == 0 else ys
                STORE("sp", dst.at(t, dst.ap[t * 128:(t + 1) * 128, :]), y_[:])
        P.barrier()
    return finish(nc, P, es_all, [yp, ys, st_out])


def finish(nc, P, es_all, outs):
    P.barrier()
    P.emit()
    try:
        es_all.close()
    except Exception:
        pass
    P.close()
    return nc


_NC_CACHE = {}


def kernel(**inputs):
    I = {k: np.asarray(v) for k, v in inputs.items()}
    if "nc" not in _NC_CACHE:
        _NC_CACHE["nc"] = build_program()
    nc = _NC_CACHE["nc"]
    in_maps = [_core_inputs(r, I) for r in range(8)]
    res = run_bass_kernel_spmd(nc, in_maps, core_ids=list(range(8)))
    y_prompt = np.concatenate([res.results[r]["yp"].reshape(4, 256, D) for r in range(8)], 0)
    y_sample = np.zeros((2, 4096, D), np.float32)
    for r in range(8):
        y_sample[r // 4, 1024 * (r % 4):1024 * (r % 4 + 1)] = res.results[r]["ys"]
    new_state = np.concatenate([res.results[r]["st"].reshape(4, 1, 2, 8, 128, 128) for r in range(8)], 0)
    return (y_prompt.astype(np.float32), y_sample, new_state.astype(np.float32))
```

```python
import math
import contextlib
import numpy as np
import ml_dtypes
import concourse.bass as bass
import concourse.mybir as mybir
from concourse.bass_utils import run_bass_kernel_spmd

F32 = mybir.dt.float32
BF16 = mybir.dt.bfloat16
AF = mybir.ActivationFunctionType
ALU = mybir.AluOpType
AX = mybir.AxisListType

D = 1024
NH = 8
LP = 256
LS = 1024
ALPHA = (2.0 * 1) ** 0.25
MAGIC = 12582912.0
TWO_PI = 2.0 * math.pi
DEBUG = False


class Buf:
    __slots__ = ("name", "wtok", "rtoks")

    def __init__(self, name):
        self.name = name
        self.wtok = []
        self.rtoks = []


class V:
    __slots__ = ("ap", "b", "o")

    def __init__(self, ap, b, o=None):
        self.ap = ap
        self.b = b
        self.o = o

    def __getitem__(self, k):
        return V(self.ap[k], self.b, self.o)

    def r(self, pat, **kw):
        return V(self.ap.rearrange(pat, **kw), self.b, self.o)

    def bc(self, axis, shape):
        return V(self.ap.unsqueeze(axis).to_broadcast(shape), self.b, self.o)

    def pb(self, n=128):
        return V(self.ap.partition_broadcast(n), self.b, self.o)


class Tl:
    def __init__(self, P, t, name):
        self.P = P
        self.t = t
        self.b = Buf(name)
        self.name = name
        self.dsem = None

    def __getitem__(self, k):
        return V(self.t[k], self.b, self)

    def sem(self):
        if self.dsem is None:
            self.dsem = self.P.new_sem("d_" + self.name)
        return self.dsem


class Dr:
    def __init__(self, ap, name):
        self.ap = ap
        self.name = name
        self.bufs = {}

    def at(self, key, ap=None):
        if key not in self.bufs:
            self.bufs[key] = Buf(self.name + str(key))
        return V(self.ap if ap is None else ap, self.bufs[key], None)


class Planner:
    ENGS = ("pe", "act", "dve", "pool", "sp")

    def __init__(self, nc):
        self.nc = nc
        self.streams = {e: [] for e in self.ENGS}
        self.sems = {}
        self.cnt = {}
        self.waited = {e: {} for e in self.ENGS}
        self._ctx = []
        self.free_sems = []
        self.ninst = 0
        for e in ("pe", "act", "dve", "pool"):
            self.new_sem("E_" + e)

    def new_sem(self, name):
        if name in self.sems:
            name = name + "_%d" % len(self.sems)
        cm = self.nc.semaphore(name)
        h = cm.__enter__()
        self._ctx.append(cm)
        self.sems[name] = h
        self.cnt[name] = 0
        return name

    def close(self):
        for cm in reversed(self._ctx):
            cm.__exit__(None, None, None)

    def _waits(self, eng, deps):
        need = {}
        for (s, v) in deps:
            if need.get(s, 0) < v:
                need[s] = v
        out = []
        for s, v in need.items():
            if s == "E_pe" and eng == "pe":
                continue
            if self.waited[eng].get(s, 0) >= v:
                continue
            self.waited[eng][s] = v
            out.append((s, v))
        return out

    @staticmethod
    def _deps(reads, writes):
        deps = []
        for b in reads:
            deps += b.wtok
        for b in writes:
            deps += b.wtok
            deps += b.rtoks
        return deps

    @staticmethod
    def _commit(tok, reads, writes):
        for b in reads:
            b.rtoks.append(tok)
        for b in writes:
            b.wtok = [tok]
            b.rtoks = []

    def op(self, eng, fn, reads=(), writes=()):
        waits = self._waits(eng, self._deps(reads, writes))
        sname = "E_" + eng
        self.cnt[sname] += 1
        tok = (sname, self.cnt[sname])
        sems = self.sems
        self.ninst += 1 + len(waits)

        def thunk(h, waits=waits, fn=fn, sname=sname):
            for (s, v) in waits:
                h.wait_ge(sems[s], v)
            fn(h).then_inc(sems[sname], 1)
        self.streams[eng].append(thunk)
        self._commit(tok, reads, writes)
        return tok

    def dma(self, q, sem, pairs, reads=(), writes=()):
        waits = self._waits(q, self._deps(reads, writes))
        self.cnt[sem] += 16 * len(pairs)
        tok = (sem, self.cnt[sem])
        sems = self.sems
        self.ninst += len(pairs) + len(waits)

        def thunk(h, waits=waits, pairs=pairs, sem=sem):
            for (s, v) in waits:
                h.wait_ge(sems[s], v)
            for (o, i) in pairs:
                h.dma_start(out=o, in_=i).then_inc(sems[sem], 16)
        self.streams[q].append(thunk)
        self._commit(tok, reads, writes)
        return tok

    def barrier(self):
        snap = [(s, v) for s, v in self.cnt.items() if v > 0]
        sems = self.sems
        for eng in self.ENGS:
            waits = self._waits(eng, snap)
            self.ninst += len(waits)

            def thunk(h, waits=waits):
                for (s, v) in waits:
                    h.wait_ge(sems[s], v)
            self.streams[eng].append(thunk)

    def emit(self):
        nc = self.nc
        st = self.streams
        with nc.allow_non_contiguous_dma(reason="small vectors"), nc.Block() as block:
            @block.tensor
            def _(h):
                for t in st["pe"]:
                    t(h)

            @block.scalar
            def _(h):
                for t in st["act"]:
                    t(h)

            @block.vector
            def _(h):
                for t in st["dve"]:
                    t(h)

            @block.gpsimd
            def _(h):
                for t in st["pool"]:
                    t(h)

            @block.sync
            def _(h):
                for t in st["sp"]:
                    t(h)


def _bf(a):
    return np.ascontiguousarray(a.astype(np.float32)).astype(ml_dtypes.bfloat16)


def _ptile(a):
    n = a.shape[0] // 128
    return np.ascontiguousarray(a.reshape(n, 128, -1).transpose(1, 0, 2))


def _dft_consts(L):
    N = 2 * L
    f = np.arange(L, dtype=np.float64)[:, None] + 0.5
    n = np.arange(L, dtype=np.float64)[None, :]
    w = 2 * np.pi * f / N
    Cs = np.cos(w * (n + 0.5))
    Ss = np.sin(w * (n + 0.5))
    C0T = (np.cos(w * n) * (2.0 / N)).T
    S0Tn = (-np.sin(w * n) * (2.0 / N)).T
    return [_bf(_ptile(m)) for m in (Cs, Ss, C0T, S0Tn)]


def _filt_feats(L, pos):
    pos = np.asarray(pos)
    t_all = np.linspace(0.0, 1.0, L, dtype=np.float32)
    t = t_all[pos][:, None]
    wpos = (np.float32(2.0 * math.pi / L) * np.arange(L, dtype=np.float32))[pos][:, None]
    bands = np.linspace(1e-4, 15.0, 16, dtype=np.float32)[None, :]
    z = np.concatenate([t, np.cos(bands * wpos), -np.sin(bands * wpos)], axis=-1).astype(np.float32)
    return np.ascontiguousarray(z.T), t_all[pos]


def _ctx_order(c):
    return list(range(0, c)) + list(range(3, c, -1))


_CONST_CACHE = {}


def _shared_consts():
    if _CONST_CACHE:
        return _CONST_CACHE
    s = np.arange(128)[:, None]
    c = np.arange(128)[None, :]
    cc = {}
    cc["ident"] = _bf(np.eye(128))
    mq_f = (s <= c).astype(np.float32) - (s <= 63)
    mst_f = (s > c).astype(np.float32)
    mq_b = (s >= c).astype(np.float32) - (s >= 64)
    mst_b = (s < c).astype(np.float32)
    cc["cmat"] = np.ascontiguousarray(np.stack([mq_f, mst_f, mq_b, mst_b], 1).astype(np.float32))
    cc["mst"] = (mst_f.astype(np.float32), mst_b.astype(np.float32))
    cols_f = np.stack([(np.arange(128) <= 63), np.ones(128)], 1)
    cols_b = np.stack([(np.arange(128) >= 64), np.ones(128)], 1)
    cc["cols2"] = np.ascontiguousarray(np.stack([cols_f, cols_b], 1).astype(np.float32))
    mk_f = np.tile((s <= c).astype(np.float32), (1, 4))
    mk_b = np.tile((s >= c).astype(np.float32), (1, 4))
    cc["masks"] = _bf(np.stack([mk_f, mk_b], 1))
    cs, ss, c0, s0 = _dft_consts(LS)
    cc["dftS"] = np.ascontiguousarray(np.stack([cs, ss, c0, s0], 1))
    cs, ss, c0, s0 = _dft_consts(LP)
    cc["dftP"] = np.ascontiguousarray(np.stack([cs, ss, c0, s0], 1))
    max_decay = math.log(1e-2) / 0.3
    min_decay = math.log(1e-2) / 1.5
    deltas = np.linspace(min_decay, max_decay, D, dtype=np.float32)
    cc["absd"] = np.abs(deltas).astype(np.float32)
    _CONST_CACHE.update(cc)
    return cc


def _core_inputs(r, I):
    cc = _shared_consts()
    b, c = r // 4, r % 4
    f32 = np.float32
    m = {}
    m["xp"] = np.ascontiguousarray(I["x_prompt"][4 * r:4 * r + 4].reshape(1024, D))
    order = _ctx_order(c)
    segs = [c] + order
    m["xs"] = np.ascontiguousarray(np.concatenate([I["x_sample"][b, 1024 * g:1024 * g + 1024] for g in segs], 0))
    cond2 = np.stack([I["c_ctx"], I["c"][b]], 0)
    m["condT"] = np.ascontiguousarray(cond2.reshape(2, 8, 128).transpose(2, 1, 0))
    m["s0"] = np.ascontiguousarray(I["state_hgrn"][b, 0].reshape(16, 128, 128).transpose(1, 0, 2))
    m["ada_w"] = I["ada_w"][0]
    m["ada_b2"] = np.ascontiguousarray(np.broadcast_to(I["ada_b"][0][None], (2, 6 * D)))
    w_in = I["w_in"][0]
    m["w_in"] = w_in
    m["w_fsel"] = np.ascontiguousarray(np.stack(
        [w_in[:, 1024:2048] if g < c else w_in[:, 2048:3072] for g in order], 0))
    lbl = I["hgrn_lb_logits"]
    rows = [lbl[:, 0], lbl[:, 1]] + [lbl[:, 0] if g < c else lbl[:, 1] for g in order]
    m["lbl5"] = np.ascontiguousarray(np.stack(rows, 0).transpose(1, 0, 2))
    m["normw"] = np.ascontiguousarray(np.tile(I["hgrn_norm_w"][0], 8))
    m["convw"] = np.ascontiguousarray(I["hy_conv_w"][0].reshape(3, 24, 128).transpose(2, 1, 0))
    m["convb"] = np.ascontiguousarray(I["hy_conv_b"][0].reshape(24, 128).T)
    m["fw1"] = I["filt_w1"][0]
    m["fw2"] = I["filt_w2"][0]
    m["fw3"] = I["filt_w3"][0]
    m["fb"] = np.ascontiguousarray(np.stack([I["filt_b1"][0], I["filt_b2"][0], I["filt_b3"][0]], 1))
    m["ffreq"] = np.ascontiguousarray(I["filt_freq"][0].T)
    w4 = I["filt_w4"][0]
    w4f, w4b = w4[:, :D], w4[:, D:]
    zt_list, t_list, w4_list = [], [], [w4f, w4b]
    zp, tp = _filt_feats(LP, np.arange(LP))
    zt_list.append(zp)
    t_list.append(tp)
    for g in segs:
        dl = c - g
        j = np.arange(LS)
        if dl >= 0:
            pf = LS * dl + j
            wf = w4f
        else:
            pf = LS * (-dl) - j
            wf = w4b
        if dl > 0:
            pb = LS * dl - j
            wb = w4f
        else:
            pb = LS * (-dl) + j
            wb = w4b
        pb = np.clip(pb, 0, 4095)
        for (pp, ww) in ((pf, wf), (pb, wb)):
            z, t = _filt_feats(4096, pp)
            zt_list.append(z)
            t_list.append(t)
            w4_list.append(ww)
    m["zt"] = np.ascontiguousarray(np.concatenate(zt_list, 1))
    tt = np.concatenate(t_list, 0)
    m["negt"] = np.ascontiguousarray((-tt).reshape(66, 128).T.astype(f32))
    m["w4all"] = np.ascontiguousarray(np.stack(w4_list, 1))
    m["skipT"] = np.ascontiguousarray(I["hy_skip"][0].reshape(8, 128).T)
    m["absd"] = cc["absd"]
    m["proj_a"] = I["proj_a"][0]
    m["proj_b"] = I["proj_b"][0]
    m["w_out"] = I["w_out"][0]
    m["lnrows"] = np.ascontiguousarray(np.stack([I["ln1_g"][0], I["ln1_b"][0], I["ln2_g"][0], I["ln2_b"][0]], 0))
    m["ffn_w_in"] = I["ffn_w_in"][0]
    m["ffn_w_out"] = I["ffn_w_out"][0]
    m["ident"] = cc["ident"]
    m["cmat"] = cc["cmat"]
    mst_f, mst_b = cc["mst"]
    m["mctx"] = np.ascontiguousarray(np.stack([mst_f if g < c else mst_b for g in order], 1))
    m["cols2"] = cc["cols2"]
    m["masks"] = cc["masks"]
    mfb = np.zeros((128, 3, 2), f32)
    for k, g in enumerate(order):
        mfb[:, k, 0] = 1.0 if g < c else 0.0
        mfb[:, k, 1] = 0.0 if g < c else 1.0
    m["mfb"] = mfb
    m["dftS"] = cc["dftS"]
    m["dftP"] = cc["dftP"]
    return m


IN_SPECS = [
    ("xp", [1024, D], F32), ("xs", [4096, D], F32), ("condT", [128, 8, 2], F32), ("s0", [128, 16, 128], F32),
    ("ada_w", [D, 6 * D], F32), ("ada_b2", [2, 6 * D], F32), ("w_in", [D, 10240], F32),
    ("w_fsel", [3, D, D], F32), ("lbl5", [2, 5, D], F32), ("normw", [D], F32),
    ("convw", [128, 24, 3], F32), ("convb", [128, 24], F32), ("fw1", [33, 64], F32), ("fw2", [64, 64], F32),
    ("fw3", [64, 64], F32), ("fb", [64, 3], F32), ("ffreq", [64, 3], F32), ("zt", [33, 8448], F32),
    ("negt", [128, 66], F32), ("w4all", [64, 10, D], F32), ("skipT", [128, 8], F32), ("absd", [D], F32),
    ("proj_a", [D, D], F32), ("proj_b", [D, D], F32), ("w_out", [D, D], F32), ("lnrows", [4, D], F32),
    ("ffn_w_in", [D, 5632], F32), ("ffn_w_out", [2816, D], F32), ("ident", [128, 128], BF16),
    ("cmat", [128, 4, 128], F32), ("mctx", [128, 3, 128], F32), ("cols2", [128, 2, 2], F32),
    ("masks", [128, 2, 512], BF16), ("mfb", [128, 3, 2], F32), ("dftS", [128, 4, 8, 1024], BF16),
    ("dftP", [128, 4, 2, 256], BF16),
]


def build_program(stop_after=None):
    nc = bass.Bass("TRN2", target_bir_lowering=False)
    P = Planner(nc)
    I = {}
    for (name, shape, dt) in IN_SPECS:
        I[name] = Dr(nc.dram_tensor(name, shape, dt, kind="ExternalInput").ap(), name)
    yp = Dr(nc.dram_tensor("yp", [1024, D], F32, kind="ExternalOutput").ap(), "yp")
    ys = Dr(nc.dram_tensor("ys", [1024, D], F32, kind="ExternalOutput").ap(), "ys")
    st_out = Dr(nc.dram_tensor("st", [4, 2, 8, 128, 128], F32, kind="ExternalOutput").ap(), "st")
    skind = "ExternalOutput" if DEBUG else "Internal"

    def scratch(name, shape, dt):
        return Dr(nc.dram_tensor(name, shape, dt, kind=skind).ap(), name)
    rows_d = scratch("rows_d", [2, 6, D], F32)
    lb_d = scratch("lb_d", [5, 2, D], F32)
    hT_d = scratch("hT_d", [40, 128, 8, 128], BF16)
    oaT_d = scratch("oaT_d", [128, 8, 2048], BF16)
    obT_d = scratch("obT_d", [128, 8, 2048], BF16)
    x1_d = scratch("x1_d", [16, 128, D], F32)
    h2T_d = scratch("h2T_d", [16, 128, 8, 128], BF16)
    qc_d = scratch("qc_d", [16, 2, 128, 512], F32)
    vc_d = scratch("vc_d", [16, 2, 128, 512], BF16)

    es_all = contextlib.ExitStack()

    uid = [0]

    def alloc(es, name, shape, dt, psum=False):
        uid[0] += 1
        name = "%s_%d" % (name, uid[0])
        cm = nc.psum_tensor("t_" + name, shape, dt) if psum else nc.sbuf_tensor("t_" + name, shape, dt)
        return Tl(P, es.enter_context(cm), name)

    def bufs(*vs):
        out = []
        for v in vs:
            if v is None or isinstance(v, (int, float)):
                continue
            if v.b not in out:
                out.append(v.b)
        return out

    def A(v):
        return v.ap if isinstance(v, V) else v

    def MM(out, lhsT, rhs, start=True, stop=True):
        P.op("pe", lambda h: h.matmul(out.ap, lhsT.ap, rhs.ap, start=start, stop=stop),
             reads=bufs(lhsT, rhs), writes=bufs(out))

    def TR(out, in_, ident):
        P.op("pe", lambda h: h.transpose(out.ap, in_.ap, ident.ap), reads=bufs(in_, ident), writes=bufs(out))

    def ACT(out, in_, func, bias=None, scale=None, eng="act"):
        kw = {}
        if bias is not None:
            kw["bias"] = A(bias)
        if scale is not None:
            kw["scale"] = A(scale)
        P.op(eng, lambda h: h.activation(out.ap, in_.ap, func, **kw), reads=bufs(in_, bias, scale), writes=bufs(out))

    def TT(eng, out, in0, in1, op):
        P.op(eng, lambda h: h.tensor_tensor(out.ap, in0.ap, in1.ap, op), reads=bufs(in0, in1), writes=bufs(out))

    def TS(eng, out, in0, s1, s2, op0, op1=None):
        if op1 is None:
            P.op(eng, lambda h: h.tensor_scalar(out.ap, in0.ap, A(s1), None, op0), reads=bufs(in0, s1), writes=bufs(out))
        else:
            P.op(eng, lambda h: h.tensor_scalar(out.ap, in0.ap, A(s1), A(s2), op0, op1),
                 reads=bufs(in0, s1, s2), writes=bufs(out))

    def STT(eng, out, in0, scalar, in1, op0, op1):
        P.op(eng, lambda h: h.scalar_tensor_tensor(out.ap, in0.ap, A(scalar), in1.ap, op0, op1),
             reads=bufs(in0, scalar, in1), writes=bufs(out))

    def CP(eng, out, in_):
        if eng == "act":
            P.op(eng, lambda h: h.copy(out.ap, in_.ap), reads=bufs(in_), writes=bufs(out))
        else:
            P.op(eng, lambda h: h.tensor_copy(out.ap, in_.ap), reads=bufs(in_), writes=bufs(out))

    def MEMSET(eng, out, val):
        P.op(eng, lambda h: h.memset(out.ap, val), reads=[], writes=bufs(out))

    def LOAD(q, out, in_):
        pairs = list(zip(out, in_)) if isinstance(out, list) else [(out, in_)]
        tl = pairs[0][0].o
        P.dma(q, tl.sem(), [(o.ap, i.ap) for (o, i) in pairs],
              reads=bufs(*[i for (_, i) in pairs]), writes=bufs(*[o for (o, _) in pairs]))

    def STORE(q, out, in_):
        tl = in_.o
        P.dma(q, tl.sem(), [(out.ap, in_.ap)], reads=bufs(in_), writes=bufs(out))

    def wview(dr, c0, c1, key=None):
        return dr.at(key if key is not None else "w", dr.ap[:, c0:c1].rearrange("(k p) n -> p k n", p=128))

    es0 = es_all
    ident = alloc(es0, "ident", [128, 128], BF16)
    ones1 = alloc(es0, "ones1", [128, 1], F32)
    LOAD("sp", ident[:], I["ident"].at("w"))
    MEMSET("dve", ones1[:], 1.0)

    def psum_std(es):
        pa_ = [alloc(es, "pa%d" % i, [128, 512], F32, psum=True) for i in range(4)]
        pw_ = alloc(es, "pw", [128, 1024], F32, psum=True)
        pt_ = [alloc(es, "pt%d" % i, [128, 1024], BF16, psum=True) for i in range(2)]
        return pa_, pw_, pt_

    def rowload(q, out, dr_v):
        LOAD(q, out, dr_v.pb(128))

    def hv(v):
        return v.r("p (h v) -> p h v", h=8)

    with contextlib.ExitStack() as es:
        pa, pw, pt = psum_std(es)
        scT = alloc(es, "scT", [128, 8, 2], F32)
        adab = alloc(es, "adab", [2, 6 * D], F32)
        mod = alloc(es, "mod", [2, 6 * D], F32)
        adaw = [alloc(es, "adaw%d" % i, [128, 8, 512], F32) for i in range(2)]
        lnr = alloc(es, "lnr", [2, 2, D], F32)
        lbt = alloc(es, "lbt", [5, 2, D], F32)
        lbo = alloc(es, "lbo", [5, 2, D], F32)
        LOAD("sp", scT[:], I["condT"].at("w"))
        LOAD("sp", adab[:], I["ada_b2"].at("w"))
        ACT(scT[:], scT[:], AF.Silu)
        adw = I["ada_w"]
        for blk in range(12):
            t = adaw[blk % 2]
            LOAD("sp" if blk % 2 == 0 else "act", t[:], wview(adw, blk * 512, (blk + 1) * 512))
            for k in range(8):
                MM(pa[0][0:2, :], scT[:, k, :], t[:, k, :], start=(k == 0), stop=(k == 7))
            TT("dve", mod[:, blk * 512:(blk + 1) * 512], pa[0][0:2, :], adab[:, blk * 512:(blk + 1) * 512], ALU.add)
        LOAD("sp", [lnr[:, 0, :], lnr[:, 1, :]],
             [I["lnrows"].at("w", I["lnrows"].ap[0]).pb(2), I["lnrows"].at("w", I["lnrows"].ap[1]).pb(2)])
        TS("dve", mod[:, 1 * D:2 * D], mod[:, 1 * D:2 * D], 1.0, None, ALU.add)
        TS("dve", mod[:, 4 * D:5 * D], mod[:, 4 * D:5 * D], 1.0, None, ALU.add)
        TT("dve", lnr[:, 1, :], lnr[:, 1, :], mod[:, 4 * D:5 * D], ALU.mult)
        TT("dve", lnr[:, 1, :], lnr[:, 1, :], mod[:, 3 * D:4 * D], ALU.add)
        TT("dve", lnr[:, 0, :], lnr[:, 0, :], mod[:, 4 * D:5 * D], ALU.mult)
        rv = rows_d.ap
        STORE("sp", rows_d.at(0, rv[:, 0, :]), mod[:, 0:D])
        STORE("sp", rows_d.at(1, rv[:, 1, :]), mod[:, D:2 * D])
        STORE("sp", rows_d.at(2, rv[:, 2, :]), mod[:, 2 * D:3 * D])
        STORE("sp", rows_d.at(3, rv[:, 3, :]), lnr[:, 0, :])
        STORE("sp", rows_d.at(4, rv[:, 4, :]), lnr[:, 1, :])
        STORE("sp", rows_d.at(5, rv[:, 5, :]), mod[:, 5 * D:6 * D])
        LOAD("sp", [lbt[:, 0, :], lbt[:, 1, :]], [I["lbl5"].at("w", I["lbl5"].ap[0]), I["lbl5"].at("w", I["lbl5"].ap[1])])
        TT("dve", lbt[:, 0, :], lbt[:, 0, :], lbt[:, 1, :], ALU.subtract)
        ACT(lbo[:, 0, :], lbt[:, 0, :], AF.Sigmoid)
        TS("dve", lbo[:, 1, :], lbo[:, 0, :], -0.5, 0.5, ALU.mult, ALU.add)
        TT("dve", lbo[:, 0, :], lbo[:, 0, :], lbo[:, 1, :], ALU.add)
        STORE("sp", lb_d.at("w"), lbo[:])
        P.barrier()

        mrow = alloc(es, "mrow", [128, 2, 2, D], F32)
        for cnd in range(2):
            rowload("sp", mrow[:, cnd, 0, :], rows_d.at(0, rv[cnd, 0, :]))
            rowload("sp", mrow[:, cnd, 1, :], rows_d.at(1, rv[cnd, 1, :]))
        xt = [alloc(es, "xt%d" % i, [128, D], F32) for i in range(3)]
        hb = [alloc(es, "hb%d" % i, [128, D], BF16) for i in range(2)]
        hTt = [alloc(es, "hTt%d" % i, [128, 8, 128], BF16) for i in range(2)]
        def xload(ti):
            src = I["xp"].ap[ti * 128:(ti + 1) * 128, :] if ti < 8 else I["xs"].ap[(ti - 8) * 128:(ti - 7) * 128, :]
            LOAD("sp", xt[ti % 3][:], (I["xp"] if ti < 8 else I["xs"]).at("w", src))
        xload(0)
        xload(1)
        for ti in range(40):
            cnd = 0 if ti < 8 else 1
            x_ = xt[ti % 3]
            h_ = hb[ti % 2]
            o_ = hTt[ti % 2]
            if ti + 2 < 40:
                xload(ti + 2)
            TT("dve", x_[:], x_[:], mrow[:, cnd, 1, :], ALU.mult)
            TT("pool", h_[:], x_[:], mrow[:, cnd, 0, :], ALU.add)
            for k in range(8):
                TR(pt[ti % 2][:, k * 128:(k + 1) * 128], h_[:, k * 128:(k + 1) * 128], ident[:])
            CP("act", o_[:].r("p k t -> p (k t)"), pt[ti % 2][:])
            STORE("sp", hT_d.at(ti, hT_d.ap[ti]), o_[:])
        P.barrier()
    if stop_after == 0:
        return finish(nc, P, es_all, [yp, ys, st_out])

    es_ab = contextlib.ExitStack()
    cmat = alloc(es_ab, "cmat", [128, 4, 128], F32)
    cols2 = alloc(es_ab, "cols2", [128, 2, 2], F32)
    masks = alloc(es_ab, "masks", [128, 2, 512], BF16)
    Sst = [[alloc(es_ab, "S%d_%d" % (d, hh), [128, 4, 128], F32) for hh in range(2)] for d in range(2)]
    PW = [alloc(es_ab, "PW%d" % i, [128, 512], F32, psum=True) for i in range(2)]
    PC = [alloc(es_ab, "PC%d" % i, [128, 512], F32, psum=True) for i in range(2)]
    PS = [alloc(es_ab, "PS%d" % i, [128, 512], F32, psum=True) for i in range(2)]
    PT = [alloc(es_ab, "PT%d" % i, [128, 1024], BF16, psum=True) for i in range(2)]
    LOAD("sp", cmat[:], I["cmat"].at("w"))
    LOAD("sp", cols2[:], I["cols2"].at("w"))
    LOAD("sp", masks[:], I["masks"].at("w"))
    for d in range(2):
        for hh in range(2):
            LOAD("sp", Sst[d][hh][:], I["s0"].at("w", I["s0"].ap[:, 8 * d + 4 * hh:8 * d + 4 * hh + 4, :]))

    def hv4(v):
        return v.r("p (h v) -> p h v", h=4)

    def run_rr(gens):
        alive = [True] * len(gens)
        while any(alive):
            for i, g in enumerate(gens):
                if alive[i]:
                    try:
                        next(g)
                    except StopIteration:
                        alive[i] = False

    with contextlib.ExitStack() as es:
        Wi = alloc(es, "Wi", [128, 8, D], BF16)
        Wf = [alloc(es, "Wf%d" % i, [128, 8, D], BF16) for i in range(2)]
        lbr = alloc(es, "lbr", [128, 2, D], F32)
        mctx = alloc(es, "mctx", [128, 3, 128], F32)
        mfb = alloc(es, "mfb", [128, 3, 2], F32)
        hTa = [alloc(es, "hTa%d" % i, [128, 8, 128], BF16) for i in range(4)]
        PTf = [V(PT[k].t[:].bitcast(F32), PT[k].b, PT[k]) for k in range(2)]
        banksA = {(0, 0): (PW[0][:], PC[0][:]), (0, 1): (PW[1][:], PC[1][:]),
                  (1, 0): (PS[0][:], PTf[0]), (1, 1): (PS[1][:], PTf[1])}
        scA = {}
        for tp in range(2):
            for hh in range(2):
                sfx = "A%d%d" % (tp, hh)
                scA[(tp, hh)] = dict(
                    sg=alloc(es, "sg" + sfx, [128, 512], F32), lf=alloc(es, "lf" + sfx, [128, 512], F32),
                    kk=alloc(es, "kk" + sfx, [128, 512], F32), kst=alloc(es, "kst" + sfx, [128, 512], BF16),
                    vb=alloc(es, "vb" + sfx, [128, 512], BF16), Dt=alloc(es, "Dt" + sfx, [128, 4], F32),
                    al=alloc(es, "al" + sfx, [128, 4], F32), be=alloc(es, "be" + sfx, [128, 4], F32),
                    tmpU=alloc(es, "tmpU" + sfx, [128, 4, 128], F32))
        aggA = [dict(Dagg=alloc(es, "DaggA%d" % hh, [128, 4], F32), Uagg=alloc(es, "UaggA%d" % hh, [128, 4, 128], F32),
                     al=alloc(es, "alS%d" % hh, [128, 4], F32)) for hh in range(2)]
        LOAD("pool", Wi[:], wview(I["w_in"], 3072, 4096))
        LOAD("sp", mctx[:], I["mctx"].at("w"))
        LOAD("sp", mfb[:], I["mfb"].at("w"))

        def ctx_gen(sl, tp, hh, hT, W_):
            c = scA[(tp, hh)]
            g = aggA[hh]
            X, Y = banksA[(tp, hh)]
            cs = slice(hh * 512, (hh + 1) * 512)
            sg, lf, kk, kst, vb = c["sg"], c["lf"], c["kk"], c["kst"], c["vb"]
            Dt, al, be, tmpU = c["Dt"], c["al"], c["be"], c["tmpU"]
            Dagg, Uagg = g["Dagg"], g["Uagg"]
            for k in range(8):
                MM(X, hT[:, k, :], W_[:, k, cs], start=(k == 0), stop=(k == 7))
            ACT(sg[:], X, AF.Tanh, scale=0.5)
            yield
            TT("dve", sg[:], sg[:], lbr[:, 1, cs], ALU.mult)
            TT("dve", sg[:], sg[:], lbr[:, 0, cs], ALU.add)
            yield
            ACT(lf[:], sg[:], AF.Ln)
            TS("pool", kk[:], sg[:], -1.0, 1.0, ALU.mult, ALU.add)
            yield
            MM(Y, mctx[:, sl, :], lf[:])
            ACT(sg[:], Y, AF.Exp)
            TT("dve", kst[:], kk[:], sg[:], ALU.mult)
            yield
            for h in range(4):
                MM(X[:, h:h + 1], lf[:, h * 128:(h + 1) * 128], ones1[:])
            ACT(Dt[:], X[:, 0:4], AF.Exp)
            yield
            for k in range(8):
                MM(Y, hT[:, k, :], Wi[:, k, cs], start=(k == 0), stop=(k == 7))
            CP("act", vb[:], Y)
            yield
            for h in range(4):
                MM(X[:, h * 128:(h + 1) * 128], kst[:, h * 128:(h + 1) * 128], vb[:, h * 128:(h + 1) * 128])
            yield
            TS("dve", al[:], Dt[:], -1.0, mfb[:, sl, 0:1], ALU.add, ALU.mult)
            TS("dve", al[:], al[:], 1.0, None, ALU.add)
            TS("dve", be[:], Dagg[:], -1.0, mfb[:, sl, 1:2], ALU.add, ALU.mult)
            TS("dve", be[:], be[:], 1.0, None, ALU.add)
            TT("dve", Dagg[:], Dagg[:], Dt[:], ALU.mult)
            TT("dve", tmpU[:], hv4(X), be[:].bc(2, [128, 4, 128]), ALU.mult)
            TT("pool", Uagg[:], Uagg[:], al[:].bc(2, [128, 4, 128]), ALU.mult)
            TT("pool", Uagg[:], Uagg[:], tmpU[:], ALU.add)
            yield

        def hload(ti):
            LOAD("sp", hTa[ti % 4][:], hT_d.at(ti, hT_d.ap[ti]))
        hload(16)
        hload(17)
        for sl in range(3):
            W_ = Wf[sl % 2]
            LOAD("pool", W_[:], I["w_fsel"].at("w", I["w_fsel"].ap[sl].rearrange("(k p) n -> p k n", p=128)))
            rowload("sp", lbr[:, 0, :], lb_d.at("w", lb_d.ap[2 + sl, 0, :]))
            rowload("sp", lbr[:, 1, :], lb_d.at("w", lb_d.ap[2 + sl, 1, :]))
            for hh in range(2):
                MEMSET("dve", aggA[hh]["Uagg"][:], 0.0)
                MEMSET("dve", aggA[hh]["Dagg"][:], 1.0)
            for t in range(0, 8, 2):
                ti = 16 + sl * 8 + t
                for nx in (ti + 2, ti + 3):
                    if nx < 40:
                        hload(nx)
                gens = []
                for tp in range(2):
                    for hh in range(2):
                        gens.append(ctx_gen(sl, tp, hh, hTa[(ti + tp) % 4], W_))
                run_rr(gens)
            for hh in range(2):
                g = aggA[hh]
                for d in range(2):
                    TS("dve", g["al"][:], g["Dagg"][:], -1.0, mfb[:, sl, d:d + 1], ALU.add, ALU.mult)
                    TS("dve", g["al"][:], g["al"][:], 1.0, None, ALU.add)
                    TT("dve", Sst[d][hh][:], Sst[d][hh][:], g["al"][:].bc(2, [128, 4, 128]), ALU.mult)
                    STT("dve", Sst[d][hh][:], g["Uagg"][:], mfb[:, sl, d:d + 1], Sst[d][hh][:], ALU.mult, ALU.add)
        P.barrier()
    if stop_after == 1:
        return finish(nc, P, es_all, [yp, ys, st_out])

    with contextlib.ExitStack() as es:
        Wg = {}
        for gi, (nm, c0) in enumerate((("q", 0), ("ff", 1024), ("i", 3072), ("fb", 2048), ("g", 4096))):
            Wg[nm] = alloc(es, "W" + nm, [128, 8, D], BF16)
            LOAD("pool", Wg[nm][:], wview(I["w_in"], c0, c0 + 1024))
        lbr = alloc(es, "lbrB", [128, 2, 2, D], F32)
        nrow = alloc(es, "nrow", [128, D], F32)
        for d in range(2):
            rowload("sp", lbr[:, d, 0, :], lb_d.at("w", lb_d.ap[d, 0, :]))
            rowload("sp", lbr[:, d, 1, :], lb_d.at("w", lb_d.ap[d, 1, :]))
        rowload("sp", nrow[:], I["normw"].at("w"))
        hTa = [alloc(es, "hTb%d" % i, [128, 8, 128], BF16) for i in range(2)]
        scB = []
        for hh in range(2):
            c = {}
            for nm in ("qs", "sg", "lf", "kk", "e1", "osum", "gs"):
                c[nm] = alloc(es, nm + "B%d" % hh, [128, 512], F32)
            for nm in ("qin", "kin", "kst", "vb", "scm", "oab"):
                c[nm] = alloc(es, nm + "B%d" % hh, [128, 512], BF16)
            for nm in ("qinT", "kinT", "Sq"):
                c[nm] = alloc(es, nm + "B%d" % hh, [128, 4, 128], BF16)
            c["eb"] = alloc(es, "ebB%d" % hh, [128, 4, 2], F32)
            c["ssq"] = alloc(es, "ssqB%d" % hh, [128, 4], F32)
            c["ofb"] = alloc(es, "ofbB%d" % hh, [128, 8, 512], BF16)
            c["Sp"] = [alloc(es, "SpB%d_%d" % (hh, d), [128, 4, 128], F32) for d in range(2)]
            c["oaT"] = [alloc(es, "oaTB%d_%d" % (hh, i), [128, 4, 128], BF16) for i in range(2)]
            c["no"] = 0
            scB.append(c)

        def pass_gen(d, S, slot, final, tokcol, hh, hT, tix):
            c = scB[hh]
            cs = slice(hh * 512, (hh + 1) * 512)
            PWh, PCh, PSh, PTh = PW[hh], PC[hh], PS[hh], PT[hh]
            qs, sg, lf, kk, e1, osum, gs = c["qs"], c["sg"], c["lf"], c["kk"], c["e1"], c["osum"], c["gs"]
            qin, kin, kst, vb, scm, oab = c["qin"], c["kin"], c["kst"], c["vb"], c["scm"], c["oab"]
            qinT, kinT, Sq, eb, ssq, ofb = c["qinT"], c["kinT"], c["Sq"], c["eb"], c["ssq"], c["ofb"]

            def proj(W, dst):
                for k in range(8):
                    MM(dst[:], hT[:, k, :], W[:, k, cs], start=(k == 0), stop=(k == 7))
            if d == 0:
                proj(Wg["q"], PWh)
                ACT(qs[:], PWh[:], AF.Tanh, scale=0.5)
            else:
                LOAD("sp", qs[:], qc_d.at((tix, hh), qc_d.ap[tix, hh]))
                LOAD("sp", vb[:], vc_d.at((tix, hh), vc_d.ap[tix, hh]))
                if final:
                    proj(Wg["g"], PWh)
                    ACT(gs[:], PWh[:], AF.Tanh, scale=0.5)
            proj(Wg["ff" if d == 0 else "fb"], PCh)
            ACT(sg[:], PCh[:], AF.Tanh, scale=0.5)
            if d == 1 and final:
                STT("dve", gs[:], gs[:], 1.0, PWh[:], ALU.add, ALU.mult)
            if d == 0:
                STT("dve", qs[:], qs[:], 1.0, PWh[:], ALU.add, ALU.mult)
                STORE("sp", qc_d.at((tix, hh), qc_d.ap[tix, hh]), qs[:])
            yield
            TT("dve", sg[:], sg[:], lbr[:, d, 1, cs], ALU.mult)
            TT("dve", sg[:], sg[:], lbr[:, d, 0, cs], ALU.add)
            if d == 0:
                proj(Wg["i"], PWh)
                CP("act", vb[:], PWh[:])
                STORE("sp", vc_d.at((tix, hh), vc_d.ap[tix, hh]), vb[:])
            yield
            ACT(lf[:], sg[:], AF.Ln)
            TS("pool", kk[:], sg[:], -1.0, 1.0, ALU.mult, ALU.add)
            yield
            MM(PCh[:], cmat[:, 2 * d, :], lf[:])
            ACT(e1[:], PCh[:], AF.Exp)
            STT("dve", qin[:], qs[:], 0.5, e1[:], ALU.mult, ALU.mult)
            yield
            ACT(e1[:], PCh[:], AF.Exp, scale=-1.0)
            TT("pool", kin[:], kk[:], e1[:], ALU.mult)
            MM(PWh[:], cmat[:, 2 * d + 1, :], lf[:])
            yield
            ACT(e1[:], PWh[:], AF.Exp)
            TT("dve", kst[:], kk[:], e1[:], ALU.mult)
            for h in range(4):
                MM(PSh[:, 2 * h:2 * h + 2], lf[:, h * 128:(h + 1) * 128], cols2[:, d, :])
            ACT(eb[:].r("p h t -> p (h t)"), PSh[:, 0:8], AF.Exp)
            yield
            for h in range(4):
                TR(PTh[:, h * 128:(h + 1) * 128], qin[:, h * 128:(h + 1) * 128], ident[:])
            CP("act", qinT[:].r("p h t -> p (h t)"), PTh[:, 0:512])
            for h in range(4):
                TR(PTh[:, 512 + h * 128:512 + (h + 1) * 128], kin[:, h * 128:(h + 1) * 128], ident[:])
            CP("dve", kinT[:].r("p h t -> p (h t)"), PTh[:, 512:1024])
            TT("pool", Sq[:], S[:], eb[:, :, 0].bc(2, [128, 4, 128]), ALU.mult)
            yield
            for h in range(4):
                MM(PSh[:, h * 128:(h + 1) * 128], kinT[:, h, :], qinT[:, h, :])
            TT("dve", scm[:], PSh[:], masks[:, d, :], ALU.mult)
            yield
            for h in range(4):
                MM(PCh[:, h * 128:(h + 1) * 128], scm[:, h * 128:(h + 1) * 128], vb[:, h * 128:(h + 1) * 128],
                   start=True, stop=False)
                MM(PCh[:, h * 128:(h + 1) * 128], qinT[:, h, :], Sq[:, h, :], start=False, stop=True)
            if not final:
                CP("act", ofb[:, slot, :], PCh[:])
            else:
                TT("dve", osum[:], PCh[:], ofb[:, slot, :], ALU.add)
            for h in range(4):
                MM(PSh[:, h * 128:(h + 1) * 128], kst[:, h * 128:(h + 1) * 128], vb[:, h * 128:(h + 1) * 128])
            yield
            TT("pool", S[:], S[:], eb[:, :, 1].bc(2, [128, 4, 128]), ALU.mult)
            TT("dve", S[:], S[:], hv4(PSh[:]), ALU.add)
            yield
            if final:
                TT("pool", e1[:], osum[:], osum[:], ALU.mult)
                P.op("dve", lambda h_: h_.tensor_reduce(ssq[:].ap, hv4(e1[:]).ap, AX.X, ALU.add),
                     reads=[e1.b], writes=[ssq.b])
                TS("dve", ssq[:], ssq[:], 1.0 / 128.0, 1e-6, ALU.mult, ALU.add)
                ACT(ssq[:], ssq[:], AF.Ln)
                ACT(ssq[:], ssq[:], AF.Exp, scale=-0.5)
                yield
                TT("dve", hv4(osum[:]), hv4(osum[:]), ssq[:].bc(2, [128, 4, 128]), ALU.mult)
                TT("pool", osum[:], osum[:], nrow[:, cs], ALU.mult)
                STT("dve", oab[:], osum[:], 0.5, gs[:], ALU.mult, ALU.mult)
                yield
                for h in range(4):
                    TR(PTh[:, h * 128:(h + 1) * 128], oab[:, h * 128:(h + 1) * 128], ident[:])
                o_ = c["oaT"][c["no"] % 2]
                c["no"] += 1
                CP("act", o_[:].r("p h t -> p (h t)"), PTh[:, 0:512])
                STORE("sp", oaT_d.at(("t", tokcol, hh), oaT_d.ap[:, 4 * hh:4 * hh + 4, tokcol:tokcol + 128]), o_[:])
                yield

        sched = []
        for j in range(4):
            sched += [(2 * j, 0, j, 0, False), (2 * j + 1, 0, j, 1, False), (2 * j + 1, 1, j, 1, True), (2 * j, 1, j, 0, True)]
        sched += [(8 + t, 0, 4, t, False) for t in range(8)] + [(8 + t, 1, 4, t, True) for t in range(7, -1, -1)]
        LOAD("sp", hTa[0][:], hT_d.at(sched[0][0], hT_d.ap[sched[0][0]]))
        for idx, (ti, d, unit, slot, final) in enumerate(sched):
            hT = hTa[idx % 2]
            if idx + 1 < len(sched):
                nti = sched[idx + 1][0]
                LOAD("sp", hTa[(idx + 1) % 2][:], hT_d.at(nti, hT_d.ap[nti]))
            tokcol = ti * 128 if unit < 4 else 1024 + (ti - 8) * 128
            if unit < 4 and d == 0 and slot == 0:
                for hh in range(2):
                    MEMSET("dve", scB[hh]["Sp"][0][:], 0.0)
                    MEMSET("pool", scB[hh]["Sp"][1][:], 0.0)
            gens = []
            for hh in range(2):
                S = scB[hh]["Sp"][d] if unit < 4 else Sst[d][hh]
                gens.append(pass_gen(d, S, slot, final, tokcol, hh, hT, ti))
            run_rr(gens)
            if unit < 4 and ((d == 0 and slot == 1) or (d == 1 and slot == 0)):
                for hh in range(2):
                    STORE("sp", st_out.at((d, unit, hh), st_out.ap[unit, d, 4 * hh:4 * hh + 4].rearrange("h a v -> a h v")),
                          scB[hh]["Sp"][d][:])
        P.barrier()
    es_ab.close()
    if stop_after == 2:
        return finish(nc, P, es_all, [yp, ys, st_out])

    with contextlib.ExitStack() as es:
        pa = [alloc(es, "q%d" % i, [128, 512], F32, psum=True) for i in range(7)]
        pt = [alloc(es, "ptc", [128, 1024], BF16, psum=True)]
        rot = {"i": 0}

        def nb():
            rot["i"] += 1
            return pa[4 + rot["i"] % 3]
        dS = alloc(es, "dS", [128, 4, 8, 1024], BF16)
        dP = alloc(es, "dP", [128, 4, 2, 256], BF16)
        LOAD("act", [dS[:, i] for i in range(4)], [I["dftS"].at("w", I["dftS"].ap[:, i]) for i in range(4)])
        LOAD("act", dP[:], I["dftP"].at("w"))
        absr = alloc(es, "absr", [128, D], F32)
        rowload("sp", absr[:], I["absd"].at("w"))
        negt = alloc(es, "negt", [128, 66], F32)
        convw = alloc(es, "convw", [128, 24, 3], F32)
        convb = alloc(es, "convb", [128, 24], F32)
        skipT = alloc(es, "skipT", [128, 8], F32)
        LOAD("sp", negt[:], I["negt"].at("w"))
        LOAD("sp", convw[:], I["convw"].at("w"))
        LOAD("sp", convb[:], I["convb"].at("w"))
        LOAD("sp", skipT[:], I["skipT"].at("w"))
        h3T = alloc(es, "h3T", [64, 8448], BF16)
        fw1 = alloc(es, "fw1", [33, 64], F32)
        fw2 = alloc(es, "fw2", [64, 64], F32)
        fw3 = alloc(es, "fw3", [64, 64], F32)
        fbt = alloc(es, "fbt", [64, 3], F32)
        frq = alloc(es, "frq", [64, 3], F32)

        Wx = [alloc(es, "Wx%d" % i, [128, 8, 3, 128], BF16) for i in range(2)]
        w4c = [alloc(es, "w4c%d" % i, [64, 10, 128], BF16) for i in range(2)]
        hTblk = [alloc(es, "hTblk%d" % i, [128, 8, 512], BF16) for i in range(2)]
        cy = [alloc(es, "cy%d" % i, [128, 512], F32) for i in range(2)]
        ub2 = [alloc(es, "ub%d" % i, [128, 512], BF16) for i in range(2)]
        uTk = [alloc(es, "uTk%d" % i, [128, 2048], BF16) for i in range(2)]
        x0k = [alloc(es, "x0k%d" % i, [128, 2048], BF16) for i in range(2)]
        utm = [alloc(es, "utm%d" % i, [128, 40, 128], BF16) for i in range(2)]
        ew = alloc(es, "ew", [128, 512], F32)
        hwf = alloc(es, "hwf", [128, 512], F32)
        hwb = alloc(es, "hwb", [128, 512], F32)
        hpm2 = [alloc(es, "hpm%d" % i, [128, 2, 8, 128], BF16) for i in range(2)]
        Ksb = alloc(es, "Ksb", [128, 2, 1024], BF16)
        CSsb2 = [alloc(es, "CSsb%d" % i, [128, 2, 1024], BF16) for i in range(2)]
        identf = alloc(es, "identf", [128, 128], F32)
        CP("dve", identf[:], ident[:])
        tA = alloc(es, "tA", [128, 1024], F32)
        PQ = alloc(es, "PQ", [128, 2, 1024], F32)
        PQT = alloc(es, "PQT", [128, 2, 8, 128], BF16)

        def taps(ct, s, njt, tile_f, tile_b, wf, wb, same_pos, hpm):
            for g0 in range(0, njt, 4):
                ng = min(4, njt - g0)
                for side, (tb, wi, dst) in enumerate(((tile_f, wf, hwf), (tile_b, wb, hwb))):
                    for i in range(ng):
                        tcol = (tb + g0 + i) * 128
                        MM(pa[2 + side][:, i * 128:(i + 1) * 128], h3T[:, tcol:tcol + 128], w4c[s][:, wi, :])
                    if side == 0 or not same_pos:
                        for i in range(ng):
                            ACT(ew[:, i * 128:(i + 1) * 128], absr[:, ct * 128:(ct + 1) * 128], AF.Exp,
                                scale=negt[:, tb + g0 + i:tb + g0 + i + 1])
                    STT("dve", dst[:, 0:ng * 128], ew[:, 0:ng * 128], 0.05, pa[2 + side][:, 0:ng * 128], ALU.add, ALU.mult)
                    if side == 1 and g0 == 0:
                        MEMSET("dve", dst[0:1, 0:128], 0.0)
                TT("pool", hpm[:, 0, g0:g0 + ng, :],
                   hwf[:, 0:ng * 128].r("p (j c) -> p j c", j=ng), hwb[:, 0:ng * 128].r("p (j c) -> p j c", j=ng), ALU.add)
                TT("pool", hpm[:, 1, g0:g0 + ng, :],
                   hwf[:, 0:ng * 128].r("p (j c) -> p j c", j=ng), hwb[:, 0:ng * 128].r("p (j c) -> p j c", j=ng),
                   ALU.subtract)
                yield

        def pq_update(first, CSsb):
            Cu, Su = CSsb[:, 0, :], CSsb[:, 1, :]
            Kr, Ki = Ksb[:, 0, :], Ksb[:, 1, :]
            Pv, Qv = PQ[:, 0, :], PQ[:, 1, :]
            if first:
                TT("dve", Pv, Cu, Kr, ALU.mult)
                TT("pool", Qv, Su, Kr, ALU.mult)
            else:
                TT("dve", tA[:], Cu, Kr, ALU.mult)
                TT("dve", Pv, Pv, tA[:], ALU.add)
                TT("dve", tA[:], Su, Kr, ALU.mult)
                TT("dve", Qv, Qv, tA[:], ALU.add)
            TT("dve", tA[:], Su, Ki, ALU.mult)
            TT("dve", Pv, Pv, tA[:], ALU.add)
            TT("dve", tA[:], Cu, Ki, ALU.mult)
            TT("dve", Qv, Qv, tA[:], ALU.subtract)

        def pq_transpose():
            for w in range(2):
                for g in range(2):
                    bk = nb()
                    for i in range(4):
                        TR(bk[:, i * 128:(i + 1) * 128], PQ[:, w, (4 * g + i) * 128:(4 * g + i + 1) * 128], identf[:])
                    CP("act", PQT[:, w, 4 * g:4 * g + 4, :].r("p i c -> p (i c)"), bk[:])

        def epilogue_half(ct, s, tok0, half, bk):
            hs = slice(half * 512, (half + 1) * 512)
            ts_ = slice(tok0 + half * 512, tok0 + (half + 1) * 512)
            STT("dve", tA[:, hs], uTk[s][:, ts_], skipT[:, ct:ct + 1], bk[:], ALU.mult, ALU.add)
            TT("pool", Ksb[:, 0, hs], tA[:, hs], x0k[s][:, ts_], ALU.mult)
            if half == 1:
                STORE("sp", obT_d.at(("c", ct, tok0), obT_d.ap[:, ct, tok0:tok0 + 1024]), Ksb[:, 0, :])

        def blk_load(bi):
            LOAD("sp", [hTblk[bi % 2][:, :, i * 128:(i + 1) * 128] for i in range(4)],
                 [hT_d.at(4 * bi + i, hT_d.ap[4 * bi + i]) for i in range(4)])

        def part1(ct, s):
            LOAD("pool", [Wx[s][:, :, g, :] for g in range(3)],
                 [wview(I["w_in"], 5120 + g * 1024 + ct * 128, 5120 + g * 1024 + (ct + 1) * 128) for g in range(3)])
            LOAD("pool", w4c[s][:], I["w4all"].at("w", I["w4all"].ap[:, :, ct * 128:(ct + 1) * 128]))
            blk_load(0)

            def u_transpose(bi):
                ub = ub2[bi % 2]
                for i in range(4):
                    TR(pt[0][:, i * 128:(i + 1) * 128], ub[:, i * 128:(i + 1) * 128], ident[:])
                CP("act", utm[s][:, 4 * bi:4 * bi + 4, :].r("p i c -> p (i c)"), pt[0][:, 0:512])

            for bi in range(10):
                if bi + 1 < 10:
                    blk_load(bi + 1)
                hb_ = hTblk[bi % 2]
                ub = ub2[bi % 2]
                groups = (1, 2, 0) if bi < 4 else (1, 2)
                rl = 256 if bi < 2 else 64
                for gi, g in enumerate(groups):
                    ps = pa[gi % 2]
                    for k in range(8):
                        MM(ps[:], Wx[s][:, k, g, :], hb_[:, k, :], start=(k == 0), stop=(k == 7))
                    ci = g * 8 + ct
                    if gi == 2:
                        TT("pool", ub[:], cy[0][:], cy[1][:], ALU.mult)
                    y_ = cy[gi % 2]
                    ACT(y_[:], ps[:], AF.Identity, bias=convb[:, ci:ci + 1], scale=convw[:, ci, 1:2])
                    y3 = y_[:].r("p (r t) -> p r t", t=rl)
                    x3 = ps[:].r("p (r t) -> p r t", t=rl)
                    STT("dve", y3[:, :, 1:rl], x3[:, :, 0:rl - 1], convw[:, ci, 0:1], y3[:, :, 1:rl], ALU.mult, ALU.add)
                    STT("dve", y3[:, :, 0:rl - 1], x3[:, :, 1:rl], convw[:, ci, 2:3], y3[:, :, 0:rl - 1], ALU.mult, ALU.add)
                if len(groups) == 2:
                    TT("pool", ub[:], cy[0][:], cy[1][:], ALU.mult)
                if bi < 4:
                    CP("pool", uTk[s][:, bi * 512:(bi + 1) * 512], ub[:])
                    CP("act", x0k[s][:, bi * 512:(bi + 1) * 512], cy[0][:])
                if bi > 0:
                    u_transpose(bi - 1)
                yield
            u_transpose(9)
            yield

        def part2(ct, s):
            U = utm[s]

            def cusu_prompt(CSsb):
                for w in range(2):
                    for jp in range(2):
                        bk = nb()
                        for j in (2 * jp, 2 * jp + 1):
                            for nt in range(2):
                                MM(bk[:, (j % 2) * 256:(j % 2 + 1) * 256], U[:, 2 * j + nt, :], dP[:, w, nt, :],
                                   start=(nt == 0), stop=(nt == 1))
                        CP("act", CSsb[:, w, jp * 512:(jp + 1) * 512], bk[:])

            def cusu_sample(kb, CSsb):
                for w in range(2):
                    for half in range(2):
                        bk = nb()
                        for nt in range(8):
                            MM(bk[:], U[:, 8 + 8 * kb + nt, :],
                               dS[:, w, nt, half * 512:(half + 1) * 512], start=(nt == 0), stop=(nt == 7))
                        CP("act", CSsb[:, w, half * 512:(half + 1) * 512], bk[:])

            def k_sample(hpm):
                for w in range(2):
                    for half in range(2):
                        bk = nb()
                        for jt in range(8):
                            MM(bk[:], hpm[:, w, jt, :],
                               dS[:, 2 + w, jt, half * 512:(half + 1) * 512], start=(jt == 0), stop=(jt == 7))
                        CP("act", Ksb[:, w, half * 512:(half + 1) * 512], bk[:])

            def stap(kb):
                return taps(ct, s, 8, 2 + 16 * kb, 2 + 16 * kb + 8, 2 + 2 * kb, 3 + 2 * kb, False, hpm2[(kb + 1) % 2])

            yield from taps(ct, s, 2, 0, 0, 0, 1, True, hpm2[0])
            cusu_prompt(CSsb2[0])
            yield
            yield from stap(0)
            hp_ = hpm2[0]
            bk = nb()
            for jt in range(2):
                MM(bk[:, 0:256], hp_[:, 0, jt, :], dP[:, 2, jt, :], start=(jt == 0), stop=(jt == 1))
            for jt in range(2):
                MM(bk[:, 256:512], hp_[:, 1, jt, :], dP[:, 3, jt, :], start=(jt == 0), stop=(jt == 1))
            for j in range(4):
                CP("act", Ksb[:, 0, j * 256:(j + 1) * 256], bk[:, 0:256])
                CP("act", Ksb[:, 1, j * 256:(j + 1) * 256], bk[:, 256:512])
            yield
            cusu_sample(0, CSsb2[1])
            yield
            pq_update(True, CSsb2[0])
            yield
            pq_transpose()
            for jp in range(2):
                bk = nb()
                for j in (2 * jp, 2 * jp + 1):
                    n = 0
                    for w in range(2):
                        for ft in range(2):
                            MM(bk[:, (j % 2) * 256:(j % 2 + 1) * 256], PQT[:, w, 2 * j + ft, :], dP[:, w, ft, :],
                               start=(n == 0), stop=(n == 3))
                            n += 1
                epilogue_half(ct, s, 0, jp, bk)
            yield
            for kb in range(4):
                if kb + 1 < 4:
                    yield from stap(kb + 1)
                k_sample(hpm2[(kb + 1) % 2])
                yield
                if kb + 1 < 4:
                    cusu_sample(kb + 1, CSsb2[kb % 2])
                    yield
                pq_update(kb == 0, CSsb2[(kb + 1) % 2])
                yield
            pq_transpose()
            for half in range(2):
                bk = nb()
                n = 0
                for w in range(2):
                    for ft in range(8):
                        MM(bk[:], PQT[:, w, ft, :],
                           dS[:, w, ft, half * 512:(half + 1) * 512], start=(n == 0), stop=(n == 15))
                        n += 1
                epilogue_half(ct, s, 1024, half, bk)
            yield

        zb1 = Buf("z1buf")
        zsem = [PQ.sem(), P.new_sem("zsem1")]

        def mlp_setup():
            LOAD("sp", fw1[:], I["fw1"].at("w"))
            LOAD("sp", fw2[:], I["fw2"].at("w"))
            LOAD("sp", fw3[:], I["fw3"].at("w"))
            LOAD("sp", fbt[:], I["fb"].at("w"))
            LOAD("sp", frq[:], I["ffreq"].at("w"))
            TT("dve", fbt[:], fbt[:], frq[:], ALU.mult)

        def mlp_gen(cid):
            ws = [fw1, fw2, fw3]
            if cid == 0:
                aa, kf, ss = tA[0:64, 0:512], tA[0:64, 512:1024], PQ[0:64, 0, 0:512]
                zv = PQ[0:33, 0, 512:1024]
                banks = (pa[4], pa[5])
            else:
                aa, kf, ss = ew[0:64, :], hwf[0:64, :], hwb[0:64, :]
                zv = V(PQ.t[0:33, 1, 0:512], zb1, None)
                banks = (pa[6], pa[2])
            for ch in range(cid, 17, 2):
                c0 = ch * 512
                n = min(512, 8448 - c0)
                P.dma("sp", zsem[cid], [(zv[:, 0:n].ap, I["zt"].ap[:, c0:c0 + n])], reads=[], writes=[zv.b])
                src = zv
                for l in range(3):
                    pv = banks[l % 2][0:64, 0:n]
                    MM(pv, ws[l][:], src[:, 0:n])
                    TS("dve", aa[:, 0:n], pv, frq[:, l:l + 1], fbt[:, l:l + 1], ALU.mult, ALU.add)
                    TS("dve", kf[:, 0:n], aa[:, 0:n], 1.0 / TWO_PI, MAGIC, ALU.mult, ALU.add)
                    TS("dve", kf[:, 0:n], kf[:, 0:n], -MAGIC, None, ALU.add)
                    STT("dve", aa[:, 0:n], kf[:, 0:n], -TWO_PI, aa[:, 0:n], ALU.mult, ALU.add)
                    if l < 2:
                        ACT(ss[:, 0:n], aa[:, 0:n], AF.Sin)
                        src = ss
                    else:
                        ACT(h3T[:, c0:c0 + n], aa[:, 0:n], AF.Sin)
                    yield

        def chain(*gs):
            for g in gs:
                yield from g

        mlp_setup()
        run_rr([mlp_gen(0), mlp_gen(1), chain(part1(0, 0), part1(1, 1))])
        for ct in range(8):
            g2 = part2(ct, ct % 2)
            g1 = part1(ct + 1, (ct + 1) % 2) if 1 <= ct < 7 else iter(())
            alive = [True, True]
            while alive[0] or alive[1]:
                for _ in range(1):
                    if alive[1]:
                        try:
                            next(g2)
                        except StopIteration:
                            alive[1] = False
                if alive[0]:
                    try:
                        next(g1)
                    except StopIteration:
                        alive[0] = False
        P.barrier()
    if stop_after == 3:
        return finish(nc, P, es_all, [yp, ys, st_out])

    rv = rows_d.ap

    def layer_norm_tile(es_tiles, r, grow, brow, out_main, extra=None):
        stats, mv, xn = es_tiles
        for c2 in range(2):
            P.op("dve", lambda h_, c2=c2: h_.bn_stats(stats[:, c2, :].ap, r[:, c2 * 512:(c2 + 1) * 512].ap),
                 reads=[r.b], writes=[stats.b])
        P.op("dve", lambda h_: h_.bn_aggr(mv[:, 0:2].ap, stats[:].ap), reads=[stats.b], writes=[mv.b])
        TS("dve", mv[:, 2:3], mv[:, 1:2], 1e-5, None, ALU.add)
        ACT(mv[:, 2:3], mv[:, 2:3], AF.Sqrt)
        P.op("dve", lambda h_: h_.reciprocal(mv[:, 3:4].ap, mv[:, 2:3].ap), reads=[mv.b], writes=[mv.b])
        TS("dve", xn[:], r[:], mv[:, 0:1], mv[:, 3:4], ALU.subtract, ALU.mult)
        TT("pool", out_main, xn[:], grow, ALU.mult)
        TT("pool", out_main, out_main, brow, ALU.add)
        if extra is not None:
            G, B, o2, tmp = extra
            TT("dve", tmp, xn[:], G, ALU.mult)
            TT("dve", o2, tmp, B, ALU.add)

    with contextlib.ExitStack() as es:
        pa, pw, pt = psum_std(es)
        pA = alloc(es, "pA", [128, 8, D], BF16)
        pB = alloc(es, "pB", [128, 8, D], BF16)
        wO = alloc(es, "wO", [128, 8, D], BF16)
        Wmg = alloc(es, "Wmg", [128, 8, 2 * D], BF16)
        LOAD("pool", Wmg[:, :, 0:D], wview(I["w_in"], 8192, 8192 + D))
        LOAD("pool", pA[:], wview(I["proj_a"], 0, D))
        LOAD("pool", Wmg[:, :, D:2 * D], wview(I["w_in"], 8192 + D, 10240))
        LOAD("pool", pB[:], wview(I["proj_b"], 0, D))
        LOAD("pool", wO[:], wview(I["w_out"], 0, D))
        oaTh = alloc(es, "oaTh", [128, 8, 1024], BF16)
        obTh = alloc(es, "obTh", [128, 8, 1024], BF16)
        hTh = alloc(es, "hTh", [128, 8, 1024], BF16)
        mT = alloc(es, "mT", [128, 8, 1024], BF16)
        rws = alloc(es, "rws", [128, 5, D], F32)
        gas = alloc(es, "gas", [128, 512], F32)
        gbs = alloc(es, "gbs", [128, 512], F32)
        m1 = alloc(es, "m1", [128, 512], F32)
        m2 = alloc(es, "m2", [128, 512], F32)
        xt = [alloc(es, "xtD%d" % i, [128, D], F32) for i in range(2)]
        rr = alloc(es, "rr", [128, D], F32)
        xn = alloc(es, "xn", [128, D], F32)
        x1t = [alloc(es, "x1t%d" % i, [128, D], F32) for i in range(2)]
        h2b = alloc(es, "h2b", [128, D], BF16)
        h2Tt = [alloc(es, "h2Tt%d" % i, [128, 8, 128], BF16) for i in range(2)]
        stats = alloc(es, "stats", [128, 2, 6], F32)
        mv = alloc(es, "mv", [128, 4], F32)
        for hf in range(2):
            LOAD("sp", oaTh[:], oaT_d.at(("h", hf), oaT_d.ap[:, :, hf * 1024:(hf + 1) * 1024]))
            LOAD("sp", obTh[:], obT_d.at(("h", hf), obT_d.ap[:, :, hf * 1024:(hf + 1) * 1024]))
            LOAD("sp", [hTh[:, :, t * 128:(t + 1) * 128] for t in range(8)],
                 [hT_d.at(hf * 8 + t, hT_d.ap[hf * 8 + t]) for t in range(8)])
            rowload("sp", rws[:, 0, :], rows_d.at(2, rv[hf, 2, :]))
            rowload("sp", rws[:, 1, :], rows_d.at(3, rv[hf, 3, :]))
            rowload("sp", rws[:, 2, :], rows_d.at(4, rv[hf, 4, :]))
            rowload("sp", rws[:, 3, :], I["lnrows"].at("w", I["lnrows"].ap[0]))
            rowload("sp", rws[:, 4, :], I["lnrows"].at("w", I["lnrows"].ap[1]))
            for j in range(8):
                for tc in range(2):
                    ts_ = slice(tc * 512, (tc + 1) * 512)
                    cs_ = slice(j * 128, (j + 1) * 128)
                    cs2 = slice(D + j * 128, D + (j + 1) * 128)
                    for k in range(8):
                        MM(pa[2][:], Wmg[:, k, cs_], hTh[:, k, ts_], start=(k == 0), stop=(k == 7))
                    for k in range(8):
                        MM(pa[3][:], Wmg[:, k, cs2], hTh[:, k, ts_], start=(k == 0), stop=(k == 7))
                    for k in range(8):
                        MM(pa[0][:], pA[:, k, cs_], oaTh[:, k, ts_], start=(k == 0), stop=(k == 7))
                    for k in range(8):
                        MM(pa[1][:], pB[:, k, cs_], obTh[:, k, ts_], start=(k == 0), stop=(k == 7))
                    ACT(gas[:], pa[2][:], AF.Sigmoid)
                    ACT(gbs[:], pa[3][:], AF.Sigmoid)
                    TT("dve", m1[:], gas[:], pa[0][:], ALU.mult)
                    TT("dve", m2[:], gbs[:], pa[1][:], ALU.mult)
                    TT("pool", mT[:, j, ts_], m1[:], m2[:], ALU.add)
            for t in range(8):
                gi = hf * 8 + t
                for half2 in range(2):
                    for k in range(8):
                        MM(pw[:, half2 * 512:(half2 + 1) * 512], mT[:, k, t * 128:(t + 1) * 128],
                           wO[:, k, half2 * 512:(half2 + 1) * 512], start=(k == 0), stop=(k == 7))
                x_ = xt[t % 2]
                if t == 0:
                    LOAD("sp", xt[0][:], (I["xp"] if hf == 0 else I["xs"]).at("w", (I["xp"] if hf == 0 else I["xs"]).ap[0:128, :]))
                if t + 1 < 8:
                    LOAD("sp", xt[(t + 1) % 2][:], (I["xp"] if hf == 0 else I["xs"]).at(
                        "w", (I["xp"] if hf == 0 else I["xs"]).ap[(t + 1) * 128:(t + 2) * 128, :]))
                TT("dve", rr[:], pw[:], rws[:, 0, :], ALU.mult)
                STT("dve", rr[:], x_[:], ALPHA, rr[:], ALU.mult, ALU.add)
                x1_ = x1t[t % 2]
                layer_norm_tile((stats, mv, xn), rr, rws[:, 3, :], rws[:, 4, :], x1_[:],
                                extra=(rws[:, 1, :], rws[:, 2, :], h2b[:], rr[:]))
                STORE("sp", x1_d.at(gi, x1_d.ap[gi]), x1_[:])
                for k in range(8):
                    TR(pt[t % 2][:, k * 128:(k + 1) * 128], h2b[:, k * 128:(k + 1) * 128], ident[:])
                o_ = h2Tt[t % 2]
                CP("act", o_[:].r("p k t -> p (k t)"), pt[t % 2][:])
                STORE("sp", h2T_d.at(gi, h2T_d.ap[gi]), o_[:])
        P.barrier()
    if stop_after == 4:
        return finish(nc, P, es_all, [yp, ys, st_out])

    with contextlib.ExitStack() as es:
        pa, pw, pt = psum_std(es)
        wout = alloc(es, "wout", [128, 22, D], BF16)
        actT = alloc(es, "actT", [128, 22, 1024], BF16)
        h2Th = alloc(es, "h2Th", [128, 8, 1024], BF16)
        Wfi = [alloc(es, "Wfi%d" % i, [128, 8, 2, 256], BF16) for i in range(2)]
        rws = alloc(es, "rws2", [128, 3, D], F32)
        sgt = [alloc(es, "sgt%d" % i, [128, 512], F32) for i in range(2)]
        x1t = [alloc(es, "x1u%d" % i, [128, D], F32) for i in range(2)]
        rr = alloc(es, "rr2", [128, D], F32)
        xn = alloc(es, "xn2", [128, D], F32)
        yt = [alloc(es, "yt%d" % i, [128, D], F32) for i in range(2)]
        stats = alloc(es, "stats2", [128, 2, 6], F32)
        mv = alloc(es, "mv2", [128, 4], F32)
        for hf in range(2):
            LOAD("sp", [h2Th[:, :, t * 128:(t + 1) * 128] for t in range(8)],
                 [h2T_d.at(hf * 8 + t, h2T_d.ap[hf * 8 + t]) for t in range(8)])
            rowload("sp", rws[:, 0, :], rows_d.at(5, rv[hf, 5, :]))
            rowload("sp", rws[:, 1, :], I["lnrows"].at("w", I["lnrows"].ap[2]))
            rowload("sp", rws[:, 2, :], I["lnrows"].at("w", I["lnrows"].ap[3]))
            def wfi_load(fbk):
                W2 = Wfi[fbk % 2]
                LOAD("pool", [W2[:, :, 0, :], W2[:, :, 1, :]],
                     [wview(I["ffn_w_in"], fbk * 256, (fbk + 1) * 256),
                      wview(I["ffn_w_in"], 2816 + fbk * 256, 2816 + (fbk + 1) * 256)])
            wfi_load(0)
            for fbk in range(11):
                W_ = Wfi[fbk % 2]
                if fbk + 1 < 11:
                    wfi_load(fbk + 1)
                if hf == 0 and fbk == 1:
                    LOAD("pool", wout[:], wview(I["ffn_w_out"], 0, D))
                for sub in range(2):
                    j = fbk * 2 + sub
                    for tc in range(2):
                        ts_ = slice(tc * 512, (tc + 1) * 512)
                        pg, pu, sg_ = pa[2 * tc], pa[2 * tc + 1], sgt[tc]
                        for k in range(8):
                            MM(pg[:], W_[:, k, 0, sub * 128:(sub + 1) * 128], h2Th[:, k, ts_],
                               start=(k == 0), stop=(k == 7))
                        for k in range(8):
                            MM(pu[:], W_[:, k, 1, sub * 128:(sub + 1) * 128], h2Th[:, k, ts_],
                               start=(k == 0), stop=(k == 7))
                        ACT(sg_[:], pg[:], AF.Silu)
                        TT("dve", actT[:, j, ts_], sg_[:], pu[:], ALU.mult)
            for t in range(8):
                gi = hf * 8 + t
                for half2 in range(2):
                    for j in range(22):
                        MM(pw[:, half2 * 512:(half2 + 1) * 512], actT[:, j, t * 128:(t + 1) * 128],
                           wout[:, j, half2 * 512:(half2 + 1) * 512], start=(j == 0), stop=(j == 21))
                x1_ = x1t[t % 2]
                if t == 0:
                    LOAD("sp", x1t[0][:], x1_d.at(gi, x1_d.ap[gi]))
                if t + 1 < 8:
                    LOAD("sp", x1t[(t + 1) % 2][:], x1_d.at(gi + 1, x1_d.ap[gi + 1]))
                TT("dve", rr[:], pw[:], rws[:, 0, :], ALU.mult)
                STT("dve", rr[:], x1_[:], ALPHA, rr[:], ALU.mult, ALU.add)
                y_ = yt[t % 2]
                layer_norm_tile((stats, mv, xn), rr, rws[:, 1, :], rws[:, 2, :], y_[:])
                dst = yp if hf == 0 else ys
                STORE("sp", dst.at(t, dst.ap[t * 128:(t + 1) * 128, :]), y_[:])
        P.barrier()
    return finish(nc, P, es_all, [yp, ys, st_out])


def finish(nc, P, es_all, outs):
    P.barrier()
    P.emit()
    try:
        es_all.close()
    except Exception:
        pass
    P.close()
    return nc


_NC_CACHE = {}


def kernel(**inputs):
    I = {k: np.asarray(v) for k, v in inputs.items()}
    if "nc" not in _NC_CACHE:
        _NC_CACHE["nc"] = build_program()
    nc = _NC_CACHE["nc"]
    in_maps = [_core_inputs(r, I) for r in range(8)]
    res = run_bass_kernel_spmd(nc, in_maps, core_ids=list(range(8)))
    y_prompt = np.concatenate([res.results[r]["yp"].reshape(4, 256, D) for r in range(8)], 0)
    y_sample = np.zeros((2, 4096, D), np.float32)
    for r in range(8):
        y_sample[r // 4, 1024 * (r % 4):1024 * (r % 4 + 1)] = res.results[r]["ys"]
    new_state = np.concatenate([res.results[r]["st"].reshape(4, 1, 2, 8, 128, 128) for r in range(8)], 0)
    return (y_prompt.astype(np.float32), y_sample, new_state.astype(np.float32))
```

```python
import math
import contextlib
import numpy as np
import ml_dtypes
import concourse.bass as bass
import concourse.mybir as mybir
from concourse.bass_utils import run_bass_kernel_spmd

F32 = mybir.dt.float32
BF16 = mybir.dt.bfloat16
AF = mybir.ActivationFunctionType
ALU = mybir.AluOpType
AX = mybir.AxisListType

D = 1024
NH = 8
LP = 256
LS = 1024
ALPHA = (2.0 * 1) ** 0.25
MAGIC = 12582912.0
TWO_PI = 2.0 * math.pi
DEBUG = False


class Buf:
    __slots__ = ("name", "wtok", "rtoks")

    def __init__(self, name):
        self.name = name
        self.wtok = []
        self.rtoks = []


class V:
    __slots__ = ("ap", "b", "o")

    def __init__(self, ap, b, o=None):
        self.ap = ap
        self.b = b
        self.o = o

    def __getitem__(self, k):
        return V(self.ap[k], self.b, self.o)

    def r(self, pat, **kw):
        return V(self.ap.rearrange(pat, **kw), self.b, self.o)

    def bc(self, axis, shape):
        return V(self.ap.unsqueeze(axis).to_broadcast(shape), self.b, self.o)

    def pb(self, n=128):
        return V(self.ap.partition_broadcast(n), self.b, self.o)


class Tl:
    def __init__(self, P, t, name):
        self.P = P
        self.t = t
        self.b = Buf(name)
        self.name = name
        self.dsem = None

    def __getitem__(self, k):
        return V(self.t[k], self.b, self)

    def sem(self):
        if self.dsem is None:
            self.dsem = self.P.new_sem("d_" + self.name)
        return self.dsem


class Dr:
    def __init__(self, ap, name):
        self.ap = ap
        self.name = name
        self.bufs = {}

    def at(self, key, ap=None):
        if key not in self.bufs:
            self.bufs[key] = Buf(self.name + str(key))
        return V(self.ap if ap is None else ap, self.bufs[key], None)


class Planner:
    ENGS = ("pe", "act", "dve", "pool", "sp")

    def __init__(self, nc):
        self.nc = nc
        self.streams = {e: [] for e in self.ENGS}
        self.sems = {}
        self.cnt = {}
        self.waited = {e: {} for e in self.ENGS}
        self._ctx = []
        self.free_sems = []
        self.ninst = 0
        for e in ("pe", "act", "dve", "pool"):
            self.new_sem("E_" + e)

    def new_sem(self, name):
        if name in self.sems:
            name = name + "_%d" % len(self.sems)
        cm = self.nc.semaphore(name)
        h = cm.__enter__()
        self._ctx.append(cm)
        self.sems[name] = h
        self.cnt[name] = 0
        return name

    def close(self):
        for cm in reversed(self._ctx):
            cm.__exit__(None, None, None)

    def _waits(self, eng, deps):
        need = {}
        for (s, v) in deps:
            if need.get(s, 0) < v:
                need[s] = v
        out = []
        for s, v in need.items():
            if s == "E_pe" and eng == "pe":
                continue
            if self.waited[eng].get(s, 0) >= v:
                continue
            self.waited[eng][s] = v
            out.append((s, v))
        return out

    @staticmethod
    def _deps(reads, writes):
        deps = []
        for b in reads:
            deps += b.wtok
        for b in writes:
            deps += b.wtok
            deps += b.rtoks
        return deps

    @staticmethod
    def _commit(tok, reads, writes):
        for b in reads:
            b.rtoks.append(tok)
        for b in writes:
            b.wtok = [tok]
            b.rtoks = []

    def op(self, eng, fn, reads=(), writes=()):
        waits = self._waits(eng, self._deps(reads, writes))
        sname = "E_" + eng
        self.cnt[sname] += 1
        tok = (sname, self.cnt[sname])
        sems = self.sems
        self.ninst += 1 + len(waits)

        def thunk(h, waits=waits, fn=fn, sname=sname):
            for (s, v) in waits:
                h.wait_ge(sems[s], v)
            fn(h).then_inc(sems[sname], 1)
        self.streams[eng].append(thunk)
        self._commit(tok, reads, writes)
        return tok

    def dma(self, q, sem, pairs, reads=(), writes=()):
        waits = self._waits(q, self._deps(reads, writes))
        self.cnt[sem] += 16 * len(pairs)
        tok = (sem, self.cnt[sem])
        sems = self.sems
        self.ninst += len(pairs) + len(waits)

        def thunk(h, waits=waits, pairs=pairs, sem=sem):
            for (s, v) in waits:
                h.wait_ge(sems[s], v)
            for (o, i) in pairs:
                h.dma_start(out=o, in_=i).then_inc(sems[sem], 16)
        self.streams[q].append(thunk)
        self._commit(tok, reads, writes)
        return tok

    def barrier(self):
        snap = [(s, v) for s, v in self.cnt.items() if v > 0]
        sems = self.sems
        for eng in self.ENGS:
            waits = self._waits(eng, snap)
            self.ninst += len(waits)

            def thunk(h, waits=waits):
                for (s, v) in waits:
                    h.wait_ge(sems[s], v)
            self.streams[eng].append(thunk)

    def emit(self):
        nc = self.nc
        st = self.streams
        with nc.allow_non_contiguous_dma(reason="small vectors"), nc.Block() as block:
            @block.tensor
            def _(h):
                for t in st["pe"]:
                    t(h)

            @block.scalar
            def _(h):
                for t in st["act"]:
                    t(h)

            @block.vector
            def _(h):
                for t in st["dve"]:
                    t(h)

            @block.gpsimd
            def _(h):
                for t in st["pool"]:
                    t(h)

            @block.sync
            def _(h):
                for t in st["sp"]:
                    t(h)


def _bf(a):
    return np.ascontiguousarray(a.astype(np.float32)).astype(ml_dtypes.bfloat16)


def _ptile(a):
    n = a.shape[0] // 128
    return np.ascontiguousarray(a.reshape(n, 128, -1).transpose(1, 0, 2))


def _dft_consts(L):
    N = 2 * L
    f = np.arange(L, dtype=np.float64)[:, None] + 0.5
    n = np.arange(L, dtype=np.float64)[None, :]
    w = 2 * np.pi * f / N
    Cs = np.cos(w * (n + 0.5))
    Ss = np.sin(w * (n + 0.5))
    C0T = (np.cos(w * n) * (2.0 / N)).T
    S0Tn = (-np.sin(w * n) * (2.0 / N)).T
    return [_bf(_ptile(m)) for m in (Cs, Ss, C0T, S0Tn)]


def _filt_feats(L, pos):
    pos = np.asarray(pos)
    t_all = np.linspace(0.0, 1.0, L, dtype=np.float32)
    t = t_all[pos][:, None]
    wpos = (np.float32(2.0 * math.pi / L) * np.arange(L, dtype=np.float32))[pos][:, None]
    bands = np.linspace(1e-4, 15.0, 16, dtype=np.float32)[None, :]
    z = np.concatenate([t, np.cos(bands * wpos), -np.sin(bands * wpos)], axis=-1).astype(np.float32)
    return np.ascontiguousarray(z.T), t_all[pos]


def _ctx_order(c):
    return list(range(0, c)) + list(range(3, c, -1))


_CONST_CACHE = {}


def _shared_consts():
    if _CONST_CACHE:
        return _CONST_CACHE
    s = np.arange(128)[:, None]
    c = np.arange(128)[None, :]
    cc = {}
    cc["ident"] = _bf(np.eye(128))
    mq_f = (s <= c).astype(np.float32) - (s <= 63)
    mst_f = (s > c).astype(np.float32)
    mq_b = (s >= c).astype(np.float32) - (s >= 64)
    mst_b = (s < c).astype(np.float32)
    cc["cmat"] = np.ascontiguousarray(np.stack([mq_f, mst_f, mq_b, mst_b], 1).astype(np.float32))
    cc["mst"] = (mst_f.astype(np.float32), mst_b.astype(np.float32))
    cols_f = np.stack([(np.arange(128) <= 63), np.ones(128)], 1)
    cols_b = np.stack([(np.arange(128) >= 64), np.ones(128)], 1)
    cc["cols2"] = np.ascontiguousarray(np.stack([cols_f, cols_b], 1).astype(np.float32))
    mk_f = np.tile((s <= c).astype(np.float32), (1, 4))
    mk_b = np.tile((s >= c).astype(np.float32), (1, 4))
    cc["masks"] = _bf(np.stack([mk_f, mk_b], 1))
    cs, ss, c0, s0 = _dft_consts(LS)
    cc["dftS"] = np.ascontiguousarray(np.stack([cs, ss, c0, s0], 1))
    cs, ss, c0, s0 = _dft_consts(LP)
    cc["dftP"] = np.ascontiguousarray(np.stack([cs, ss, c0, s0], 1))
    max_decay = math.log(1e-2) / 0.3
    min_decay = math.log(1e-2) / 1.5
    deltas = np.linspace(min_decay, max_decay, D, dtype=np.float32)
    cc["absd"] = np.abs(deltas).astype(np.float32)
    _CONST_CACHE.update(cc)
    return cc


def _core_inputs(r, I):
    cc = _shared_consts()
    b, c = r // 4, r % 4
    f32 = np.float32
    m = {}
    m["xp"] = np.ascontiguousarray(I["x_prompt"][4 * r:4 * r + 4].reshape(1024, D))
    order = _ctx_order(c)
    segs = [c] + order
    m["xs"] = np.ascontiguousarray(np.concatenate([I["x_sample"][b, 1024 * g:1024 * g + 1024] for g in segs], 0))
    cond2 = np.stack([I["c_ctx"], I["c"][b]], 0)
    m["condT"] = np.ascontiguousarray(cond2.reshape(2, 8, 128).transpose(2, 1, 0))
    m["s0"] = np.ascontiguousarray(I["state_hgrn"][b, 0].reshape(16, 128, 128).transpose(1, 0, 2))
    m["ada_w"] = I["ada_w"][0]
    m["ada_b2"] = np.ascontiguousarray(np.broadcast_to(I["ada_b"][0][None], (2, 6 * D)))
    w_in = I["w_in"][0]
    m["w_in"] = w_in
    m["w_fsel"] = np.ascontiguousarray(np.stack(
        [w_in[:, 1024:2048] if g < c else w_in[:, 2048:3072] for g in order], 0))
    lbl = I["hgrn_lb_logits"]
    rows = [lbl[:, 0], lbl[:, 1]] + [lbl[:, 0] if g < c else lbl[:, 1] for g in order]
    m["lbl5"] = np.ascontiguousarray(np.stack(rows, 0).transpose(1, 0, 2))
    m["normw"] = np.ascontiguousarray(np.tile(I["hgrn_norm_w"][0], 8))
    m["convw"] = np.ascontiguousarray(I["hy_conv_w"][0].reshape(3, 24, 128).transpose(2, 1, 0))
    m["convb"] = np.ascontiguousarray(I["hy_conv_b"][0].reshape(24, 128).T)
    m["fw1"] = I["filt_w1"][0]
    m["fw2"] = I["filt_w2"][0]
    m["fw3"] = I["filt_w3"][0]
    m["fb"] = np.ascontiguousarray(np.stack([I["filt_b1"][0], I["filt_b2"][0], I["filt_b3"][0]], 1))
    m["ffreq"] = np.ascontiguousarray(I["filt_freq"][0].T)
    w4 = I["filt_w4"][0]
    w4f, w4b = w4[:, :D], w4[:, D:]
    zt_list, t_list, w4_list = [], [], [w4f, w4b]
    zp, tp = _filt_feats(LP, np.arange(LP))
    zt_list.append(zp)
    t_list.append(tp)
    for g in segs:
        dl = c - g
        j = np.arange(LS)
        if dl >= 0:
            pf = LS * dl + j
            wf = w4f
        else:
            pf = LS * (-dl) - j
            wf = w4b
        if dl > 0:
            pb = LS * dl - j
            wb = w4f
        else:
            pb = LS * (-dl) + j
            wb = w4b
        pb = np.clip(pb, 0, 4095)
        for (pp, ww) in ((pf, wf), (pb, wb)):
            z, t = _filt_feats(4096, pp)
            zt_list.append(z)
            t_list.append(t)
            w4_list.append(ww)
    m["zt"] = np.ascontiguousarray(np.concatenate(zt_list, 1))
    tt = np.concatenate(t_list, 0)
    m["negt"] = np.ascontiguousarray((-tt).reshape(66, 128).T.astype(f32))
    m["w4all"] = np.ascontiguousarray(np.stack(w4_list, 1))
    m["skipT"] = np.ascontiguousarray(I["hy_skip"][0].reshape(8, 128).T)
    m["absd"] = cc["absd"]
    m["proj_a"] = I["proj_a"][0]
    m["proj_b"] = I["proj_b"][0]
    m["w_out"] = I["w_out"][0]
    m["lnrows"] = np.ascontiguousarray(np.stack([I["ln1_g"][0], I["ln1_b"][0], I["ln2_g"][0], I["ln2_b"][0]], 0))
    m["ffn_w_in"] = I["ffn_w_in"][0]
    m["ffn_w_out"] = I["ffn_w_out"][0]
    m["ident"] = cc["ident"]
    m["cmat"] = cc["cmat"]
    mst_f, mst_b = cc["mst"]
    m["mctx"] = np.ascontiguousarray(np.stack([mst_f if g < c else mst_b for g in order], 1))
    m["cols2"] = cc["cols2"]
    m["masks"] = cc["masks"]
    mfb = np.zeros((128, 3, 2), f32)
    for k, g in enumerate(order):
        mfb[:, k, 0] = 1.0 if g < c else 0.0
        mfb[:, k, 1] = 0.0 if g < c else 1.0
    m["mfb"] = mfb
    m["dftS"] = cc["dftS"]
    m["dftP"] = cc["dftP"]
    return m


IN_SPECS = [
    ("xp", [1024, D], F32), ("xs", [4096, D], F32), ("condT", [128, 8, 2], F32), ("s0", [128, 16, 128], F32),
    ("ada_w", [D, 6 * D], F32), ("ada_b2", [2, 6 * D], F32), ("w_in", [D, 10240], F32),
    ("w_fsel", [3, D, D], F32), ("lbl5", [2, 5, D], F32), ("normw", [D], F32),
    ("convw", [128, 24, 3], F32), ("convb", [128, 24], F32), ("fw1", [33, 64], F32), ("fw2", [64, 64], F32),
    ("fw3", [64, 64], F32), ("fb", [64, 3], F32), ("ffreq", [64, 3], F32), ("zt", [33, 8448], F32),
    ("negt", [128, 66], F32), ("w4all", [64, 10, D], F32), ("skipT", [128, 8], F32), ("absd", [D], F32),
    ("proj_a", [D, D], F32), ("proj_b", [D, D], F32), ("w_out", [D, D], F32), ("lnrows", [4, D], F32),
    ("ffn_w_in", [D, 5632], F32), ("ffn_w_out", [2816, D], F32), ("ident", [128, 128], BF16),
    ("cmat", [128, 4, 128], F32), ("mctx", [128, 3, 128], F32), ("cols2", [128, 2, 2], F32),
    ("masks", [128, 2, 512], BF16), ("mfb", [128, 3, 2], F32), ("dftS", [128, 4, 8, 1024], BF16),
    ("dftP", [128, 4, 2, 256], BF16),
]


def build_program(stop_after=None):
    nc = bass.Bass("TRN2", target_bir_lowering=False)
    P = Planner(nc)
    I = {}
    for (name, shape, dt) in IN_SPECS:
        I[name] = Dr(nc.dram_tensor(name, shape, dt, kind="ExternalInput").ap(), name)
    yp = Dr(nc.dram_tensor("yp", [1024, D], F32, kind="ExternalOutput").ap(), "yp")
    ys = Dr(nc.dram_tensor("ys", [1024, D], F32, kind="ExternalOutput").ap(), "ys")
    st_out = Dr(nc.dram_tensor("st", [4, 2, 8, 128, 128], F32, kind="ExternalOutput").ap(), "st")
    skind = "ExternalOutput" if DEBUG else "Internal"

    def scratch(name, shape, dt):
        return Dr(nc.dram_tensor(name, shape, dt, kind=skind).ap(), name)
    rows_d = scratch("rows_d", [2, 6, D], F32)
    lb_d = scratch("lb_d", [5, 2, D], F32)
    hT_d = scratch("hT_d", [40, 128, 8, 128], BF16)
    oaT_d = scratch("oaT_d", [128, 8, 2048], BF16)
    obT_d = scratch("obT_d", [128, 8, 2048], BF16)
    x1_d = scratch("x1_d", [16, 128, D], F32)
    h2T_d = scratch("h2T_d", [16, 128, 8, 128], BF16)
    qc_d = scratch("qc_d", [16, 2, 128, 512], F32)
    vc_d = scratch("vc_d", [16, 2, 128, 512], BF16)

    es_all = contextlib.ExitStack()

    uid = [0]

    def alloc(es, name, shape, dt, psum=False):
        uid[0] += 1
        name = "%s_%d" % (name, uid[0])
        cm = nc.psum_tensor("t_" + name, shape, dt) if psum else nc.sbuf_tensor("t_" + name, shape, dt)
        return Tl(P, es.enter_context(cm), name)

    def bufs(*vs):
        out = []
        for v in vs:
            if v is None or isinstance(v, (int, float)):
                continue
            if v.b not in out:
                out.append(v.b)
        return out

    def A(v):
        return v.ap if isinstance(v, V) else v

    def MM(out, lhsT, rhs, start=True, stop=True):
        P.op("pe", lambda h: h.matmul(out.ap, lhsT.ap, rhs.ap, start=start, stop=stop),
             reads=bufs(lhsT, rhs), writes=bufs(out))

    def TR(out, in_, ident):
        P.op("pe", lambda h: h.transpose(out.ap, in_.ap, ident.ap), reads=bufs(in_, ident), writes=bufs(out))

    def ACT(out, in_, func, bias=None, scale=None, eng="act"):
        kw = {}
        if bias is not None:
            kw["bias"] = A(bias)
        if scale is not None:
            kw["scale"] = A(scale)
        P.op(eng, lambda h: h.activation(out.ap, in_.ap, func, **kw), reads=bufs(in_, bias, scale), writes=bufs(out))

    def TT(eng, out, in0, in1, op):
        P.op(eng, lambda h: h.tensor_tensor(out.ap, in0.ap, in1.ap, op), reads=bufs(in0, in1), writes=bufs(out))

    def TS(eng, out, in0, s1, s2, op0, op1=None):
        if op1 is None:
            P.op(eng, lambda h: h.tensor_scalar(out.ap, in0.ap, A(s1), None, op0), reads=bufs(in0, s1), writes=bufs(out))
        else:
            P.op(eng, lambda h: h.tensor_scalar(out.ap, in0.ap, A(s1), A(s2), op0, op1),
                 reads=bufs(in0, s1, s2), writes=bufs(out))

    def STT(eng, out, in0, scalar, in1, op0, op1):
        P.op(eng, lambda h: h.scalar_tensor_tensor(out.ap, in0.ap, A(scalar), in1.ap, op0, op1),
             reads=bufs(in0, scalar, in1), writes=bufs(out))

    def CP(eng, out, in_):
        if eng == "act":
            P.op(eng, lambda h: h.copy(out.ap, in_.ap), reads=bufs(in_), writes=bufs(out))
        else:
            P.op(eng, lambda h: h.tensor_copy(out.ap, in_.ap), reads=bufs(in_), writes=bufs(out))

    def MEMSET(eng, out, val):
        P.op(eng, lambda h: h.memset(out.ap, val), reads=[], writes=bufs(out))

    def LOAD(q, out, in_):
        pairs = list(zip(out, in_)) if isinstance(out, list) else [(out, in_)]
        tl = pairs[0][0].o
        P.dma(q, tl.sem(), [(o.ap, i.ap) for (o, i) in pairs],
              reads=bufs(*[i for (_, i) in pairs]), writes=bufs(*[o for (o, _) in pairs]))

    def STORE(q, out, in_):
        tl = in_.o
        P.dma(q, tl.sem(), [(out.ap, in_.ap)], reads=bufs(in_), writes=bufs(out))

    def wview(dr, c0, c1, key=None):
        return dr.at(key if key is not None else "w", dr.ap[:, c0:c1].rearrange("(k p) n -> p k n", p=128))

    es0 = es_all
    ident = alloc(es0, "ident", [128, 128], BF16)
    ones1 = alloc(es0, "ones1", [128, 1], F32)
    LOAD("sp", ident[:], I["ident"].at("w"))
    MEMSET("dve", ones1[:], 1.0)

    def psum_std(es):
        pa_ = [alloc(es, "pa%d" % i, [128, 512], F32, psum=True) for i in range(4)]
        pw_ = alloc(es, "pw", [128, 1024], F32, psum=True)
        pt_ = [alloc(es, "pt%d" % i, [128, 1024], BF16, psum=True) for i in range(2)]
        return pa_, pw_, pt_

    def rowload(q, out, dr_v):
        LOAD(q, out, dr_v.pb(128))

    def hv(v):
        return v.r("p (h v) -> p h v", h=8)

    with contextlib.ExitStack() as es:
        pa, pw, pt = psum_std(es)
        scT = alloc(es, "scT", [128, 8, 2], F32)
        adab = alloc(es, "adab", [2, 6 * D], F32)
        mod = alloc(es, "mod", [2, 6 * D], F32)
        adaw = [alloc(es, "adaw%d" % i, [128, 8, 512], F32) for i in range(2)]
        lnr = alloc(es, "lnr", [2, 2, D], F32)
        lbt = alloc(es, "lbt", [5, 2, D], F32)
        lbo = alloc(es, "lbo", [5, 2, D], F32)
        LOAD("sp", scT[:], I["condT"].at("w"))
        LOAD("sp", adab[:], I["ada_b2"].at("w"))
        ACT(scT[:], scT[:], AF.Silu)
        adw = I["ada_w"]
        for blk in range(12):
            t = adaw[blk % 2]
            LOAD("sp" if blk % 2 == 0 else "act", t[:], wview(adw, blk * 512, (blk + 1) * 512))
            for k in range(8):
                MM(pa[0][0:2, :], scT[:, k, :], t[:, k, :], start=(k == 0), stop=(k == 7))
            TT("dve", mod[:, blk * 512:(blk + 1) * 512], pa[0][0:2, :], adab[:, blk * 512:(blk + 1) * 512], ALU.add)
        LOAD("sp", [lnr[:, 0, :], lnr[:, 1, :]],
             [I["lnrows"].at("w", I["lnrows"].ap[0]).pb(2), I["lnrows"].at("w", I["lnrows"].ap[1]).pb(2)])
        TS("dve", mod[:, 1 * D:2 * D], mod[:, 1 * D:2 * D], 1.0, None, ALU.add)
        TS("dve", mod[:, 4 * D:5 * D], mod[:, 4 * D:5 * D], 1.0, None, ALU.add)
        TT("dve", lnr[:, 1, :], lnr[:, 1, :], mod[:, 4 * D:5 * D], ALU.mult)
        TT("dve", lnr[:, 1, :], lnr[:, 1, :], mod[:, 3 * D:4 * D], ALU.add)
        TT("dve", lnr[:, 0, :], lnr[:, 0, :], mod[:, 4 * D:5 * D], ALU.mult)
        rv = rows_d.ap
        STORE("sp", rows_d.at(0, rv[:, 0, :]), mod[:, 0:D])
        STORE("sp", rows_d.at(1, rv[:, 1, :]), mod[:, D:2 * D])
        STORE("sp", rows_d.at(2, rv[:, 2, :]), mod[:, 2 * D:3 * D])
        STORE("sp", rows_d.at(3, rv[:, 3, :]), lnr[:, 0, :])
        STORE("sp", rows_d.at(4, rv[:, 4, :]), lnr[:, 1, :])
        STORE("sp", rows_d.at(5, rv[:, 5, :]), mod[:, 5 * D:6 * D])
        LOAD("sp", [lbt[:, 0, :], lbt[:, 1, :]], [I["lbl5"].at("w", I["lbl5"].ap[0]), I["lbl5"].at("w", I["lbl5"].ap[1])])
        TT("dve", lbt[:, 0, :], lbt[:, 0, :], lbt[:, 1, :], ALU.subtract)
        ACT(lbo[:, 0, :], lbt[:, 0, :], AF.Sigmoid)
        TS("dve", lbo[:, 1, :], lbo[:, 0, :], -0.5, 0.5, ALU.mult, ALU.add)
        TT("dve", lbo[:, 0, :], lbo[:, 0, :], lbo[:, 1, :], ALU.add)
        STORE("sp", lb_d.at("w"), lbo[:])
        P.barrier()

        mrow = alloc(es, "mrow", [128, 2, 2, D], F32)
        for cnd in range(2):
            rowload("sp", mrow[:, cnd, 0, :], rows_d.at(0, rv[cnd, 0, :]))
            rowload("sp", mrow[:, cnd, 1, :], rows_d.at(1, rv[cnd, 1, :]))
        xt = [alloc(es, "xt%d" % i, [128, D], F32) for i in range(3)]
        hb = [alloc(es, "hb%d" % i, [128, D], BF16) for i in range(2)]
        hTt = [alloc(es, "hTt%d" % i, [128, 8, 128], BF16) for i in range(2)]
        def xload(ti):
            src = I["xp"].ap[ti * 128:(ti + 1) * 128, :] if ti < 8 else I["xs"].ap[(ti - 8) * 128:(ti - 7) * 128, :]
            LOAD("sp", xt[ti % 3][:], (I["xp"] if ti < 8 else I["xs"]).at("w", src))
        xload(0)
        xload(1)
        for ti in range(40):
            cnd = 0 if ti < 8 else 1
            x_ = xt[ti % 3]
            h_ = hb[ti % 2]
            o_ = hTt[ti % 2]
            if ti + 2 < 40:
                xload(ti + 2)
            TT("dve", x_[:], x_[:], mrow[:, cnd, 1, :], ALU.mult)
            TT("pool", h_[:], x_[:], mrow[:, cnd, 0, :], ALU.add)
            for k in range(8):
                TR(pt[ti % 2][:, k * 128:(k + 1) * 128], h_[:, k * 128:(k + 1) * 128], ident[:])
            CP("act", o_[:].r("p k t -> p (k t)"), pt[ti % 2][:])
            STORE("sp", hT_d.at(ti, hT_d.ap[ti]), o_[:])
        P.barrier()
    if stop_after == 0:
        return finish(nc, P, es_all, [yp, ys, st_out])

    es_ab = contextlib.ExitStack()
    cmat = alloc(es_ab, "cmat", [128, 4, 128], F32)
    cols2 = alloc(es_ab, "cols2", [128, 2, 2], F32)
    masks = alloc(es_ab, "masks", [128, 2, 512], BF16)
    Sst = [[alloc(es_ab, "S%d_%d" % (d, hh), [128, 4, 128], F32) for hh in range(2)] for d in range(2)]
    PW = [alloc(es_ab, "PW%d" % i, [128, 512], F32, psum=True) for i in range(2)]
    PC = [alloc(es_ab, "PC%d" % i, [128, 512], F32, psum=True) for i in range(2)]
    PS = [alloc(es_ab, "PS%d" % i, [128, 512], F32, psum=True) for i in range(2)]
    PT = [alloc(es_ab, "PT%d" % i, [128, 1024], BF16, psum=True) for i in range(2)]
    dmy = alloc(es_ab, "dmy", [128, 2], F32)
    MEMSET("dve", dmy[:], 1.0)

    def TABLE_PREFETCH(func):
        ACT(dmy[:, 1:2], dmy[:, 0:1], func)

    LOAD("sp", cmat[:], I["cmat"].at("w"))
    LOAD("sp", cols2[:], I["cols2"].at("w"))
    LOAD("sp", masks[:], I["masks"].at("w"))
    for d in range(2):
        for hh in range(2):
            LOAD("sp", Sst[d][hh][:], I["s0"].at("w", I["s0"].ap[:, 8 * d + 4 * hh:8 * d + 4 * hh + 4, :]))

    def hv4(v):
        return v.r("p (h v) -> p h v", h=4)

    def run_rr(gens):
        alive = [True] * len(gens)
        while any(alive):
            for i, g in enumerate(gens):
                if alive[i]:
                    try:
                        next(g)
                    except StopIteration:
                        alive[i] = False

    with contextlib.ExitStack() as es:
        Wi = alloc(es, "Wi", [128, 8, D], BF16)
        Wf = [alloc(es, "Wf%d" % i, [128, 8, D], BF16) for i in range(2)]
        lbr = alloc(es, "lbr", [128, 2, D], F32)
        mctx = alloc(es, "mctx", [128, 3, 128], F32)
        mfb = alloc(es, "mfb", [128, 3, 2], F32)
        hTa = [alloc(es, "hTa%d" % i, [128, 8, 128], BF16) for i in range(4)]
        PTf = [V(PT[k].t[:].bitcast(F32), PT[k].b, PT[k]) for k in range(2)]
        banksA = {(0, 0): (PW[0][:], PC[0][:]), (0, 1): (PW[1][:], PC[1][:]),
                  (1, 0): (PS[0][:], PTf[0]), (1, 1): (PS[1][:], PTf[1])}
        scA = {}
        for tp in range(2):
            for hh in range(2):
                sfx = "A%d%d" % (tp, hh)
                scA[(tp, hh)] = dict(
                    sg=alloc(es, "sg" + sfx, [128, 512], F32), lf=alloc(es, "lf" + sfx, [128, 512], F32),
                    kk=alloc(es, "kk" + sfx, [128, 512], F32), kst=alloc(es, "kst" + sfx, [128, 512], BF16),
                    vb=alloc(es, "vb" + sfx, [128, 512], BF16), Dt=alloc(es, "Dt" + sfx, [128, 4], F32),
                    al=alloc(es, "al" + sfx, [128, 4], F32), be=alloc(es, "be" + sfx, [128, 4], F32),
                    tmpU=alloc(es, "tmpU" + sfx, [128, 4, 128], F32))
        aggA = [dict(Dagg=alloc(es, "DaggA%d" % hh, [128, 4], F32), Uagg=alloc(es, "UaggA%d" % hh, [128, 4, 128], F32),
                     al=alloc(es, "alS%d" % hh, [128, 4], F32)) for hh in range(2)]
        LOAD("pool", Wi[:], wview(I["w_in"], 3072, 4096))
        LOAD("sp", mctx[:], I["mctx"].at("w"))
        LOAD("sp", mfb[:], I["mfb"].at("w"))

        def ctx_gen(sl, tp, hh, hT, W_):
            c = scA[(tp, hh)]
            g = aggA[hh]
            X, Y = banksA[(tp, hh)]
            cs = slice(hh * 512, (hh + 1) * 512)
            sg, lf, kk, kst, vb = c["sg"], c["lf"], c["kk"], c["kst"], c["vb"]
            Dt, al, be, tmpU = c["Dt"], c["al"], c["be"], c["tmpU"]
            Dagg, Uagg = g["Dagg"], g["Uagg"]
            for k in range(8):
                MM(X, hT[:, k, :], W_[:, k, cs], start=(k == 0), stop=(k == 7))
            ACT(sg[:], X, AF.Tanh, scale=0.5)
            if tp == 1 and hh == 1:
                TABLE_PREFETCH(AF.Ln)
            yield
            TT("dve", sg[:], sg[:], lbr[:, 1, cs], ALU.mult)
            TT("dve", sg[:], sg[:], lbr[:, 0, cs], ALU.add)
            yield
            ACT(lf[:], sg[:], AF.Ln)
            TS("pool", kk[:], sg[:], -1.0, 1.0, ALU.mult, ALU.add)
            yield
            MM(Y, mctx[:, sl, :], lf[:])
            ACT(sg[:], Y, AF.Exp)
            TT("dve", kst[:], kk[:], sg[:], ALU.mult)
            yield
            for h in range(4):
                MM(X[:, h:h + 1], lf[:, h * 128:(h + 1) * 128], ones1[:])
            ACT(Dt[:], X[:, 0:4], AF.Exp)
            yield
            for k in range(8):
                MM(Y, hT[:, k, :], Wi[:, k, cs], start=(k == 0), stop=(k == 7))
            CP("act", vb[:], Y)
            if tp == 1 and hh == 1:
                TABLE_PREFETCH(AF.Tanh)
            yield
            for h in range(4):
                MM(X[:, h * 128:(h + 1) * 128], kst[:, h * 128:(h + 1) * 128], vb[:, h * 128:(h + 1) * 128])
            yield
            TS("dve", al[:], Dt[:], -1.0, mfb[:, sl, 0:1], ALU.add, ALU.mult)
            TS("dve", al[:], al[:], 1.0, None, ALU.add)
            TS("dve", be[:], Dagg[:], -1.0, mfb[:, sl, 1:2], ALU.add, ALU.mult)
            TS("dve", be[:], be[:], 1.0, None, ALU.add)
            TT("dve", Dagg[:], Dagg[:], Dt[:], ALU.mult)
            TT("dve", tmpU[:], hv4(X), be[:].bc(2, [128, 4, 128]), ALU.mult)
            TT("pool", Uagg[:], Uagg[:], al[:].bc(2, [128, 4, 128]), ALU.mult)
            TT("pool", Uagg[:], Uagg[:], tmpU[:], ALU.add)
            yield

        def hload(ti):
            LOAD("sp", hTa[ti % 4][:], hT_d.at(ti, hT_d.ap[ti]))
        hload(16)
        hload(17)
        for sl in range(3):
            W_ = Wf[sl % 2]
            LOAD("pool", W_[:], I["w_fsel"].at("w", I["w_fsel"].ap[sl].rearrange("(k p) n -> p k n", p=128)))
            rowload("sp", lbr[:, 0, :], lb_d.at("w", lb_d.ap[2 + sl, 0, :]))
            rowload("sp", lbr[:, 1, :], lb_d.at("w", lb_d.ap[2 + sl, 1, :]))
            for hh in range(2):
                MEMSET("dve", aggA[hh]["Uagg"][:], 0.0)
                MEMSET("dve", aggA[hh]["Dagg"][:], 1.0)
            for t in range(0, 8, 2):
                ti = 16 + sl * 8 + t
                for nx in (ti + 2, ti + 3):
                    if nx < 40:
                        hload(nx)
                gens = []
                for tp in range(2):
                    for hh in range(2):
                        gens.append(ctx_gen(sl, tp, hh, hTa[(ti + tp) % 4], W_))
                run_rr(gens)
            for hh in range(2):
                g = aggA[hh]
                for d in range(2):
                    TS("dve", g["al"][:], g["Dagg"][:], -1.0, mfb[:, sl, d:d + 1], ALU.add, ALU.mult)
                    TS("dve", g["al"][:], g["al"][:], 1.0, None, ALU.add)
                    TT("dve", Sst[d][hh][:], Sst[d][hh][:], g["al"][:].bc(2, [128, 4, 128]), ALU.mult)
                    STT("dve", Sst[d][hh][:], g["Uagg"][:], mfb[:, sl, d:d + 1], Sst[d][hh][:], ALU.mult, ALU.add)
        P.barrier()
    if stop_after == 1:
        return finish(nc, P, es_all, [yp, ys, st_out])

    with contextlib.ExitStack() as es:
        Wg = {}
        for gi, (nm, c0) in enumerate((("q", 0), ("ff", 1024), ("i", 3072), ("fb", 2048), ("g", 4096))):
            Wg[nm] = alloc(es, "W" + nm, [128, 8, D], BF16)
            LOAD("pool", Wg[nm][:], wview(I["w_in"], c0, c0 + 1024))
        lbr = alloc(es, "lbrB", [128, 2, 2, D], F32)
        nrow = alloc(es, "nrow", [128, D], F32)
        for d in range(2):
            rowload("sp", lbr[:, d, 0, :], lb_d.at("w", lb_d.ap[d, 0, :]))
            rowload("sp", lbr[:, d, 1, :], lb_d.at("w", lb_d.ap[d, 1, :]))
        rowload("sp", nrow[:], I["normw"].at("w"))
        hTa = [alloc(es, "hTb%d" % i, [128, 8, 128], BF16) for i in range(2)]
        scB = []
        for hh in range(2):
            c = {}
            for nm in ("qs", "sg", "lf", "kk", "e1", "osum", "gs"):
                c[nm] = alloc(es, nm + "B%d" % hh, [128, 512], F32)
            for nm in ("qin", "kin", "kst", "vb", "scm", "oab"):
                c[nm] = alloc(es, nm + "B%d" % hh, [128, 512], BF16)
            for nm in ("qinT", "kinT", "Sq"):
                c[nm] = alloc(es, nm + "B%d" % hh, [128, 4, 128], BF16)
            c["eb"] = alloc(es, "ebB%d" % hh, [128, 4, 2], F32)
            c["ssq"] = alloc(es, "ssqB%d" % hh, [128, 4], F32)
            c["ofb"] = alloc(es, "ofbB%d" % hh, [128, 8, 512], BF16)
            c["Sp"] = [alloc(es, "SpB%d_%d" % (hh, d), [128, 4, 128], F32) for d in range(2)]
            c["oaT"] = [alloc(es, "oaTB%d_%d" % (hh, i), [128, 4, 128], BF16) for i in range(2)]
            c["no"] = 0
            scB.append(c)

        def pass_gen(d, S, slot, final, tokcol, hh, hT, tix):
            c = scB[hh]
            cs = slice(hh * 512, (hh + 1) * 512)
            PWh, PCh, PSh, PTh = PW[hh], PC[hh], PS[hh], PT[hh]
            qs, sg, lf, kk, e1, osum, gs = c["qs"], c["sg"], c["lf"], c["kk"], c["e1"], c["osum"], c["gs"]
            qin, kin, kst, vb, scm, oab = c["qin"], c["kin"], c["kst"], c["vb"], c["scm"], c["oab"]
            qinT, kinT, Sq, eb, ssq, ofb = c["qinT"], c["kinT"], c["Sq"], c["eb"], c["ssq"], c["ofb"]

            def proj(W, dst):
                for k in range(8):
                    MM(dst[:], hT[:, k, :], W[:, k, cs], start=(k == 0), stop=(k == 7))
            if d == 0:
                proj(Wg["q"], PWh)
                ACT(qs[:], PWh[:], AF.Tanh, scale=0.5)
            else:
                LOAD("sp", qs[:], qc_d.at((tix, hh), qc_d.ap[tix, hh]))
                LOAD("sp", vb[:], vc_d.at((tix, hh), vc_d.ap[tix, hh]))
                if final:
                    proj(Wg["g"], PWh)
                    ACT(gs[:], PWh[:], AF.Tanh, scale=0.5)
            proj(Wg["ff" if d == 0 else "fb"], PCh)
            ACT(sg[:], PCh[:], AF.Tanh, scale=0.5)
            if hh == 1:
                TABLE_PREFETCH(AF.Ln)
            if d == 1 and final:
                STT("dve", gs[:], gs[:], 1.0, PWh[:], ALU.add, ALU.mult)
            if d == 0:
                STT("dve", qs[:], qs[:], 1.0, PWh[:], ALU.add, ALU.mult)
                STORE("sp", qc_d.at((tix, hh), qc_d.ap[tix, hh]), qs[:])
            yield
            TT("dve", sg[:], sg[:], lbr[:, d, 1, cs], ALU.mult)
            TT("dve", sg[:], sg[:], lbr[:, d, 0, cs], ALU.add)
            if d == 0:
                proj(Wg["i"], PWh)
                CP("act", vb[:], PWh[:])
                STORE("sp", vc_d.at((tix, hh), vc_d.ap[tix, hh]), vb[:])
            yield
            ACT(lf[:], sg[:], AF.Ln)
            TS("pool", kk[:], sg[:], -1.0, 1.0, ALU.mult, ALU.add)
            yield
            MM(PCh[:], cmat[:, 2 * d, :], lf[:])
            ACT(e1[:], PCh[:], AF.Exp)
            STT("dve", qin[:], qs[:], 0.5, e1[:], ALU.mult, ALU.mult)
            yield
            ACT(e1[:], PCh[:], AF.Exp, scale=-1.0)
            TT("pool", kin[:], kk[:], e1[:], ALU.mult)
            MM(PWh[:], cmat[:, 2 * d + 1, :], lf[:])
            yield
            ACT(e1[:], PWh[:], AF.Exp)
            TT("dve", kst[:], kk[:], e1[:], ALU.mult)
            for h in range(4):
                MM(PSh[:, 2 * h:2 * h + 2], lf[:, h * 128:(h + 1) * 128], cols2[:, d, :])
            ACT(eb[:].r("p h t -> p (h t)"), PSh[:, 0:8], AF.Exp)
            yield
            for h in range(4):
                TR(PTh[:, h * 128:(h + 1) * 128], qin[:, h * 128:(h + 1) * 128], ident[:])
            CP("act", qinT[:].r("p h t -> p (h t)"), PTh[:, 0:512])
            for h in range(4):
                TR(PTh[:, 512 + h * 128:512 + (h + 1) * 128], kin[:, h * 128:(h + 1) * 128], ident[:])
            CP("dve", kinT[:].r("p h t -> p (h t)"), PTh[:, 512:1024])
            TT("pool", Sq[:], S[:], eb[:, :, 0].bc(2, [128, 4, 128]), ALU.mult)
            yield
            for h in range(4):
                MM(PSh[:, h * 128:(h + 1) * 128], kinT[:, h, :], qinT[:, h, :])
            TT("dve", scm[:], PSh[:], masks[:, d, :], ALU.mult)
            yield
            for h in range(4):
                MM(PCh[:, h * 128:(h + 1) * 128], scm[:, h * 128:(h + 1) * 128], vb[:, h * 128:(h + 1) * 128],
                   start=True, stop=False)
                MM(PCh[:, h * 128:(h + 1) * 128], qinT[:, h, :], Sq[:, h, :], start=False, stop=True)
            if not final:
                CP("act", ofb[:, slot, :], PCh[:])
            else:
                TT("dve", osum[:], PCh[:], ofb[:, slot, :], ALU.add)
            for h in range(4):
                MM(PSh[:, h * 128:(h + 1) * 128], kst[:, h * 128:(h + 1) * 128], vb[:, h * 128:(h + 1) * 128])
            yield
            TT("pool", S[:], S[:], eb[:, :, 1].bc(2, [128, 4, 128]), ALU.mult)
            TT("dve", S[:], S[:], hv4(PSh[:]), ALU.add)
            yield
            if final:
                TT("pool", e1[:], osum[:], osum[:], ALU.mult)
                P.op("dve", lambda h_: h_.tensor_reduce(ssq[:].ap, hv4(e1[:]).ap, AX.X, ALU.add),
                     reads=[e1.b], writes=[ssq.b])
                TS("dve", ssq[:], ssq[:], 1.0 / 128.0, 1e-6, ALU.mult, ALU.add)
                ACT(ssq[:], ssq[:], AF.Ln)
                ACT(ssq[:], ssq[:], AF.Exp, scale=-0.5)
                yield
                TT("dve", hv4(osum[:]), hv4(osum[:]), ssq[:].bc(2, [128, 4, 128]), ALU.mult)
                TT("pool", osum[:], osum[:], nrow[:, cs], ALU.mult)
                STT("dve", oab[:], osum[:], 0.5, gs[:], ALU.mult, ALU.mult)
                yield
                for h in range(4):
                    TR(PTh[:, h * 128:(h + 1) * 128], oab[:, h * 128:(h + 1) * 128], ident[:])
                o_ = c["oaT"][c["no"] % 2]
                c["no"] += 1
                CP("act", o_[:].r("p h t -> p (h t)"), PTh[:, 0:512])
                STORE("sp", oaT_d.at(("t", tokcol, hh), oaT_d.ap[:, 4 * hh:4 * hh + 4, tokcol:tokcol + 128]), o_[:])
                if hh == 1:
                    TABLE_PREFETCH(AF.Tanh)
                yield
            elif hh == 1:
                TABLE_PREFETCH(AF.Tanh)
                yield

        sched = []
        for j in range(4):
            sched += [(2 * j, 0, j, 0, False), (2 * j + 1, 0, j, 1, False), (2 * j + 1, 1, j, 1, True), (2 * j, 1, j, 0, True)]
        sched += [(8 + t, 0, 4, t, False) for t in range(8)] + [(8 + t, 1, 4, t, True) for t in range(7, -1, -1)]
        LOAD("sp", hTa[0][:], hT_d.at(sched[0][0], hT_d.ap[sched[0][0]]))
        for idx, (ti, d, unit, slot, final) in enumerate(sched):
            hT = hTa[idx % 2]
            if idx + 1 < len(sched):
                nti = sched[idx + 1][0]
                LOAD("sp", hTa[(idx + 1) % 2][:], hT_d.at(nti, hT_d.ap[nti]))
            tokcol = ti * 128 if unit < 4 else 1024 + (ti - 8) * 128
            if unit < 4 and d == 0 and slot == 0:
                for hh in range(2):
                    MEMSET("dve", scB[hh]["Sp"][0][:], 0.0)
                    MEMSET("pool", scB[hh]["Sp"][1][:], 0.0)
            gens = []
            for hh in range(2):
                S = scB[hh]["Sp"][d] if unit < 4 else Sst[d][hh]
                gens.append(pass_gen(d, S, slot, final, tokcol, hh, hT, ti))
            run_rr(gens)
            if unit < 4 and ((d == 0 and slot == 1) or (d == 1 and slot == 0)):
                for hh in range(2):
                    STORE("sp", st_out.at((d, unit, hh), st_out.ap[unit, d, 4 * hh:4 * hh + 4].rearrange("h a v -> a h v")),
                          scB[hh]["Sp"][d][:])
        P.barrier()
    es_ab.close()
    if stop_after == 2:
        return finish(nc, P, es_all, [yp, ys, st_out])

    with contextlib.ExitStack() as es:
        pa = [alloc(es, "q%d" % i, [128, 512], F32, psum=True) for i in range(7)]
        pt = [alloc(es, "ptc", [128, 1024], BF16, psum=True)]
        rot = {"i": 0}

        def nb():
            rot["i"] += 1
            return pa[4 + rot["i"] % 3]
        dS = alloc(es, "dS", [128, 4, 8, 1024], BF16)
        dP = alloc(es, "dP", [128, 4, 2, 256], BF16)
        LOAD("act", [dS[:, i] for i in range(4)], [I["dftS"].at("w", I["dftS"].ap[:, i]) for i in range(4)])
        LOAD("act", dP[:], I["dftP"].at("w"))
        absr = alloc(es, "absr", [128, D], F32)
        rowload("sp", absr[:], I["absd"].at("w"))
        negt = alloc(es, "negt", [128, 66], F32)
        convw = alloc(es, "convw", [128, 24, 3], F32)
        convb = alloc(es, "convb", [128, 24], F32)
        skipT = alloc(es, "skipT", [128, 8], F32)
        LOAD("sp", negt[:], I["negt"].at("w"))
        LOAD("sp", convw[:], I["convw"].at("w"))
        LOAD("sp", convb[:], I["convb"].at("w"))
        LOAD("sp", skipT[:], I["skipT"].at("w"))
        h3T = alloc(es, "h3T", [64, 8448], BF16)
        fw1 = alloc(es, "fw1", [33, 64], F32)
        fw2 = alloc(es, "fw2", [64, 64], F32)
        fw3 = alloc(es, "fw3", [64, 64], F32)
        fbt = alloc(es, "fbt", [64, 3], F32)
        frq = alloc(es, "frq", [64, 3], F32)

        Wx = [alloc(es, "Wx%d" % i, [128, 8, 3, 128], BF16) for i in range(2)]
        w4c = [alloc(es, "w4c%d" % i, [64, 10, 128], BF16) for i in range(2)]
        hTblk = [alloc(es, "hTblk%d" % i, [128, 8, 512], BF16) for i in range(2)]
        cy = [alloc(es, "cy%d" % i, [128, 512], F32) for i in range(2)]
        ub2 = [alloc(es, "ub%d" % i, [128, 512], BF16) for i in range(2)]
        uTk = [alloc(es, "uTk%d" % i, [128, 2048], BF16) for i in range(2)]
        x0k = [alloc(es, "x0k%d" % i, [128, 2048], BF16) for i in range(2)]
        utm = [alloc(es, "utm%d" % i, [128, 40, 128], BF16) for i in range(2)]
        ew = alloc(es, "ew", [128, 512], F32)
        hwf = alloc(es, "hwf", [128, 512], F32)
        hwb = alloc(es, "hwb", [128, 512], F32)
        hpm2 = [alloc(es, "hpm%d" % i, [128, 2, 8, 128], BF16) for i in range(2)]
        Ksb = alloc(es, "Ksb", [128, 2, 1024], BF16)
        CSsb2 = [alloc(es, "CSsb%d" % i, [128, 2, 1024], BF16) for i in range(2)]
        identf = alloc(es, "identf", [128, 128], F32)
        CP("dve", identf[:], ident[:])
        tA = alloc(es, "tA", [128, 1024], F32)
        PQ = alloc(es, "PQ", [128, 2, 1024], F32)
        PQT = alloc(es, "PQT", [128, 2, 8, 128], BF16)

        def taps(ct, s, njt, tile_f, tile_b, wf, wb, same_pos, hpm):
            for g0 in range(0, njt, 4):
                ng = min(4, njt - g0)
                for side, (tb, wi, dst) in enumerate(((tile_f, wf, hwf), (tile_b, wb, hwb))):
                    for i in range(ng):
                        tcol = (tb + g0 + i) * 128
                        MM(pa[2 + side][:, i * 128:(i + 1) * 128], h3T[:, tcol:tcol + 128], w4c[s][:, wi, :])
                    if side == 0 or not same_pos:
                        for i in range(ng):
                            ACT(ew[:, i * 128:(i + 1) * 128], absr[:, ct * 128:(ct + 1) * 128], AF.Exp,
                                scale=negt[:, tb + g0 + i:tb + g0 + i + 1])
                    STT("dve", dst[:, 0:ng * 128], ew[:, 0:ng * 128], 0.05, pa[2 + side][:, 0:ng * 128], ALU.add, ALU.mult)
                    if side == 1 and g0 == 0:
                        MEMSET("dve", dst[0:1, 0:128], 0.0)
                TT("pool", hpm[:, 0, g0:g0 + ng, :],
                   hwf[:, 0:ng * 128].r("p (j c) -> p j c", j=ng), hwb[:, 0:ng * 128].r("p (j c) -> p j c", j=ng), ALU.add)
                TT("pool", hpm[:, 1, g0:g0 + ng, :],
                   hwf[:, 0:ng * 128].r("p (j c) -> p j c", j=ng), hwb[:, 0:ng * 128].r("p (j c) -> p j c", j=ng),
                   ALU.subtract)
                yield

        def pq_update(first, CSsb):
            Cu, Su = CSsb[:, 0, :], CSsb[:, 1, :]
            Kr, Ki = Ksb[:, 0, :], Ksb[:, 1, :]
            Pv, Qv = PQ[:, 0, :], PQ[:, 1, :]
            if first:
                TT("dve", Pv, Cu, Kr, ALU.mult)
                TT("pool", Qv, Su, Kr, ALU.mult)
            else:
                TT("dve", tA[:], Cu, Kr, ALU.mult)
                TT("dve", Pv, Pv, tA[:], ALU.add)
                TT("dve", tA[:], Su, Kr, ALU.mult)
                TT("dve", Qv, Qv, tA[:], ALU.add)
            TT("dve", tA[:], Su, Ki, ALU.mult)
            TT("dve", Pv, Pv, tA[:], ALU.add)
            TT("dve", tA[:], Cu, Ki, ALU.mult)
            TT("dve", Qv, Qv, tA[:], ALU.subtract)

        def pq_transpose():
            for w in range(2):
                for g in range(2):
                    bk = nb()
                    for i in range(4):
                        TR(bk[:, i * 128:(i + 1) * 128], PQ[:, w, (4 * g + i) * 128:(4 * g + i + 1) * 128], identf[:])
                    CP("act", PQT[:, w, 4 * g:4 * g + 4, :].r("p i c -> p (i c)"), bk[:])

        def epilogue_half(ct, s, tok0, half, bk):
            hs = slice(half * 512, (half + 1) * 512)
            ts_ = slice(tok0 + half * 512, tok0 + (half + 1) * 512)
            STT("dve", tA[:, hs], uTk[s][:, ts_], skipT[:, ct:ct + 1], bk[:], ALU.mult, ALU.add)
            TT("pool", Ksb[:, 0, hs], tA[:, hs], x0k[s][:, ts_], ALU.mult)
            if half == 1:
                STORE("sp", obT_d.at(("c", ct, tok0), obT_d.ap[:, ct, tok0:tok0 + 1024]), Ksb[:, 0, :])

        def blk_load(bi):
            LOAD("sp", [hTblk[bi % 2][:, :, i * 128:(i + 1) * 128] for i in range(4)],
                 [hT_d.at(4 * bi + i, hT_d.ap[4 * bi + i]) for i in range(4)])

        def part1(ct, s):
            LOAD("pool", [Wx[s][:, :, g, :] for g in range(3)],
                 [wview(I["w_in"], 5120 + g * 1024 + ct * 128, 5120 + g * 1024 + (ct + 1) * 128) for g in range(3)])
            LOAD("pool", w4c[s][:], I["w4all"].at("w", I["w4all"].ap[:, :, ct * 128:(ct + 1) * 128]))
            blk_load(0)

            def u_transpose(bi):
                ub = ub2[bi % 2]
                for i in range(4):
                    TR(pt[0][:, i * 128:(i + 1) * 128], ub[:, i * 128:(i + 1) * 128], ident[:])
                CP("act", utm[s][:, 4 * bi:4 * bi + 4, :].r("p i c -> p (i c)"), pt[0][:, 0:512])

            for bi in range(10):
                if bi + 1 < 10:
                    blk_load(bi + 1)
                hb_ = hTblk[bi % 2]
                ub = ub2[bi % 2]
                groups = (1, 2, 0) if bi < 4 else (1, 2)
                rl = 256 if bi < 2 else 64
                for gi, g in enumerate(groups):
                    ps = pa[gi % 2]
                    for k in range(8):
                        MM(ps[:], Wx[s][:, k, g, :], hb_[:, k, :], start=(k == 0), stop=(k == 7))
                    ci = g * 8 + ct
                    if gi == 2:
                        TT("pool", ub[:], cy[0][:], cy[1][:], ALU.mult)
                    y_ = cy[gi % 2]
                    ACT(y_[:], ps[:], AF.Identity, bias=convb[:, ci:ci + 1], scale=convw[:, ci, 1:2])
                    y3 = y_[:].r("p (r t) -> p r t", t=rl)
                    x3 = ps[:].r("p (r t) -> p r t", t=rl)
                    STT("dve", y3[:, :, 1:rl], x3[:, :, 0:rl - 1], convw[:, ci, 0:1], y3[:, :, 1:rl], ALU.mult, ALU.add)
                    STT("dve", y3[:, :, 0:rl - 1], x3[:, :, 1:rl], convw[:, ci, 2:3], y3[:, :, 0:rl - 1], ALU.mult, ALU.add)
                if len(groups) == 2:
                    TT("pool", ub[:], cy[0][:], cy[1][:], ALU.mult)
                if bi < 4:
                    CP("pool", uTk[s][:, bi * 512:(bi + 1) * 512], ub[:])
                    CP("act", x0k[s][:, bi * 512:(bi + 1) * 512], cy[0][:])
                if bi > 0:
                    u_transpose(bi - 1)
                yield
            u_transpose(9)
            yield

        def part2(ct, s):
            U = utm[s]

            def cusu_prompt(CSsb):
                for w in range(2):
                    for jp in range(2):
                        bk = nb()
                        for j in (2 * jp, 2 * jp + 1):
                            for nt in range(2):
                                MM(bk[:, (j % 2) * 256:(j % 2 + 1) * 256], U[:, 2 * j + nt, :], dP[:, w, nt, :],
                                   start=(nt == 0), stop=(nt == 1))
                        CP("act", CSsb[:, w, jp * 512:(jp + 1) * 512], bk[:])

            def cusu_sample(kb, CSsb):
                for w in range(2):
                    for half in range(2):
                        bk = nb()
                        for nt in range(8):
                            MM(bk[:], U[:, 8 + 8 * kb + nt, :],
                               dS[:, w, nt, half * 512:(half + 1) * 512], start=(nt == 0), stop=(nt == 7))
                        CP("act", CSsb[:, w, half * 512:(half + 1) * 512], bk[:])

            def k_sample(hpm):
                for w in range(2):
                    for half in range(2):
                        bk = nb()
                        for jt in range(8):
                            MM(bk[:], hpm[:, w, jt, :],
                               dS[:, 2 + w, jt, half * 512:(half + 1) * 512], start=(jt == 0), stop=(jt == 7))
                        CP("act", Ksb[:, w, half * 512:(half + 1) * 512], bk[:])

            def stap(kb):
                return taps(ct, s, 8, 2 + 16 * kb, 2 + 16 * kb + 8, 2 + 2 * kb, 3 + 2 * kb, False, hpm2[(kb + 1) % 2])

            yield from taps(ct, s, 2, 0, 0, 0, 1, True, hpm2[0])
            cusu_prompt(CSsb2[0])
            yield
            yield from stap(0)
            hp_ = hpm2[0]
            bk = nb()
            for jt in range(2):
                MM(bk[:, 0:256], hp_[:, 0, jt, :], dP[:, 2, jt, :], start=(jt == 0), stop=(jt == 1))
            for jt in range(2):
                MM(bk[:, 256:512], hp_[:, 1, jt, :], dP[:, 3, jt, :], start=(jt == 0), stop=(jt == 1))
            for j in range(4):
                CP("act", Ksb[:, 0, j * 256:(j + 1) * 256], bk[:, 0:256])
                CP("act", Ksb[:, 1, j * 256:(j + 1) * 256], bk[:, 256:512])
            yield
            cusu_sample(0, CSsb2[1])
            yield
            pq_update(True, CSsb2[0])
            yield
            pq_transpose()
            for jp in range(2):
                bk = nb()
                for j in (2 * jp, 2 * jp + 1):
                    n = 0
                    for w in range(2):
                        for ft in range(2):
                            MM(bk[:, (j % 2) * 256:(j % 2 + 1) * 256], PQT[:, w, 2 * j + ft, :], dP[:, w, ft, :],
                               start=(n == 0), stop=(n == 3))
                            n += 1
                epilogue_half(ct, s, 0, jp, bk)
            yield
            for kb in range(4):
                if kb + 1 < 4:
                    yield from stap(kb + 1)
                k_sample(hpm2[(kb + 1) % 2])
                yield
                if kb + 1 < 4:
                    cusu_sample(kb + 1, CSsb2[kb % 2])
                    yield
                pq_update(kb == 0, CSsb2[(kb + 1) % 2])
                yield
            pq_transpose()
            for half in range(2):
                bk = nb()
                n = 0
                for w in range(2):
                    for ft in range(8):
                        MM(bk[:], PQT[:, w, ft, :],
                           dS[:, w, ft, half * 512:(half + 1) * 512], start=(n == 0), stop=(n == 15))
                        n += 1
                epilogue_half(ct, s, 1024, half, bk)
            yield

        zb1 = Buf("z1buf")
        zsem = [PQ.sem(), P.new_sem("zsem1")]

        def mlp_setup():
            LOAD("sp", fw1[:], I["fw1"].at("w"))
            LOAD("sp", fw2[:], I["fw2"].at("w"))
            LOAD("sp", fw3[:], I["fw3"].at("w"))
            LOAD("sp", fbt[:], I["fb"].at("w"))
            LOAD("sp", frq[:], I["ffreq"].at("w"))
            TT("dve", fbt[:], fbt[:], frq[:], ALU.mult)

        def mlp_gen(cid):
            ws = [fw1, fw2, fw3]
            if cid == 0:
                aa, kf, ss = tA[0:64, 0:512], tA[0:64, 512:1024], PQ[0:64, 0, 0:512]
                zv = PQ[0:33, 0, 512:1024]
                banks = (pa[4], pa[5])
            else:
                aa, kf, ss = ew[0:64, :], hwf[0:64, :], hwb[0:64, :]
                zv = V(PQ.t[0:33, 1, 0:512], zb1, None)
                banks = (pa[6], pa[2])
            for ch in range(cid, 17, 2):
                c0 = ch * 512
                n = min(512, 8448 - c0)
                P.dma("sp", zsem[cid], [(zv[:, 0:n].ap, I["zt"].ap[:, c0:c0 + n])], reads=[], writes=[zv.b])
                src = zv
                for l in range(3):
                    pv = banks[l % 2][0:64, 0:n]
                    MM(pv, ws[l][:], src[:, 0:n])
                    TS("dve", aa[:, 0:n], pv, frq[:, l:l + 1], fbt[:, l:l + 1], ALU.mult, ALU.add)
                    TS("dve", kf[:, 0:n], aa[:, 0:n], 1.0 / TWO_PI, MAGIC, ALU.mult, ALU.add)
                    TS("dve", kf[:, 0:n], kf[:, 0:n], -MAGIC, None, ALU.add)
                    STT("dve", aa[:, 0:n], kf[:, 0:n], -TWO_PI, aa[:, 0:n], ALU.mult, ALU.add)
                    if l < 2:
                        ACT(ss[:, 0:n], aa[:, 0:n], AF.Sin)
                        src = ss
                    else:
                        ACT(h3T[:, c0:c0 + n], aa[:, 0:n], AF.Sin)
                    yield

        def chain(*gs):
            for g in gs:
                yield from g

        mlp_setup()
        run_rr([mlp_gen(0), mlp_gen(1), chain(part1(0, 0), part1(1, 1))])
        for ct in range(8):
            g2 = part2(ct, ct % 2)
            g1 = part1(ct + 1, (ct + 1) % 2) if 1 <= ct < 7 else iter(())
            alive = [True, True]
            while alive[0] or alive[1]:
                for _ in range(1):
                    if alive[1]:
                        try:
                            next(g2)
                        except StopIteration:
                            alive[1] = False
                if alive[0]:
                    try:
                        next(g1)
                    except StopIteration:
                        alive[0] = False
        P.barrier()
    if stop_after == 3:
        return finish(nc, P, es_all, [yp, ys, st_out])

    rv = rows_d.ap

    def layer_norm_tile(es_tiles, r, grow, brow, out_main, extra=None):
        stats, mv, xn = es_tiles
        for c2 in range(2):
            P.op("dve", lambda h_, c2=c2: h_.bn_stats(stats[:, c2, :].ap, r[:, c2 * 512:(c2 + 1) * 512].ap),
                 reads=[r.b], writes=[stats.b])
        P.op("dve", lambda h_: h_.bn_aggr(mv[:, 0:2].ap, stats[:].ap), reads=[stats.b], writes=[mv.b])
        TS("dve", mv[:, 2:3], mv[:, 1:2], 1e-5, None, ALU.add)
        ACT(mv[:, 2:3], mv[:, 2:3], AF.Sqrt)
        P.op("dve", lambda h_: h_.reciprocal(mv[:, 3:4].ap, mv[:, 2:3].ap), reads=[mv.b], writes=[mv.b])
        TS("dve", xn[:], r[:], mv[:, 0:1], mv[:, 3:4], ALU.subtract, ALU.mult)
        TT("pool", out_main, xn[:], grow, ALU.mult)
        TT("pool", out_main, out_main, brow, ALU.add)
        if extra is not None:
            G, B, o2, tmp = extra
            TT("dve", tmp, xn[:], G, ALU.mult)
            TT("dve", o2, tmp, B, ALU.add)

    with contextlib.ExitStack() as es:
        pa, pw, pt = psum_std(es)
        pA = alloc(es, "pA", [128, 8, D], BF16)
        pB = alloc(es, "pB", [128, 8, D], BF16)
        wO = alloc(es, "wO", [128, 8, D], BF16)
        Wmg = alloc(es, "Wmg", [128, 8, 2 * D], BF16)
        LOAD("pool", Wmg[:, :, 0:D], wview(I["w_in"], 8192, 8192 + D))
        LOAD("pool", pA[:], wview(I["proj_a"], 0, D))
        LOAD("pool", Wmg[:, :, D:2 * D], wview(I["w_in"], 8192 + D, 10240))
        LOAD("pool", pB[:], wview(I["proj_b"], 0, D))
        LOAD("pool", wO[:], wview(I["w_out"], 0, D))
        oaTh = alloc(es, "oaTh", [128, 8, 1024], BF16)
        obTh = alloc(es, "obTh", [128, 8, 1024], BF16)
        hTh = alloc(es, "hTh", [128, 8, 1024], BF16)
        mT = alloc(es, "mT", [128, 8, 1024], BF16)
        rws = alloc(es, "rws", [128, 5, D], F32)
        gas = alloc(es, "gas", [128, 512], F32)
        gbs = alloc(es, "gbs", [128, 512], F32)
        m1 = alloc(es, "m1", [128, 512], F32)
        m2 = alloc(es, "m2", [128, 512], F32)
        xt = [alloc(es, "xtD%d" % i, [128, D], F32) for i in range(2)]
        rr = alloc(es, "rr", [128, D], F32)
        xn = alloc(es, "xn", [128, D], F32)
        x1t = [alloc(es, "x1t%d" % i, [128, D], F32) for i in range(2)]
        h2b = alloc(es, "h2b", [128, D], BF16)
        h2Tt = [alloc(es, "h2Tt%d" % i, [128, 8, 128], BF16) for i in range(2)]
        stats = alloc(es, "stats", [128, 2, 6], F32)
        mv = alloc(es, "mv", [128, 4], F32)
        for hf in range(2):
            LOAD("sp", oaTh[:], oaT_d.at(("h", hf), oaT_d.ap[:, :, hf * 1024:(hf + 1) * 1024]))
            LOAD("sp", obTh[:], obT_d.at(("h", hf), obT_d.ap[:, :, hf * 1024:(hf + 1) * 1024]))
            LOAD("sp", [hTh[:, :, t * 128:(t + 1) * 128] for t in range(8)],
                 [hT_d.at(hf * 8 + t, hT_d.ap[hf * 8 + t]) for t in range(8)])
            rowload("sp", rws[:, 0, :], rows_d.at(2, rv[hf, 2, :]))
            rowload("sp", rws[:, 1, :], rows_d.at(3, rv[hf, 3, :]))
            rowload("sp", rws[:, 2, :], rows_d.at(4, rv[hf, 4, :]))
            rowload("sp", rws[:, 3, :], I["lnrows"].at("w", I["lnrows"].ap[0]))
            rowload("sp", rws[:, 4, :], I["lnrows"].at("w", I["lnrows"].ap[1]))
            for j in range(8):
                for tc in range(2):
                    ts_ = slice(tc * 512, (tc + 1) * 512)
                    cs_ = slice(j * 128, (j + 1) * 128)
                    cs2 = slice(D + j * 128, D + (j + 1) * 128)
                    for k in range(8):
                        MM(pa[2][:], Wmg[:, k, cs_], hTh[:, k, ts_], start=(k == 0), stop=(k == 7))
                    for k in range(8):
                        MM(pa[3][:], Wmg[:, k, cs2], hTh[:, k, ts_], start=(k == 0), stop=(k == 7))
                    for k in range(8):
                        MM(pa[0][:], pA[:, k, cs_], oaTh[:, k, ts_], start=(k == 0), stop=(k == 7))
                    for k in range(8):
                        MM(pa[1][:], pB[:, k, cs_], obTh[:, k, ts_], start=(k == 0), stop=(k == 7))
                    ACT(gas[:], pa[2][:], AF.Sigmoid)
                    ACT(gbs[:], pa[3][:], AF.Sigmoid)
                    TT("dve", m1[:], gas[:], pa[0][:], ALU.mult)
                    TT("dve", m2[:], gbs[:], pa[1][:], ALU.mult)
                    TT("pool", mT[:, j, ts_], m1[:], m2[:], ALU.add)
            for t in range(8):
                gi = hf * 8 + t
                for half2 in range(2):
                    for k in range(8):
                        MM(pw[:, half2 * 512:(half2 + 1) * 512], mT[:, k, t * 128:(t + 1) * 128],
                           wO[:, k, half2 * 512:(half2 + 1) * 512], start=(k == 0), stop=(k == 7))
                x_ = xt[t % 2]
                if t == 0:
                    LOAD("sp", xt[0][:], (I["xp"] if hf == 0 else I["xs"]).at("w", (I["xp"] if hf == 0 else I["xs"]).ap[0:128, :]))
                if t + 1 < 8:
                    LOAD("sp", xt[(t + 1) % 2][:], (I["xp"] if hf == 0 else I["xs"]).at(
                        "w", (I["xp"] if hf == 0 else I["xs"]).ap[(t + 1) * 128:(t + 2) * 128, :]))
                TT("dve", rr[:], pw[:], rws[:, 0, :], ALU.mult)
                STT("dve", rr[:], x_[:], ALPHA, rr[:], ALU.mult, ALU.add)
                x1_ = x1t[t % 2]
                layer_norm_tile((stats, mv, xn), rr, rws[:, 3, :], rws[:, 4, :], x1_[:],
                                extra=(rws[:, 1, :], rws[:, 2, :], h2b[:], rr[:]))
                STORE("sp", x1_d.at(gi, x1_d.ap[gi]), x1_[:])
                for k in range(8):
                    TR(pt[t % 2][:, k * 128:(k + 1) * 128], h2b[:, k * 128:(k + 1) * 128], ident[:])
                o_ = h2Tt[t % 2]
                CP("act", o_[:].r("p k t -> p (k t)"), pt[t % 2][:])
                STORE("sp", h2T_d.at(gi, h2T_d.ap[gi]), o_[:])
        P.barrier()
    if stop_after == 4:
        return finish(nc, P, es_all, [yp, ys, st_out])

    with contextlib.ExitStack() as es:
        pa, pw, pt = psum_std(es)
        wout = alloc(es, "wout", [128, 22, D], BF16)
        actT = alloc(es, "actT", [128, 22, 1024], BF16)
        h2Th = alloc(es, "h2Th", [128, 8, 1024], BF16)
        Wfi = [alloc(es, "Wfi%d" % i, [128, 8, 2, 256], BF16) for i in range(2)]
        rws = alloc(es, "rws2", [128, 3, D], F32)
        sgt = [alloc(es, "sgt%d" % i, [128, 512], F32) for i in range(2)]
        x1t = [alloc(es, "x1u%d" % i, [128, D], F32) for i in range(2)]
        rr = alloc(es, "rr2", [128, D], F32)
        xn = alloc(es, "xn2", [128, D], F32)
        yt = [alloc(es, "yt%d" % i, [128, D], F32) for i in range(2)]
        stats = alloc(es, "stats2", [128, 2, 6], F32)
        mv = alloc(es, "mv2", [128, 4], F32)
        for hf in range(2):
            LOAD("sp", [h2Th[:, :, t * 128:(t + 1) * 128] for t in range(8)],
                 [h2T_d.at(hf * 8 + t, h2T_d.ap[hf * 8 + t]) for t in range(8)])
            rowload("sp", rws[:, 0, :], rows_d.at(5, rv[hf, 5, :]))
            rowload("sp", rws[:, 1, :], I["lnrows"].at("w", I["lnrows"].ap[2]))
            rowload("sp", rws[:, 2, :], I["lnrows"].at("w", I["lnrows"].ap[3]))
            def wfi_load(fbk):
                W2 = Wfi[fbk % 2]
                LOAD("pool", [W2[:, :, 0, :], W2[:, :, 1, :]],
                     [wview(I["ffn_w_in"], fbk * 256, (fbk + 1) * 256),
                      wview(I["ffn_w_in"], 2816 + fbk * 256, 2816 + (fbk + 1) * 256)])
            wfi_load(0)
            for fbk in range(11):
                W_ = Wfi[fbk % 2]
                if fbk + 1 < 11:
                    wfi_load(fbk + 1)
                if hf == 0 and fbk == 1:
                    LOAD("pool", wout[:], wview(I["ffn_w_out"], 0, D))
                for sub in range(2):
                    j = fbk * 2 + sub
                    for tc in range(2):
                        ts_ = slice(tc * 512, (tc + 1) * 512)
                        pg, pu, sg_ = pa[2 * tc], pa[2 * tc + 1], sgt[tc]
                        for k in range(8):
                            MM(pg[:], W_[:, k, 0, sub * 128:(sub + 1) * 128], h2Th[:, k, ts_],
                               start=(k == 0), stop=(k == 7))
                        for k in range(8):
                            MM(pu[:], W_[:, k, 1, sub * 128:(sub + 1) * 128], h2Th[:, k, ts_],
                               start=(k == 0), stop=(k == 7))
                        ACT(sg_[:], pg[:], AF.Silu)
                        TT("dve", actT[:, j, ts_], sg_[:], pu[:], ALU.mult)
            for t in range(8):
                gi = hf * 8 + t
                for half2 in range(2):
                    for j in range(22):
                        MM(pw[:, half2 * 512:(half2 + 1) * 512], actT[:, j, t * 128:(t + 1) * 128],
                           wout[:, j, half2 * 512:(half2 + 1) * 512], start=(j == 0), stop=(j == 21))
                x1_ = x1t[t % 2]
                if t == 0:
                    LOAD("sp", x1t[0][:], x1_d.at(gi, x1_d.ap[gi]))
                if t + 1 < 8:
                    LOAD("sp", x1t[(t + 1) % 2][:], x1_d.at(gi + 1, x1_d.ap[gi + 1]))
                TT("dve", rr[:], pw[:], rws[:, 0, :], ALU.mult)
                STT("dve", rr[:], x1_[:], ALPHA, rr[:], ALU.mult, ALU.add)
                y_ = yt[t % 2]
                layer_norm_tile((stats, mv, xn), rr, rws[:, 1, :], rws[:, 2, :], y_[:])
                dst = yp if hf == 0 else ys
                STORE("sp", dst.at(t, dst.ap[t * 128:(t + 1) * 128, :]), y_[:])
        P.barrier()
    return finish(nc, P, es_all, [yp, ys, st_out])


def finish(nc, P, es_all, outs):
    P.barrier()
    P.emit()
    try:
        es_all.close()
    except Exception:
        pass
    P.close()
    return nc


_NC_CACHE = {}


def kernel(**inputs):
    I = {k: np.asarray(v) for k, v in inputs.items()}
    if "nc" not in _NC_CACHE:
        _NC_CACHE["nc"] = build_program()
    nc = _NC_CACHE["nc"]
    in_maps = [_core_inputs(r, I) for r in range(8)]
    res = run_bass_kernel_spmd(nc, in_maps, core_ids=list(range(8)))
    y_prompt = np.concatenate([res.results[r]["yp"].reshape(4, 256, D) for r in range(8)], 0)
    y_sample = np.zeros((2, 4096, D), np.float32)
    for r in range(8):
        y_sample[r // 4, 1024 * (r % 4):1024 * (r % 4 + 1)] = res.results[r]["ys"]
    new_state = np.concatenate([res.results[r]["st"].reshape(4, 1, 2, 8, 128, 128) for r in range(8)], 0)
    return (y_prompt.astype(np.float32), y_sample, new_state.astype(np.float32))
```
